# Optimizing a Trainium2 kernel written in Bass

```python
import jax, jax.numpy as jnp
from jax import lax
import numpy as np

D_MODEL = 2048
BATCH = 1
SEQ = 16384
DEPTH = 1

HEAD_DIM = 128
MLA_HEADS = D_MODEL // (2 * HEAD_DIM)
MOBA_HEADS = D_MODEL // (2 * HEAD_DIM)
MLA_NOPE = 128
MLA_ROPE = 64
MLA_V = 128
KV_RANK = 512
MOBA_BLOCK = 256
MOBA_TOPK = 3
Q_CHUNK = 128
MOBA_Q_CHUNK = 64
PLE_DIM = 256
ROPE_THETA = 10000.0
EPS = 1e-6
NEG = -1e30
D_FF = -(-8 * D_MODEL // (3 * 256)) * 256

MLA_Q_COLS = MLA_HEADS * (MLA_NOPE + MLA_ROPE)
MOBA_W = MOBA_HEADS * HEAD_DIM
MLA_OUT = MLA_HEADS * MLA_V
MIX_WIDTH = MLA_OUT + MOBA_W
IN_COLS = MLA_Q_COLS + KV_RANK + MLA_ROPE + 3 * MOBA_W

kernel_name = "hybrid_mla_moba_parallel_heads"


def rmsnorm(x, g):
    xf = x.astype(jnp.float32)
    y = xf * lax.rsqrt(jnp.mean(xf * xf, axis=-1, keepdims=True) + EPS)
    return (y * g.astype(jnp.float32)).astype(x.dtype)


def rope(x, pos):
    d = x.shape[-1]
    half = d // 2
    inv = ROPE_THETA ** (-(jnp.arange(half, dtype=jnp.float32) * 2.0 / d))
    ang = pos.astype(jnp.float32)[:, :, None, None] * inv
    cos, sin = jnp.cos(ang), jnp.sin(ang)
    x1 = x[..., :half].astype(jnp.float32)
    x2 = x[..., half:].astype(jnp.float32)
    return jnp.concatenate([x1 * cos - x2 * sin, x2 * cos + x1 * sin], axis=-1).astype(x.dtype)


def mla_attention(q_nope, q_pe, k_nope, k_pe, v):
    B, S, H, _ = q_nope.shape
    nc = S // Q_CHUNK
    scale = (MLA_NOPE + MLA_ROPE) ** -0.5

    def to_chunks(t):
        return t.reshape(B, nc, Q_CHUNK, H, t.shape[-1]).transpose(1, 0, 3, 2, 4)

    kn = k_nope.transpose(0, 2, 1, 3)
    vv = v.transpose(0, 2, 1, 3)
    key_idx = jnp.arange(S)

    def step(args):
        qn, qp, c = args
        s = (jnp.einsum('bhqd,bhkd->bhqk', qn, kn, preferred_element_type=jnp.float32)
             + jnp.einsum('bhqr,bkr->bhqk', qp, k_pe, preferred_element_type=jnp.float32)) * scale
        qpos = c * Q_CHUNK + jnp.arange(Q_CHUNK)
        s = jnp.where(key_idx[None, :] <= qpos[:, None], s, NEG)
        prob = jax.nn.softmax(s, axis=-1)
        return jnp.einsum('bhqk,bhkd->bhqd', prob.astype(vv.dtype), vv)

    out = lax.map(step, (to_chunks(q_nope), to_chunks(q_pe), jnp.arange(nc)))
    return out.transpose(1, 0, 3, 2, 4).reshape(B, S, H * MLA_V)


def moba_attention(q, k, v):
    B, S, H, dh = q.shape
    nb = -(-S // MOBA_BLOCK)
    pad = nb * MOBA_BLOCK - S
    scale = dh ** -0.5
    qh = q.transpose(0, 2, 1, 3)
    kp = jnp.pad(k.transpose(0, 2, 1, 3), ((0, 0), (0, 0), (0, pad), (0, 0)))
    vp = jnp.pad(v.transpose(0, 2, 1, 3), ((0, 0), (0, 0), (0, pad), (0, 0)))
    kb = kp.reshape(B, H, nb, MOBA_BLOCK, dh)
    vb = vp.reshape(B, H, nb, MOBA_BLOCK, dh)

    counts = jnp.clip(S - jnp.arange(nb) * MOBA_BLOCK, 1, MOBA_BLOCK).astype(jnp.float32)
    kmean = kb.astype(jnp.float32).sum(axis=3) / counts[:, None]
    gate = jnp.einsum('bhsd,bhnd->bhsn', qh.astype(jnp.float32), kmean)
    qblk = jnp.arange(S) // MOBA_BLOCK
    gate = jnp.where(jnp.arange(nb)[None, :] < qblk[:, None], gate, NEG)
    ksel = min(MOBA_TOPK, nb)
    _, idx = lax.top_k(gate, ksel)

    nc = S // MOBA_Q_CHUNK
    qc_all = qh.reshape(B, H, nc, MOBA_Q_CHUNK, dh).transpose(2, 0, 1, 3, 4)
    ic_all = idx.reshape(B, H, nc, MOBA_Q_CHUNK, ksel).transpose(2, 0, 1, 3, 4)
    bi = jnp.arange(B)[:, None, None, None]
    hi = jnp.arange(H)[None, :, None, None]

    def step(args):
        qc, ic, c = args
        qpos = c * MOBA_Q_CHUNK + jnp.arange(MOBA_Q_CHUNK)
        blk = (c * MOBA_Q_CHUNK) // MOBA_BLOCK
        k_sel = kb[bi, hi, ic]
        v_sel = vb[bi, hi, ic]
        s_sel = jnp.einsum('bhqd,bhqnjd->bhqnj', qc, k_sel, preferred_element_type=jnp.float32) * scale
        valid = jnp.arange(ksel)[None, :] < (qpos // MOBA_BLOCK)[:, None]
        s_sel = jnp.where(valid[:, :, None], s_sel, NEG).reshape(B, H, MOBA_Q_CHUNK, ksel * MOBA_BLOCK)
        k_own = lax.dynamic_slice_in_dim(kp, blk * MOBA_BLOCK, MOBA_BLOCK, axis=2)
        v_own = lax.dynamic_slice_in_dim(vp, blk * MOBA_BLOCK, MOBA_BLOCK, axis=2)
        s_own = jnp.einsum('bhqd,bhjd->bhqj', qc, k_own, preferred_element_type=jnp.float32) * scale
        own_pos = blk * MOBA_BLOCK + jnp.arange(MOBA_BLOCK)
        s_own = jnp.where(own_pos[None, :] <= qpos[:, None], s_own, NEG)
        prob = jax.nn.softmax(jnp.concatenate([s_sel, s_own], axis=-1), axis=-1)
        p_sel = prob[..., :ksel * MOBA_BLOCK].reshape(B, H, MOBA_Q_CHUNK, ksel, MOBA_BLOCK)
        p_own = prob[..., ksel * MOBA_BLOCK:]
        return (jnp.einsum('bhqnj,bhqnjd->bhqd', p_sel.astype(v_sel.dtype), v_sel)
                + jnp.einsum('bhqj,bhjd->bhqd', p_own.astype(v_own.dtype), v_own))

    out = lax.map(step, (qc_all, ic_all, jnp.arange(nc)))
    return out.transpose(1, 0, 3, 2, 4).reshape(B, S, H * dh)


def setup_inputs(seed: int = 0) -> dict:
    key = jax.random.key(seed)
    ks = jax.random.split(key, 20)
    f32 = jnp.float32

    def w(k, shape, fan_in):
        return jax.random.normal(k, shape, f32) * (fan_in ** -0.5)

    def gain(k, shape):
        return 1.0 + 0.02 * jax.random.normal(k, shape, f32)

    x = jax.random.normal(ks[0], (BATCH, SEQ, D_MODEL), f32)
    p = jax.random.normal(ks[1], (DEPTH, BATCH, SEQ, PLE_DIM), f32)
    positions = jnp.broadcast_to(jnp.arange(SEQ, dtype=jnp.int32)[None, :], (BATCH, SEQ))
    return {
        "x": x,
        "p": p,
        "positions": positions,
        "attn_norm": gain(ks[2], (DEPTH, D_MODEL)),
        "w_in": w(ks[3], (DEPTH, D_MODEL, IN_COLS), D_MODEL),
        "kv_norm": gain(ks[4], (DEPTH, KV_RANK)),
        "w_ukv": w(ks[5], (DEPTH, KV_RANK, MLA_HEADS * (MLA_NOPE + MLA_V)), KV_RANK),
        "w_o": w(ks[6], (DEPTH, MIX_WIDTH, D_MODEL), MIX_WIDTH),
        "ffn_norm": gain(ks[7], (DEPTH, D_MODEL)),
        "w_gate": w(ks[8], (DEPTH, D_MODEL, D_FF), D_MODEL),
        "w_up": w(ks[9], (DEPTH, D_MODEL, D_FF), D_MODEL),
        "w_down": w(ks[10], (DEPTH, D_FF, D_MODEL), D_FF),
        "ple_norm": gain(ks[11], (DEPTH, D_MODEL)),
        "w_ple_gate": w(ks[12], (DEPTH, D_MODEL, D_MODEL), D_MODEL),
        "w_ple_proj": w(ks[13], (DEPTH, PLE_DIM, D_MODEL), PLE_DIM),
        "final_norm": gain(ks[14], (D_MODEL,)),
    }


def reference(x, p, positions, attn_norm, w_in, kv_norm, w_ukv, w_o, ffn_norm,
              w_gate, w_up, w_down, ple_norm, w_ple_gate, w_ple_proj, final_norm):
    B, S, _ = x.shape
    splits = [MLA_Q_COLS, MLA_Q_COLS + KV_RANK, MLA_Q_COLS + KV_RANK + MLA_ROPE,
              MLA_Q_COLS + KV_RANK + MLA_ROPE + MOBA_W, MLA_Q_COLS + KV_RANK + MLA_ROPE + 2 * MOBA_W]
    h = x
    for i in range(DEPTH):
        a = rmsnorm(h, attn_norm[i])
        proj = a @ w_in[i]
        q_mla, c_kv, k_pe, q_mb, k_mb, v_mb = jnp.split(proj, splits, axis=-1)

        q_mla = q_mla.reshape(B, S, MLA_HEADS, MLA_NOPE + MLA_ROPE)
        q_nope, q_pe = q_mla[..., :MLA_NOPE], q_mla[..., MLA_NOPE:]
        q_pe = rope(q_pe, positions)
        k_pe = rope(k_pe[:, :, None, :], positions)[:, :, 0, :]
        kv = (rmsnorm(c_kv, kv_norm[i]) @ w_ukv[i]).reshape(B, S, MLA_HEADS, MLA_NOPE + MLA_V)
        k_nope, v_mla = kv[..., :MLA_NOPE], kv[..., MLA_NOPE:]
        out_mla = mla_attention(q_nope, q_pe, k_nope, k_pe, v_mla)

        q_mb = rope(q_mb.reshape(B, S, MOBA_HEADS, HEAD_DIM), positions)
        k_mb = rope(k_mb.reshape(B, S, MOBA_HEADS, HEAD_DIM), positions)
        v_mb = v_mb.reshape(B, S, MOBA_HEADS, HEAD_DIM)
        out_moba = moba_attention(q_mb, k_mb, v_mb)

        h = h + jnp.concatenate([out_mla, out_moba], axis=-1) @ w_o[i]

        f = rmsnorm(h, ffn_norm[i])
        h = h + (jax.nn.silu(f @ w_gate[i]) * (f @ w_up[i])) @ w_down[i]

        g = jax.nn.sigmoid(rmsnorm(h, ple_norm[i]) @ w_ple_gate[i])
        h = h + g * (p[i] @ w_ple_proj[i])
    return rmsnorm(h, final_norm)
```

```python
import os
import numpy as np
import concourse.bass as bass
import concourse.mybir as mybir
from concourse.bass_utils import run_bass_kernel_spmd

F32 = mybir.dt.float32
BF16 = mybir.dt.bfloat16
I32 = mybir.dt.int32
ALU = mybir.AluOpType
AF = mybir.ActivationFunctionType
AX = mybir.AxisListType

D = 2048
DFF = 5632
NFF = DFF // 128
INC = 5184
EPS = 1e-6
PI = float(np.pi)
NCORES = 8

SAME_ENGINE_SYNC = True
DBG = set(os.environ.get('KDBG', '').split(','))
CUT = int(os.environ.get('KCUT', '100000000'))
STQ = 'act' if 'stact' in DBG else 'sp'
NDMASEM = 8
SKEW = 2
ARENA_F32 = 52600


class Buf:
    __slots__ = ("t", "writer", "readers", "name", "psum")

    def __init__(self, t=None, name="", psum=False):
        self.psum = psum
        self.t = t
        self.writer = None
        self.readers = {}
        self.name = name

    def __getitem__(self, idx):
        return self.t[idx]


class Op:
    __slots__ = ("eng", "fn", "deps", "needs_inc", "seq", "dma", "sem", "semval", "prev", "desc")

    def __init__(self, eng, fn):
        self.eng = eng
        self.fn = fn
        self.deps = []
        self.needs_inc = False
        self.seq = 0
        self.dma = False
        self.sem = None
        self.semval = 0
        self.prev = None


class FW:
    ENGS = ("pe", "act", "dve", "pool", "sp")

    def __init__(self, nc):
        self.nc = nc
        self.ops = {e: [] for e in self.ENGS}
        self.esem = {e: nc.alloc_semaphore("es_" + e) for e in ("pe", "act", "dve", "pool")}
        self.dsem = {q: [nc.alloc_semaphore("ds_%s%d" % (q, i)) for i in range(NDMASEM)]
                     for q in ("sp", "pool", "act")}
        self.dcnt = {q: [0] * NDMASEM for q in self.dsem}
        self.dlast = {q: [None] * NDMASEM for q in self.dsem}
        self.drr = {q: 0 for q in self.dsem}
        self.nbuf = 0
        self.base_deps = []
        self.arena = nc.alloc_sbuf_tensor("arena", [128, ARENA_F32], F32)
        self.aoff = 0
        self.banks = [Buf(nc.alloc_psum_tensor("bank%d" % i, [128, 512], F32), "bank%d" % i, psum=True)
                      for i in range(8)]

    def sb(self, shape, dt, name=None):
        self.nbuf += 1
        n = int(np.prod(shape[1:]))
        esz = 4 if dt in (F32, I32) else 2
        nf32 = (n * esz + 3) // 4
        nf32 = (nf32 + 7) // 8 * 8
        assert self.aoff + nf32 <= ARENA_F32, "SBUF arena overflow %d" % (self.aoff + nf32)
        ap = self.arena[:, self.aoff:self.aoff + nf32]
        self.aoff += nf32
        if dt != F32:
            ap = ap.bitcast(dt)
        ap = ap[:, 0:n]
        if len(shape) == 3:
            ap = ap.rearrange("p (a b) -> p a b", a=shape[1])
        elif len(shape) == 4:
            ap = ap.rearrange("p (a b c) -> p a b c", a=shape[1], b=shape[2])
        return Buf(ap, name or "sb%d" % self.nbuf)

    def dram(self, shape, dt, name, kind="Internal"):
        return Buf(self.nc.dram_tensor(name, list(shape), dt, kind=kind).ap(), name)

    def mark(self):
        return self.aoff

    def phase(self, mark):
        self.aoff = mark
        deps = []
        for e in ("pe", "act", "dve", "pool"):
            for o in reversed(self.ops[e]):
                if not o.dma:
                    o.needs_inc = True
                    deps.append(o)
                    break
        for q in self.dsem:
            for o in self.dlast[q]:
                if o is not None:
                    deps.append(o)
        self.base_deps = deps

    def _track(self, op, reads, writes):
        deps = list(self.base_deps)
        for b in reads:
            if b.writer is not None:
                deps.append(b.writer)
            if b.psum:
                deps.extend(r for r in b.readers.values() if r.eng != op.eng)
        for b in writes:
            if b.writer is not None:
                deps.append(b.writer)
            deps.extend(b.readers.values())
        for b in writes:
            b.writer = op
            b.readers = {}
        for b in reads:
            key = op.eng if not op.dma else (op.eng, id(op.sem))
            b.readers[key] = op
        seen = set()
        for d in deps:
            if d is op or id(d) in seen:
                continue
            seen.add(id(d))
            if not d.dma:
                if d.eng == op.eng and not op.dma:
                    if d.eng == "pe" or not SAME_ENGINE_SYNC:
                        continue
                d.needs_inc = True
            op.deps.append(d)

    def op(self, eng, meth, reads=(), writes=(), *args, **kw):
        self.nrec = getattr(self, "nrec", 0) + 1
        if self.nrec > CUT:
            return None
        o = Op(eng, lambda e: getattr(e, meth)(*args, **kw))
        o.desc = (self.nrec, eng, meth)
        self._track(o, reads, writes)
        self.ops[eng].append(o)
        return o

    def dma(self, q, out_ap, in_ap, reads=(), writes=()):
        self.nrec = getattr(self, "nrec", 0) + 1
        if self.nrec > CUT:
            return None
        o = Op(q, lambda e: e.dma_start(out=out_ap, in_=in_ap))
        o.desc = (self.nrec, q, "dma", str(out_ap)[:80])
        o.dma = True
        k = self.drr[q]
        self.drr[q] = (k + 1) % NDMASEM
        o.sem = self.dsem[q][k]
        self.dcnt[q][k] += 1
        o.semval = 16 * self.dcnt[q][k]
        o.prev = self.dlast[q][k]
        self.dlast[q][k] = o
        self._track(o, reads, writes)
        self.ops[q].append(o)
        return o

    def emit(self, final_waits=()):
        nc = self.nc
        for e in ("pe", "act", "dve", "pool"):
            n = 0
            for o in self.ops[e]:
                if o.dma:
                    continue
                if o.needs_inc:
                    n += 1
                    o.seq = n
        stats = {}

        def run(ename, eng):
            waited = {}
            nw = 0

            def wait(sem, val):
                nonlocal nw
                if waited.get(id(sem), 0) >= val:
                    return
                waited[id(sem)] = val
                eng.wait_ge(sem, val)
                nw += 1

            for o in self.ops[ename]:
                for d in o.deps:
                    if d.dma:
                        wait(d.sem, d.semval)
                    else:
                        wait(self.esem[d.eng], d.seq)
                if o.dma:
                    if o.prev is not None:
                        wait(o.sem, o.prev.semval)
                    ins = o.fn(eng)
                    ins.then_inc(o.sem, 16)
                else:
                    ins = o.fn(eng)
                    if o.needs_inc:
                        ins.then_inc(self.esem[ename], 1)
            if ename == "sp":
                for o in final_waits:
                    if o is not None:
                        wait(o.sem, o.semval)
            stats[ename] = (len(self.ops[ename]), nw)

        with nc.Block() as block:
            @block.tensor
            def _(eng):
                run("pe", eng)

            @block.scalar
            def _(eng):
                run("act", eng)

            @block.vector
            def _(eng):
                run("dve", eng)

            @block.gpsimd
            def _(eng):
                run("pool", eng)

            @block.sync
            def _(eng):
                run("sp", eng)
        return stats


class Ring:
    def __init__(self, items):
        self.items = list(items)
        self.i = 0

    def next(self):
        b = self.items[self.i % len(self.items)]
        self.i += 1
        return b


def build(S, NSLOT, stop_after=99):
    NT = S // 512
    NQ = 512 * NSLOT
    NKT = S // 128
    NBLK = S // 256
    NQC = NQ // 128
    assert NBLK >= 8

    nc = bass.Bass("TRN2", target_bir_lowering=False)
    fw = FW(nc)
    B = fw.banks

    def ext(name, shape, dt=F32):
        return fw.dram(shape, dt, name, kind="ExternalInput")

    x_all = ext("x_all", [S, D]); x_own = ext("x_own", [NQ, D]); p_own = ext("p_own", [NQ, 256])
    posb_all = ext("posb_all", [128, S], I32); posb_own = ext("posb_own", [128, NQ], I32)
    qidxb_d = ext("qidxb", [128, NQ]); qidxc_d = ext("qidxc", [128, NQC], I32)
    kidxc_d = ext("kidxc", [128, NKT])
    g_attn_d = ext("g_attn_b", [128, D]); g_ffn_d = ext("g_ffn_b", [128, D])
    g_ple_d = ext("g_ple_b", [128, D]); g_fin_d = ext("g_fin_b", [128, D])
    g_kv_d = ext("g_kv_b", [128, 512])
    w_in = ext("w_in", [D, INC]); w_ukv = ext("w_ukv", [512, 2048]); w_o = ext("w_o", [D, D])
    w_gate = ext("w_gate", [D, DFF]); w_up = ext("w_up", [D, DFF]); w_down = ext("w_down", [DFF, D])
    w_pg = ext("w_pg", [D, D]); w_pp = ext("w_pp", [256, D])
    ident_d = ext("ident", [128, 128]); rm128_d = ext("rm128", [128, 128]); rm64_d = ext("rm64", [128, 128])
    inv_d = ext("invf", [128, 2]); blki_d = ext("blki", [128, NBLK])
    y = fw.dram([NQ, D], F32, "y", kind="ExternalOutput")

    def scratch(name, shape, n):
        t = fw.dram(shape, BF16, name)
        return t, [Buf(None, "%s_%d" % (name, i)) for i in range(n)]

    KTn, KTn_b = scratch("KTn", [8, 128, S], NT)
    KTp, KTp_b = scratch("KTp", [64, S], NT)
    KTm, KTm_b = scratch("KTm", [8, 128, S], NT)
    Vn, Vn_b = scratch("Vn", [S, 1024], NT)
    Vm, Vm_b = scratch("Vm", [S, 1024], NT)
    QTn, QTn_b = scratch("QTn", [8, 128, NQ], NSLOT)
    QTp, QTp_b = scratch("QTp", [8, 64, NQ], NSLOT)
    QTm, QTm_b = scratch("QTm", [8, 128, NQ], NSLOT)
    OT, OT_b = scratch("OT", [16, 128, NQ], NSLOT)
    Wo_b = fw.dram([D, D], BF16, "Wo_b")
    Wpg_b = fw.dram([D, D], BF16, "Wpg_b")
    Wpp_b = fw.dram([256, D], BF16, "Wpp_b")
    Wd_b = fw.dram([DFF, D], BF16, "Wd_b")
    Wg_b = fw.dram([NFF, 128, 16 * 128], BF16, "Wg_b")
    Wu_b = fw.dram([NFF, 128, 16 * 128], BF16, "Wu_b")

    identb = fw.sb([128, 128], BF16, "identb")
    rm128b = fw.sb([128, 128], BF16, "rm128b")
    rm64b = fw.sb([128, 128], BF16, "rm64b")
    invf = fw.sb([128, 2], F32, "invf")
    mpi = fw.sb([128, 1], F32, "mpi")
    epst = fw.sb([128, 1], F32, "epst")
    KM = fw.sb([128, 8, NBLK], F32, "KM")
    fw.dma("pool", identb[:, :], ident_d[:, :], reads=[ident_d], writes=[identb])
    fw.dma("pool", rm128b[:, :], rm128_d[:, :], reads=[rm128_d], writes=[rm128b])
    fw.dma("pool", rm64b[:, :], rm64_d[:, :], reads=[rm64_d], writes=[rm64b])
    fw.dma("sp", invf[:, :], inv_d[:, :], reads=[inv_d], writes=[invf])
    fw.op("pool", "memset", [], [mpi], mpi[:, :], -PI)
    fw.op("pool", "memset", [], [epst], epst[:, :], EPS)
    PMARK = fw.mark()

    def rstd_from_ss(ss, rstd, n):
        fw.op("act", "activation", [ss, epst], [rstd], out=rstd[:, :], in_=ss[:, :], func=AF.Sqrt,
                                            scale=1.0 / n, bias=epst[:, 0:1])
        fw.op("dve", "reciprocal", [rstd], [rstd], out=rstd[:, :], in_=rstd[:, :])

    def norm_rows(src_ap, src_bufs, gain, dst, ss, rstd, n):
        fw.op("act", "activation", src_bufs, [dst, ss], out=dst[:, 0:n], in_=src_ap, func=AF.Square,
                                            accum_out=ss[:, :])
        rstd_from_ss(ss, rstd, n)
        fw.op("dve", "scalar_tensor_tensor", list(src_bufs) + [rstd, gain], [dst], out=dst[:, 0:n], in0=src_ap, scalar=rstd[:, 0:1],
                                                      in1=gain[:, 0:n], op0=ALU.mult, op1=ALU.mult)

    def transpose_into(src, nchunk, dstT, col0, tpbanks, evac_engs=("act", "dve")):
        for g0 in range(0, nchunk, 8):
            gn = min(8, nchunk - g0)
            bank = tpbanks.next()
            tv = bank.t[:, :].bitcast(BF16).rearrange("p (a b) -> p a b", a=8)
            for k in range(gn):
                fw.op("pe", "transpose", [src, identb], [bank],
                    out=tv[:, k, :], in_=src[:, (g0 + k) * 128:(g0 + k + 1) * 128], identity=identb[:, :])
            eng = evac_engs[(g0 // 8) % len(evac_engs)]
            if eng == "act":
                fw.op("act", "copy", [bank], [dstT],
                    out=dstT[:, g0:g0 + gn, col0:col0 + 128], in_=tv[:, 0:gn, :])
            else:
                fw.op("dve", "tensor_copy", [bank], [dstT],
                    out=dstT[:, g0:g0 + gn, col0:col0 + 128], in_=tv[:, 0:gn, :])

    def precast():
        q = []
        for r in range(4):
            q.append((Wo_b[r * 512:(r + 1) * 512, :], w_o[r * 512:(r + 1) * 512, :], w_o, Wo_b))
        for r in range(4):
            q.append((Wpg_b[r * 512:(r + 1) * 512, :], w_pg[r * 512:(r + 1) * 512, :], w_pg, Wpg_b))
        q.append((Wpp_b[:, :], w_pp[:, :], w_pp, Wpp_b))
        for r in range(11):
            q.append((Wd_b[r * 512:(r + 1) * 512, :], w_down[r * 512:(r + 1) * 512, :], w_down, Wd_b))
        for (src, dst) in ((w_gate, Wg_b), (w_up, Wu_b)):
            sv = src.t.rearrange("(k p) c -> p k c", p=128)
            for f in range(NFF):
                q.append((dst.t[f].rearrange("p (k c) -> p k c", k=16), sv[:, :, f * 128:(f + 1) * 128], src, dst))
        return q

    def precast_issue(q, n):
        for _ in range(min(n, len(q))):
            o, i, sb_, db_ = q.pop(0)
            fw.dma("pool", o, i, reads=[sb_], writes=[db_])

    def proj_phase(mode):
        fw.phase(PMARK)
        if mode == "kv":
            xd, posd, ntile = x_all, posb_all, NT
            ranges = [(1536, 2112), (3136, 5184)]
        else:
            xd, posd, ntile = x_own, posb_own, NSLOT
            ranges = [(0, 1536), (2112, 3136)]
        if mode == "kv":
            parts = [(0, 512, 1536), (512, 576, 2048), (576, 1600, 3136), (1600, 2624, 4160)]
        else:
            parts = [(0, 768, 0), (768, 1536, 768), (1536, 2560, 2112)]
        ncols = parts[-1][1]
        W = fw.sb([128, 16, ncols], BF16, "W_" + mode)
        Wp = [Buf(None, "Wp%d" % i) for i in range(len(parts))]

        def wbuf(coff):
            for i, (lo, hi, _) in enumerate(parts):
                if lo <= coff < hi:
                    return Wp[i]
            raise AssertionError(coff)

        wv = w_in.t.rearrange("(k p) c -> p k c", p=128)

        def load_part(i):
            lo, hi, src = parts[i]
            for k in range(16):
                fw.dma("pool", W[:, k, lo:hi], wv[:, k, src:src + (hi - lo)], reads=[w_in], writes=[Wp[i]])

        if mode == "kv":
            WK = fw.sb([128, 4, 1024], BF16, "WukvK")
            WV = fw.sb([128, 4, 1024], BF16, "WukvV")
            uv = w_ukv.t.rearrange("(k p) (h t c) -> p k h t c", p=128, h=8, t=2)
            gkv = fw.sb([128, 512], F32, "gkv")
            fw.dma("sp", gkv[:, :], g_kv_d[:, :], reads=[g_kv_d], writes=[gkv])

        def load_weights():
            load_part(0)
            if mode == "kv":
                for k in range(4):
                    fw.dma("pool", WK[:, k, :].rearrange("p (h c) -> p h c", h=8), uv[:, k, :, 0, :], reads=[w_ukv], writes=[WK])
                    fw.dma("pool", WV[:, k, :].rearrange("p (h c) -> p h c", h=8), uv[:, k, :, 1, :], reads=[w_ukv], writes=[WV])
            for i in range(1, len(parts)):
                load_part(i)

        pre_q = precast() if mode == "kv" else []
        gat = fw.sb([128, D], F32, "gat")
        fw.dma("sp", gat[:, :], g_attn_d[:, :], reads=[g_attn_d], writes=[gat])

        xs_r = Ring([fw.sb([128, D], F32) for _ in range(2)])
        as_r = Ring([fw.sb([128, D], BF16) for _ in range(2)])
        aT_r = Ring([fw.sb([128, 16, 512], BF16, "aT%d" % i) for i in range(2)])
        ss_r = Ring([fw.sb([128, 1], F32) for _ in range(4)])
        rs_r = Ring([fw.sb([128, 1], F32) for _ in range(4)])
        posi = fw.sb([128, 512], I32, "posi")
        posf = fw.sb([128, 512], F32, "posf")
        tt = fw.sb([128, 512], F32, "tt"); tki = fw.sb([128, 512], I32, "tki"); tkf = fw.sb([128, 512], F32, "tkf")
        tabs = {k: fw.sb([128, 512], F32, "tab" + k) for k in ("s128", "c128", "s64", "c64")}
        t1_r = Ring([fw.sb([128, 512], F32) for _ in range(2)])
        t2_r = Ring([fw.sb([128, 512], F32) for _ in range(2)])
        xb_r = Ring([fw.sb([128, 512], BF16) for _ in range(2)])
        ob_r = Ring([fw.sb([128, 512], BF16) for _ in range(4)])
        if mode == "kv":
            ckv_r = Ring([fw.sb([128, 512], BF16) for _ in range(2)])
            kvnT = fw.sb([128, 4, 512], BF16, "kvnT")
        tp_r = Ring([B[0], B[1]])
        acc_r = Ring([B[2], B[3], B[4], B[7]])
        rp_r = Ring([B[5], B[6]])

        pending_tails = []
        new_tails = []

        def rope_out(acc, M, kind, dst_ap, dst_buf, kmean=None):
            rm = rm128b if kind == "128" else rm64b
            cs, sn = tabs["c" + kind], tabs["s" + kind]
            xb = xb_r.next(); t1 = t1_r.next()
            fw.op("act", "copy", [acc], [xb], out=xb[0:M, :], in_=acc[0:M, :])
            fw.op("dve", "tensor_tensor", [acc, cs], [t1], out=t1[0:M, :], in0=acc[0:M, :], in1=cs[0:M, :], op=ALU.mult)

            def tail():
                rp = rp_r.next(); t2 = t2_r.next(); ob = ob_r.next()
                fw.op("pe", "matmul", [rm, xb], [rp], rp[0:M, :], lhsT=rm[0:M, 0:M], rhs=xb[0:M, :], start=True, stop=True)
                fw.op("dve", "tensor_tensor", [rp, sn], [t2], out=t2[0:M, :], in0=rp[0:M, :], in1=sn[0:M, :], op=ALU.mult)
                fw.op("pool", "tensor_tensor", [t1, t2], [t1], out=t1[0:M, :], in0=t1[0:M, :], in1=t2[0:M, :], op=ALU.add)
                fw.op("act", "copy", [t1], [ob], out=ob[0:M, :], in_=t1[0:M, :])
                if kmean is not None:
                    fw.op("dve", "tensor_reduce", [t1], [KM], out=kmean, in_=t1[:, :].rearrange("p (b j) -> p b j", b=2),
                          axis=AX.X, op=ALU.add)
                fw.dma(STQ, dst_ap, ob[0:M, :], reads=[ob], writes=[dst_buf])

            new_tails.append(tail)

        pcount = [0]

        def plain_out(acc, M, dst_ap, dst_buf):
            ob = ob_r.next()
            pcount[0] += 1
            if pcount[0] % 2:
                fw.op("act", "copy", [acc], [ob], out=ob[0:M, :], in_=acc[0:M, :])
            else:
                fw.op("dve", "tensor_copy", [acc], [ob], out=ob[0:M, :], in_=acc[0:M, :])
            fw.dma(STQ, dst_ap, ob[0:M, :], reads=[ob], writes=[dst_buf])

        def fm_block(coff, M):
            acc = acc_r.next()
            for k in range(16):
                fw.op("pe", "matmul", [wbuf(coff), aT], [acc], acc[0:M, :], lhsT=W[:, k, coff:coff + M], rhs=aT[:, k, :],
                                                    start=(k == 0), stop=(k == 15))
            return acc

        def tm_block(coff, sub):
            acc = acc_r.next()
            for k in range(16):
                fw.op("pe", "matmul", [wbuf(coff), aT], [acc], acc[:, :], lhsT=aT[:, k, sub * 128:(sub + 1) * 128],
                                                    rhs=W[:, k, coff:coff + 512], start=(k == 0), stop=(k == 15))
            return acc

        def tables(st):
            tok0 = st * 512
            fw.dma("sp", posi[:, :], posd[:, tok0:tok0 + 512], reads=[posd], writes=[posi])
            fw.op("pool", "tensor_copy", [posi], [posf], out=posf[:, :], in_=posi[:, :])
            for kind, col in (("128", 0), ("64", 1)):
                for fn, shift in (("s", 0.5), ("c", 0.75)):
                    tab = tabs[fn + kind]
                    fw.op("pool", "tensor_scalar", [posf, invf], [tt],
                          out=tt[:, :], in0=posf[:, :], scalar1=invf[:, col:col + 1], scalar2=shift,
                          op0=ALU.mult, op1=ALU.add)
                    fw.op("pool", "tensor_copy", [tt], [tki], out=tki[:, :], in_=tt[:, :])
                    fw.op("pool", "tensor_copy", [tki], [tkf], out=tkf[:, :], in_=tki[:, :])
                    fw.op("pool", "tensor_tensor", [tt, tkf], [tt], out=tt[:, :], in0=tt[:, :], in1=tkf[:, :], op=ALU.subtract)
                    fw.op("dve", "scalar_tensor_tensor", [tt], [tkf], out=tkf[:, :], in0=tt[:, :], scalar=0.0, in1=tt[:, :],
                          op0=ALU.is_lt, op1=ALU.add)
                    fw.op("act", "activation", [tkf, mpi], [tab], out=tab[:, :], in_=tkf[:, :], func=AF.Sin,
                          scale=2 * PI, bias=mpi[:, 0:1])

        def prep_sub(st, sub, aT_dst):
            tok0 = st * 512
            xs = xs_r.next(); a_s = as_r.next(); ss = ss_r.next(); rs = rs_r.next()
            fw.dma("sp", xs[:, :], xd[tok0 + sub * 128:tok0 + (sub + 1) * 128, :], reads=[xd], writes=[xs])
            norm_rows(xs[:, :], [xs], gat, a_s, ss, rs, D)
            transpose_into(a_s, 16, aT_dst, sub * 128, tp_r)

        def blocks_for(st):
            tok0 = st * 512
            bl = []
            if mode == "kv":
                def ckv_blk(sub):
                    acc = tm_block(0, sub)
                    ck = ckv_r.next(); ss = ss_r.next(); rs = rs_r.next()
                    norm_rows(acc[:, :], [acc], gkv, ck, ss, rs, 512)
                    transpose_into(ck, 4, kvnT, sub * 128, tp_r)

                def vmb_blk(sub, cb):
                    acc = tm_block(1600 + cb * 512, sub)
                    plain_out(acc, 128, Vm[tok0 + sub * 128:tok0 + (sub + 1) * 128, cb * 512:(cb + 1) * 512], Vm_b[st])

                def knope_blk(h):
                    acc = acc_r.next()
                    for k in range(4):
                        fw.op("pe", "matmul", [WK, kvnT], [acc],
                              acc[:, :], lhsT=WK[:, k, h * 128:(h + 1) * 128], rhs=kvnT[:, k, :],
                              start=(k == 0), stop=(k == 3))
                    plain_out(acc, 128, KTn[h, :, tok0:tok0 + 512], KTn_b[st])

                def vmla_blk(sub, cb):
                    acc = acc_r.next()
                    for k in range(4):
                        fw.op("pe", "matmul", [WV, kvnT], [acc],
                              acc[:, :], lhsT=kvnT[:, k, sub * 128:(sub + 1) * 128],
                              rhs=WV[:, k, cb * 512:(cb + 1) * 512], start=(k == 0), stop=(k == 3))
                    plain_out(acc, 128, Vn[tok0 + sub * 128:tok0 + (sub + 1) * 128, cb * 512:(cb + 1) * 512], Vn_b[st])

                def kpe_blk():
                    acc = fm_block(512, 64)
                    rope_out(acc, 64, "64", KTp[0:64, tok0:tok0 + 512], KTp_b[st])

                def kmb_blk(h):
                    acc = fm_block(576 + h * 128, 128)
                    rope_out(acc, 128, "128", KTm[h, :, tok0:tok0 + 512], KTm_b[st], kmean=KM[:, h, 2 * st:2 * st + 2])

                for sub in range(4):
                    bl.append(lambda sub=sub: ckv_blk(sub))
                for sub in range(2):
                    for cb in range(2):
                        bl.append(lambda sub=sub, cb=cb: vmb_blk(sub, cb))
                bl.append(kpe_blk)
                for h in range(4):
                    bl.append(lambda h=h: kmb_blk(h))
                for h in range(8):
                    bl.append(lambda h=h: knope_blk(h))
                for h in range(4, 8):
                    bl.append(lambda h=h: kmb_blk(h))
                for sub in range(4):
                    for cb in range(2):
                        bl.append(lambda sub=sub, cb=cb: vmla_blk(sub, cb))
                for sub in range(2, 4):
                    for cb in range(2):
                        bl.append(lambda sub=sub, cb=cb: vmb_blk(sub, cb))
            else:
                def qn_blk(h):
                    acc = fm_block(h * 192, 128)
                    plain_out(acc, 128, QTn[h, :, tok0:tok0 + 512], QTn_b[st])

                def qp_blk(h):
                    acc = fm_block(h * 192 + 128, 64)
                    rope_out(acc, 64, "64", QTp[h, :, tok0:tok0 + 512], QTp_b[st])

                def qm_blk(h):
                    acc = fm_block(1536 + h * 128, 128)
                    rope_out(acc, 128, "128", QTm[h, :, tok0:tok0 + 512], QTm_b[st])

                for h in range(8):
                    bl.append(lambda h=h: qn_blk(h))
                    bl.append(lambda h=h: qp_blk(h))
                for h in range(8):
                    bl.append(lambda h=h: qm_blk(h))
            return bl

        nt_run = ntile if 'onetile' not in DBG else 1
        cur = {"aT": aT_r.next()}
        for sub in range(4):
            prep_sub(0, sub, cur["aT"])
        load_weights()
        for st in range(nt_run):
            aT = cur["aT"]
            tables(st)
            if st > 0 and 'noprecast' not in DBG:
                precast_issue(pre_q, 4)
            bl = blocks_for(st)
            nxt = None
            ins_at = {}
            if st + 1 < nt_run:
                nxt = aT_r.next()
                step = len(bl) // 5
                for sub in range(4):
                    ins_at[step * (sub + 1)] = sub
            for bi, blk in enumerate(bl):
                blk()
                while pending_tails:
                    pending_tails.pop(0)()
                pending_tails.extend(new_tails)
                del new_tails[:]
                if bi in ins_at:
                    prep_sub(st + 1, ins_at[bi], nxt)
            while pending_tails:
                pending_tails.pop(0)()
            if nxt is not None:
                cur["aT"] = nxt
        if 'noprecast' not in DBG:
            precast_issue(pre_q, len(pre_q))

    if stop_after >= 1:
        proj_phase("kv")
    if stop_after >= 2:
        proj_phase("q")

    def attention_phase():
        fw.phase(PMARK)
        qidxb = fw.sb([128, NQ], F32, "qidxb")
        kidxc = fw.sb([128, NKT], F32, "kidxc")
        qidxi = fw.sb([128, NQC], I32, "qidxi")
        qblkf = fw.sb([128, NQC], F32, "qblkf")
        blki = fw.sb([128, NBLK], F32, "blki")
        pastm = fw.sb([128, NQC, NBLK], F32, "pastm")
        ownm = fw.sb([128, NQC, NBLK], F32, "ownm")
        pbias = fw.sb([128, NQC, NBLK], F32, "pbias")
        KMb = fw.sb([128, 8, NBLK], BF16, "KMb")
        fw.dma("sp", qidxb[:, :], qidxb_d[:, :], reads=[qidxb_d], writes=[qidxb])
        fw.dma("sp", kidxc[:, :], kidxc_d[:, :], reads=[kidxc_d], writes=[kidxc])
        fw.dma("sp", qidxi[:, :], qidxc_d[:, :], reads=[qidxc_d], writes=[qidxi])
        fw.dma("sp", blki[:, :], blki_d[:, :], reads=[blki_d], writes=[blki])
        fw.op("dve", "tensor_single_scalar", [qidxi], [qidxi], out=qidxi[:, :], in_=qidxi[:, :], scalar=8, op=ALU.arith_shift_right)
        fw.op("dve", "tensor_copy", [qidxi], [qblkf], out=qblkf[:, :], in_=qidxi[:, :])
        for ci in range(NQC):
            fw.op("dve", "tensor_scalar", [blki, qblkf], [pastm], out=pastm[:, ci, :], in0=blki[:, :], scalar1=qblkf[:, ci:ci + 1],
                                                          scalar2=None, op0=ALU.is_lt)
            fw.op("dve", "tensor_scalar", [blki, qblkf], [ownm], out=ownm[:, ci, :], in0=blki[:, :], scalar1=qblkf[:, ci:ci + 1],
                                                          scalar2=None, op0=ALU.is_equal)
        fw.op("dve", "tensor_scalar", [pastm], [pbias], out=pbias[:, :, :], in0=pastm[:, :, :], scalar1=-1.0, scalar2=1e30,
                                               op0=ALU.add, op1=ALU.mult)
        fw.op("act", "activation", [KM], [KMb], out=KMb[:, :, :], in_=KM[:, :, :], func=AF.Copy, scale=1.0 / 256)

        NCH = 2
        Kc = [fw.sb([128, 4096], BF16, "Kc%d" % i) for i in range(NCH)]
        Kp = [fw.sb([128, 4096], BF16, "Kp%d" % i) for i in range(NCH)]
        Vc = [fw.sb([128, 32, 129], BF16, "Vc%d" % i) for i in range(NCH)]
        for i in range(NCH):
            fw.op("pool", "memset", [], [Vc[i]], Vc[i][:, :, 128:129], 1.0)
        Qn = [fw.sb([128, 512], BF16) for _ in range(2)]
        Qp = [fw.sb([128, 512], BF16) for _ in range(2)]
        pt_r = Ring([fw.sb([128, 512], BF16) for _ in range(5)])
        ptm_r = Ring([fw.sb([128, 512], BF16) for _ in range(4)])
        sc_r = Ring([B[0], B[1], B[2]])
        osets = [(B[3], B[4]), (B[5], B[6])]
        gtp = B[7]
        accs = [fw.sb([128, 129], F32, "accs%d" % c) for c in range(4)]
        on_r = Ring([fw.sb([128, 128], BF16) for _ in range(2)])
        rinv_r = Ring([fw.sb([128, 1], F32) for _ in range(2)])
        ott_r = Ring([fw.sb([128, 512], BF16) for _ in range(2)])
        gm = fw.sb([128, 4, NBLK], F32, "gm")
        mx8 = fw.sb([128, 4, 8], F32, "mx8")
        sel = fw.sb([128, 4, NBLK], F32, "sel")

        items = []
        for j in range(NSLOT):
            for h in range(8):
                items.append((j, "mla", h))
            for h in range(8):
                items.append((j, "mb", h))
        loads = []
        for ii, (j, kind, h) in enumerate(items):
            nchunk = min(j + 1, (S + 4095) // 4096)
            for ch in range(nchunk):
                loads.append((ii, ch))
        state = {"li": 0}

        def issue_load(li):
            ii, ch = loads[li]
            j, kind, h = items[ii]
            slot = li % NCH
            k0 = ch * 4096
            nk = min(4096, S - k0)
            sts = list(range(k0 // 512, (k0 + nk) // 512))
            if kind == "mla":
                fw.dma("sp", Kc[slot][:, 0:nk], KTn[h, :, k0:k0 + nk], reads=[KTn_b[s] for s in sts], writes=[Kc[slot]])
                fw.dma("sp", Kp[slot][0:64, 0:nk], KTp[0:64, k0:k0 + nk], reads=[KTp_b[s] for s in sts], writes=[Kp[slot]])
                vsrc, vb = Vn, Vn_b
            else:
                fw.dma("sp", Kc[slot][:, 0:nk], KTm[h, :, k0:k0 + nk], reads=[KTm_b[s] for s in sts], writes=[Kc[slot]])
                vsrc, vb = Vm, Vm_b
            nt = nk // 128
            half = max(1, nt // 2)
            for t0 in range(0, nt, half):
                fw.dma("sp", Vc[slot][:, t0:t0 + half, 0:128],
                       vsrc[k0 + t0 * 128:k0 + (t0 + half) * 128, h * 128:(h + 1) * 128].rearrange("(t p) c -> p t c", p=128),
                       reads=[vb[s] for s in sts], writes=[Vc[slot]])
            return slot

        def issue_q(ii):
            j, kind, h = items[ii]
            qs = ii % 2
            if kind == "mla":
                fw.dma("sp", Qn[qs][:, :], QTn[h, :, j * 512:(j + 1) * 512], reads=[QTn_b[j]], writes=[Qn[qs]])
                fw.dma("sp", Qp[qs][0:64, :], QTp[h, :, j * 512:(j + 1) * 512], reads=[QTp_b[j]], writes=[Qp[qs]])
            else:
                fw.dma("sp", Qn[qs][:, :], QTm[h, :, j * 512:(j + 1) * 512], reads=[QTm_b[j]], writes=[Qn[qs]])

        def finish(get_o, h16, j, osrc_bufs):
            ott = ott_r.next()
            tv = gtp.t[:, :].bitcast(BF16)
            for c in range(4):
                rinv = rinv_r.next(); on = on_r.next()
                o_ap = get_o(c)
                fw.op("dve", "reciprocal", osrc_bufs, [rinv], out=rinv[:, :], in_=o_ap[:, 128:129])
                fw.op("dve", "tensor_scalar", list(osrc_bufs) + [rinv], [on],
                    out=on[:, :], in0=o_ap[:, 0:128], scalar1=rinv[:, 0:1], scalar2=None, op0=ALU.mult)
                fw.op("pe", "transpose", [on, identb], [gtp], out=tv[:, c * 128:(c + 1) * 128], in_=on[:, :], identity=identb[:, :])
            fw.op("act", "copy", [gtp], [ott], out=ott[:, :], in_=tv[:, 0:512])
            fw.dma(STQ, OT[h16, :, j * 512:(j + 1) * 512], ott[:, :], reads=[ott], writes=[OT_b[j]])

        issue_q(0)
        slot_of = {0: issue_load(0)}
        li_next = 1
        li = 0
        for ii, (j, kind, h) in enumerate(items):
            if ii + 1 < len(items):
                issue_q(ii + 1)
            qs = ii % 2
            nchunk = min(j + 1, (S + 4095) // 4096)
            nkt_total = min(32 * j + 32, NKT)
            scale = (192 ** -0.5) if kind == "mla" else (128 ** -0.5)
            if kind == "mb":
                for c in range(4):
                    fw.op("pe", "matmul", [Qn[qs], KMb], [gtp], gtp[:, c * NBLK:(c + 1) * NBLK], lhsT=Qn[qs][:, c * 128:(c + 1) * 128],
                                                        rhs=KMb[:, h, :], start=True, stop=True)
                fw.op("dve", "tensor_tensor", [gtp, pbias], [gm], out=gm[:, :, :], in0=gtp[:, 0:4 * NBLK].rearrange("p (c n) -> p c n", c=4),
                                                       in1=pbias[:, 4 * j:4 * j + 4, :], op=ALU.add)
                for c in range(4):
                    fw.op("dve", "max", [gm], [mx8], out=mx8[:, c, :], in_=gm[:, c, :])
                for c in range(4):
                    fw.op("dve", "tensor_scalar", [gm, mx8], [sel], out=sel[:, c, :], in0=gm[:, c, :], scalar1=mx8[:, c, 2:3],
                                                                scalar2=None, op0=ALU.is_ge)
                fw.op("dve", "tensor_tensor", [sel, pastm], [sel], out=sel[:, :, :], in0=sel[:, :, :], in1=pastm[:, 4 * j:4 * j + 4, :], op=ALU.mult)
                fw.op("dve", "tensor_tensor", [sel, ownm], [sel], out=sel[:, :, :], in0=sel[:, :, :], in1=ownm[:, 4 * j:4 * j + 4, :], op=ALU.add)
            tiles = []
            for ch in range(nchunk):
                kt0 = ch * 32
                for ktl in range(min(32, nkt_total - kt0)):
                    tiles.append((ch, ktl, kt0 + ktl))
            st = {"slot": None, "oset_i": 0}

            def stage_a(ch, ktl, kt):
                nonlocal li, li_next
                if ktl == 0:
                    st["slot"] = slot_of.pop(li)
                    if li_next < len(loads):
                        slot_of[li_next] = issue_load(li_next)
                        li_next += 1
                    li += 1
                slot = st["slot"]
                diag = (ch == nchunk - 1)
                sc = sc_r.next()
                last_qk = (kind != "mla")
                fw.op("pe", "matmul", [Kc[slot], Qn[qs]], [sc],
                      sc[:, :], lhsT=Kc[slot][:, ktl * 128:(ktl + 1) * 128], rhs=Qn[qs][:, :], start=True, stop=last_qk)
                if kind == "mla":
                    fw.op("pe", "matmul", [Kp[slot], Qp[qs]], [sc],
                          sc[:, :], lhsT=Kp[slot][0:64, ktl * 128:(ktl + 1) * 128], rhs=Qp[qs][0:64, :], start=False, stop=True)
                pt = pt_r.next()
                fw.op("act", "activation", [sc], [pt], out=pt[:, :], in_=sc[:, :], func=AF.Exp, scale=scale)
                if diag:
                    ptm = ptm_r.next()
                    fw.op("dve", "scalar_tensor_tensor", [qidxb, kidxc, pt], [ptm],
                          out=ptm[:, :], in0=qidxb[:, j * 512:(j + 1) * 512], scalar=kidxc[:, kt:kt + 1], in1=pt[:, :],
                          op0=ALU.is_ge, op1=ALU.mult)
                    pt = ptm
                return (pt, slot, ktl, kt)

            def stage_b(pt, slot, ktl, kt):
                if kind == "mla":
                    ob = osets[0]
                    first, last = (kt == 0), (kt == nkt_total - 1)
                else:
                    ob = osets[st["oset_i"] % 2]
                    first, last = (kt % 2 == 0), (kt % 2 == 1)
                for c in range(4):
                    bank = ob[c // 2]
                    c0 = (c % 2) * 129
                    fw.op("pe", "matmul", [pt, Vc[slot]], [bank],
                          bank[:, c0:c0 + 129], lhsT=pt[:, c * 128:(c + 1) * 128], rhs=Vc[slot][:, ktl, :],
                          start=(first and c % 2 == 0), stop=last)
                if kind == "mb" and last:
                    n = kt // 2
                    for c in range(4):
                        bank = ob[c // 2]
                        c0 = (c % 2) * 129
                        if n == 0:
                            fw.op("dve", "tensor_scalar", [bank, sel], [accs[c]],
                                  out=accs[c][:, :], in0=bank[:, c0:c0 + 129], scalar1=sel[:, c, n:n + 1], scalar2=None, op0=ALU.mult)
                        else:
                            fw.op("dve", "scalar_tensor_tensor", [bank, sel, accs[c]], [accs[c]],
                                  out=accs[c][:, :], in0=bank[:, c0:c0 + 129], scalar=sel[:, c, n:n + 1], in1=accs[c][:, :],
                                  op0=ALU.mult, op1=ALU.add)
                    st["oset_i"] += 1

            pend = []
            for t in tiles:
                if t[1] == 0:
                    while pend:
                        stage_b(*pend.pop(0))
                pend.append(stage_a(*t))
                if len(pend) > SKEW:
                    stage_b(*pend.pop(0))
            while pend:
                stage_b(*pend.pop(0))
            if kind == "mla":
                ob = osets[0]
                finish(lambda c: ob[c // 2][:, (c % 2) * 129:(c % 2) * 129 + 129], h, j, [ob[0], ob[1]])
            else:
                finish(lambda c: accs[c][:, :], 8 + h, j, accs)

    if stop_after >= 3:
        attention_phase()

    def tail_phase():
        fw.phase(PMARK)
        hb = fw.sb([128, 4, D], F32, "hb")
        fT = fw.sb([128, 16, 512], BF16, "fT")
        fs_r = Ring([fw.sb([128, D], BF16) for _ in range(2)])
        HF = NFF // 2
        actT = fw.sb([128, HF, 512], BF16, "actT")
        wa_r = Ring([fw.sb([128, 16, 256], BF16) for _ in range(3)])
        wg_r = Ring([fw.sb([128, 16, 128], BF16) for _ in range(3)])
        wu_r = Ring([fw.sb([128, 16, 128], BF16) for _ in range(3)])
        wd_r = Ring([fw.sb([128, 11, 512], BF16) for _ in range(3)])
        OTt = fw.sb([128, 16, 512], BF16, "OTt")
        pT = fw.sb([128, 2, 512], BF16, "pT")
        pin_r = Ring([fw.sb([128, 256], F32) for _ in range(2)])
        pb_r = Ring([fw.sb([128, 256], BF16) for _ in range(2)])
        wpp_r = Ring([fw.sb([128, 2, 256], BF16) for _ in range(2)])
        gn = fw.sb([128, D], F32, "gn")
        sg_r = Ring([fw.sb([128, 512], F32) for _ in range(2)])
        ss_r = Ring([fw.sb([128, 1], F32) for _ in range(4)])
        rs_r = Ring([fw.sb([128, 1], F32) for _ in range(4)])
        tp_r = Ring([B[0], B[1]])
        acc_r = Ring([B[2], B[3], B[4], B[5]])
        acc2_r = Ring([B[6], B[7]])
        outs = []
        Wo_v = Wo_b.t.rearrange("(k p) c -> p k c", p=128)
        Wpg_v = Wpg_b.t.rearrange("(k p) c -> p k c", p=128)
        Wpp_v = Wpp_b.t.rearrange("(k p) c -> p k c", p=128)
        Wd_v = Wd_b.t.rearrange("(k p) c -> p k c", p=128)

        def norm_to_fT(gain_d):
            fw.dma("sp", gn[:, :], gain_d[:, :], reads=[gain_d], writes=[gn])
            for sub in range(4):
                fs = fs_r.next(); ss = ss_r.next(); rs = rs_r.next()
                norm_rows(hb[:, sub, :], [hb], gn, fs, ss, rs, D)
                transpose_into(fs, 16, fT, sub * 128, tp_r)

        for j in range(NSLOT):
            tok0 = j * 512
            for sub in range(4):
                fw.dma("sp", hb[:, sub, :], x_own[tok0 + sub * 128:tok0 + (sub + 1) * 128, :], reads=[x_own], writes=[hb])
            for hh in range(16):
                fw.dma("sp", OTt[:, hh, :], OT[hh, :, tok0:tok0 + 512], reads=[OT_b[j]], writes=[OTt])
            for cb in range(8):
                wa = wa_r.next()
                fw.dma("sp", wa[:, :, :], Wo_v[:, :, cb * 256:(cb + 1) * 256], reads=[Wo_b], writes=[wa])
                for sub in range(4):
                    acc = acc_r.next()
                    for k in range(16):
                        fw.op("pe", "matmul", [OTt, wa], [acc],
                            acc[:, 0:256], lhsT=OTt[:, k, sub * 128:(sub + 1) * 128], rhs=wa[:, k, :], start=(k == 0), stop=(k == 15))
                    fw.op("dve", "tensor_tensor", [hb, acc], [hb],
                        out=hb[:, sub, cb * 256:(cb + 1) * 256], in0=hb[:, sub, cb * 256:(cb + 1) * 256], in1=acc[:, 0:256], op=ALU.add)
            norm_to_fT(g_ffn_d)
            for half in range(2):
                for fl in range(HF):
                    f = half * HF + fl
                    wg = wg_r.next(); wu = wu_r.next()
                    fw.dma("sp", wg[:, :, :], Wg_b.t[f].rearrange("p (k c) -> p k c", k=16), reads=[Wg_b], writes=[wg])
                    fw.dma("sp", wu[:, :, :], Wu_b.t[f].rearrange("p (k c) -> p k c", k=16), reads=[Wu_b], writes=[wu])
                    ga = acc_r.next(); ua = acc_r.next()
                    for k in range(16):
                        fw.op("pe", "matmul", [wg, fT], [ga], ga[:, :], lhsT=wg[:, k, :], rhs=fT[:, k, :],
                                                                          start=(k == 0), stop=(k == 15))
                    for k in range(16):
                        fw.op("pe", "matmul", [wu, fT], [ua], ua[:, :], lhsT=wu[:, k, :], rhs=fT[:, k, :],
                                                                          start=(k == 0), stop=(k == 15))
                    sg = sg_r.next()
                    fw.op("act", "activation", [ga], [sg], out=sg[:, :], in_=ga[:, :], func=AF.Silu)
                    fw.op("dve", "tensor_tensor", [sg, ua], [actT], out=actT[:, fl, :], in0=sg[:, :], in1=ua[:, :], op=ALU.mult)
                for cb in range(4):
                    wds = []
                    for q in range(HF // 11):
                        wd = wd_r.next()
                        r0 = half * HF + q * 11
                        fw.dma("sp", wd[:, :, :], Wd_v[:, r0:r0 + 11, cb * 512:(cb + 1) * 512], reads=[Wd_b], writes=[wd])
                        wds.append(wd)
                    for sub in range(4):
                        acc = acc2_r.next()
                        for fl in range(HF):
                            wd = wds[fl // 11]
                            fw.op("pe", "matmul", [actT, wd], [acc],
                                acc[:, :], lhsT=actT[:, fl, sub * 128:(sub + 1) * 128], rhs=wd[:, fl % 11, :],
                                start=(fl == 0), stop=(fl == HF - 1))
                        fw.op("dve", "tensor_tensor", [hb, acc], [hb],
                            out=hb[:, sub, cb * 512:(cb + 1) * 512], in0=hb[:, sub, cb * 512:(cb + 1) * 512], in1=acc[:, :], op=ALU.add)
            norm_to_fT(g_ple_d)
            for sub in range(4):
                pin = pin_r.next(); pb = pb_r.next()
                fw.dma("sp", pin[:, :], p_own[tok0 + sub * 128:tok0 + (sub + 1) * 128, :], reads=[p_own], writes=[pin])
                fw.op("act", "copy", [pin], [pb], out=pb[:, :], in_=pin[:, :])
                transpose_into(pb, 2, pT, sub * 128, tp_r)
            for cb in range(8):
                wa = wa_r.next(); wpp = wpp_r.next()
                fw.dma("sp", wa[:, :, :], Wpg_v[:, :, cb * 256:(cb + 1) * 256], reads=[Wpg_b], writes=[wa])
                fw.dma("sp", wpp[:, :, :], Wpp_v[:, :, cb * 256:(cb + 1) * 256], reads=[Wpp_b], writes=[wpp])
                for sub in range(4):
                    acc = acc_r.next(); acc2 = acc2_r.next()
                    for k in range(16):
                        fw.op("pe", "matmul", [fT, wa], [acc],
                            acc[:, 0:256], lhsT=fT[:, k, sub * 128:(sub + 1) * 128], rhs=wa[:, k, :], start=(k == 0), stop=(k == 15))
                    for k in range(2):
                        fw.op("pe", "matmul", [pT, wpp], [acc2],
                            acc2[:, 0:256], lhsT=pT[:, k, sub * 128:(sub + 1) * 128], rhs=wpp[:, k, :], start=(k == 0), stop=(k == 1))
                    sg = sg_r.next()
                    fw.op("act", "activation", [acc], [sg], out=sg[:, 0:256], in_=acc[:, 0:256], func=AF.Sigmoid)
                    fw.op("dve", "tensor_tensor", [sg, acc2], [sg], out=sg[:, 0:256], in0=sg[:, 0:256], in1=acc2[:, 0:256], op=ALU.mult)
                    fw.op("pool", "tensor_tensor", [hb, sg], [hb],
                        out=hb[:, sub, cb * 256:(cb + 1) * 256], in0=hb[:, sub, cb * 256:(cb + 1) * 256], in1=sg[:, 0:256], op=ALU.add)
            fw.dma("sp", gn[:, :], g_fin_d[:, :], reads=[g_fin_d], writes=[gn])
            for sub in range(4):
                fs = fs_r.next(); ss = ss_r.next(); rs = rs_r.next()
                fw.op("act", "activation", [hb], [fs, ss], out=fs[:, :], in_=hb[:, sub, :], func=AF.Square, accum_out=ss[:, :])
                rstd_from_ss(ss, rs, D)
                fw.op("dve", "scalar_tensor_tensor", [hb, rs, gn], [hb],
                    out=hb[:, sub, :], in0=hb[:, sub, :], scalar=rs[:, 0:1], in1=gn[:, :], op0=ALU.mult, op1=ALU.mult)
                outs.append(fw.dma("sp", y[tok0 + sub * 128:tok0 + (sub + 1) * 128, :], hb[:, sub, :], reads=[hb], writes=[y]))
        return outs

    outs = []
    if stop_after >= 4:
        outs = tail_phase()
    else:
        fw.phase(PMARK)
        dummy = fw.sb([128, 8], F32, "dummy")
        fw.op("pool", "memset", [], [dummy], dummy[:, :], 0.0)
        outs = [fw.dma("sp", y[0:128, 0:8], dummy[:, :], reads=[dummy], writes=[y])]
    stats = fw.emit(final_waits=outs)
    return nc, stats


def make_in_maps(S, NSLOT, x, p, positions, attn_norm, w_in, kv_norm, w_ukv, w_o, ffn_norm,
                 w_gate, w_up, w_down, ple_norm, w_ple_gate, w_ple_proj, final_norm):
    NQ = 512 * NSLOT
    x2 = np.ascontiguousarray(np.asarray(x, dtype=np.float32).reshape(S, D))
    p2 = np.asarray(p, dtype=np.float32).reshape(S, 256)
    pos = np.asarray(positions).reshape(S).astype(np.int32)

    def bc(v, n):
        return np.ascontiguousarray(np.broadcast_to(np.asarray(v, dtype=np.float32).reshape(1, n), (128, n)))

    ident = np.eye(128, dtype=np.float32)
    rm128 = np.zeros((128, 128), np.float32)
    for i in range(64):
        rm128[i + 64, i] = -1.0
        rm128[i, i + 64] = 1.0
    rm64 = np.zeros((128, 128), np.float32)
    for i in range(32):
        rm64[i + 32, i] = -1.0
        rm64[i, i + 32] = 1.0
    invf = np.zeros((128, 2), np.float32)
    i128 = (10000.0 ** (-(np.arange(64, dtype=np.float32) * 2.0 / 128))).astype(np.float32)
    i64 = (10000.0 ** (-(np.arange(32, dtype=np.float32) * 2.0 / 64))).astype(np.float32)
    invf[:, 0] = np.tile(i128, 2) / (2 * np.pi)
    invf[:, 1] = np.tile(i64, 4) / (2 * np.pi)
    blki = bc(np.arange(S // 256, dtype=np.float32), S // 256)
    kidxc = np.ascontiguousarray(np.arange(S, dtype=np.float32).reshape(S // 128, 128).T)
    shared = {
        "x_all": x2,
        "posb_all": np.ascontiguousarray(np.broadcast_to(pos.reshape(1, S), (128, S))),
        "kidxc": kidxc,
        "g_attn_b": bc(attn_norm, D), "g_ffn_b": bc(ffn_norm, D), "g_ple_b": bc(ple_norm, D),
        "g_fin_b": bc(final_norm, D), "g_kv_b": bc(kv_norm, 512),
        "w_in": np.ascontiguousarray(np.asarray(w_in, np.float32).reshape(D, INC)),
        "w_ukv": np.ascontiguousarray(np.asarray(w_ukv, np.float32).reshape(512, 2048)),
        "w_o": np.ascontiguousarray(np.asarray(w_o, np.float32).reshape(D, D)),
        "w_gate": np.ascontiguousarray(np.asarray(w_gate, np.float32).reshape(D, DFF)),
        "w_up": np.ascontiguousarray(np.asarray(w_up, np.float32).reshape(D, DFF)),
        "w_down": np.ascontiguousarray(np.asarray(w_down, np.float32).reshape(DFF, D)),
        "w_pg": np.ascontiguousarray(np.asarray(w_ple_gate, np.float32).reshape(D, D)),
        "w_pp": np.ascontiguousarray(np.asarray(w_ple_proj, np.float32).reshape(256, D)),
        "ident": ident, "rm128": rm128, "rm64": rm64, "invf": invf, "blki": blki,
    }
    in_maps, own_idx = [], []
    for c in range(NCORES):
        idx = np.concatenate([np.arange(512 * (8 * j + c), 512 * (8 * j + c) + 512) for j in range(NSLOT)])
        own_idx.append(idx)
        m = dict(shared)
        m["x_own"] = np.ascontiguousarray(x2[idx])
        m["p_own"] = np.ascontiguousarray(p2[idx])
        m["posb_own"] = np.ascontiguousarray(np.broadcast_to(pos[idx].reshape(1, NQ), (128, NQ)))
        m["qidxb"] = np.ascontiguousarray(np.broadcast_to(idx.astype(np.float32).reshape(1, NQ), (128, NQ)))
        m["qidxc"] = np.ascontiguousarray(idx.astype(np.int32).reshape(NQ // 128, 128).T)
        in_maps.append(m)
    return in_maps, own_idx


_CACHE = {}


def kernel(**inputs):
    S = 16384
    NSLOT = S // (512 * NCORES)
    if "nc" not in _CACHE:
        _CACHE["nc"] = build(S, NSLOT)[0]
    nc = _CACHE["nc"]
    in_maps, own_idx = make_in_maps(S, NSLOT, **inputs)
    res = run_bass_kernel_spmd(nc, in_maps, core_ids=list(range(NCORES)))
    out = np.zeros((S, D), np.float32)
    for c in range(NCORES):
        out[own_idx[c]] = np.asarray(res.results[c]["y"], dtype=np.float32)
    return out.reshape(1, S, D)
```

```python
import os
import numpy as np
import concourse.bass as bass
import concourse.mybir as mybir
from concourse.bass_utils import run_bass_kernel_spmd

F32 = mybir.dt.float32
BF16 = mybir.dt.bfloat16
I32 = mybir.dt.int32
ALU = mybir.AluOpType
AF = mybir.ActivationFunctionType
AX = mybir.AxisListType

D = 2048
DFF = 5632
NFF = DFF // 128
INC = 5184
EPS = 1e-6
PI = float(np.pi)
NCORES = 8

SAME_ENGINE_SYNC = True
DBG = set(os.environ.get('KDBG', '').split(','))
CUT = int(os.environ.get('KCUT', '100000000'))
STQ = 'act' if 'stact' in DBG else 'sp'
NDMASEM = 8
SKEW = 2
ARENA_F32 = 52600


class Buf:
    __slots__ = ("t", "writer", "readers", "name", "psum")

    def __init__(self, t=None, name="", psum=False):
        self.psum = psum
        self.t = t
        self.writer = None
        self.readers = {}
        self.name = name

    def __getitem__(self, idx):
        return self.t[idx]


class Op:
    __slots__ = ("eng", "fn", "deps", "needs_inc", "seq", "dma", "sem", "semval", "prev", "desc")

    def __init__(self, eng, fn):
        self.eng = eng
        self.fn = fn
        self.deps = []
        self.needs_inc = False
        self.seq = 0
        self.dma = False
        self.sem = None
        self.semval = 0
        self.prev = None


class FW:
    ENGS = ("pe", "act", "dve", "pool", "sp")

    def __init__(self, nc):
        self.nc = nc
        self.ops = {e: [] for e in self.ENGS}
        self.esem = {e: nc.alloc_semaphore("es_" + e) for e in ("pe", "act", "dve", "pool")}
        self.dsem = {q: [nc.alloc_semaphore("ds_%s%d" % (q, i)) for i in range(NDMASEM)]
                     for q in ("sp", "pool", "act")}
        self.dcnt = {q: [0] * NDMASEM for q in self.dsem}
        self.dlast = {q: [None] * NDMASEM for q in self.dsem}
        self.drr = {q: 0 for q in self.dsem}
        self.nbuf = 0
        self.base_deps = []
        self.arena = nc.alloc_sbuf_tensor("arena", [128, ARENA_F32], F32)
        self.aoff = 0
        self.banks = [Buf(nc.alloc_psum_tensor("bank%d" % i, [128, 512], F32), "bank%d" % i, psum=True)
                      for i in range(8)]

    def sb(self, shape, dt, name=None):
        self.nbuf += 1
        n = int(np.prod(shape[1:]))
        esz = 4 if dt in (F32, I32) else 2
        nf32 = (n * esz + 3) // 4
        nf32 = (nf32 + 7) // 8 * 8
        assert self.aoff + nf32 <= ARENA_F32, "SBUF arena overflow %d" % (self.aoff + nf32)
        ap = self.arena[:, self.aoff:self.aoff + nf32]
        self.aoff += nf32
        if dt != F32:
            ap = ap.bitcast(dt)
        ap = ap[:, 0:n]
        if len(shape) == 3:
            ap = ap.rearrange("p (a b) -> p a b", a=shape[1])
        elif len(shape) == 4:
            ap = ap.rearrange("p (a b c) -> p a b c", a=shape[1], b=shape[2])
        return Buf(ap, name or "sb%d" % self.nbuf)

    def dram(self, shape, dt, name, kind="Internal"):
        return Buf(self.nc.dram_tensor(name, list(shape), dt, kind=kind).ap(), name)

    def mark(self):
        return self.aoff

    def phase(self, mark):
        self.aoff = mark
        deps = []
        for e in ("pe", "act", "dve", "pool"):
            for o in reversed(self.ops[e]):
                if not o.dma:
                    o.needs_inc = True
                    deps.append(o)
                    break
        for q in self.dsem:
            for o in self.dlast[q]:
                if o is not None:
                    deps.append(o)
        self.base_deps = deps

    def _track(self, op, reads, writes):
        deps = list(self.base_deps)
        for b in reads:
            if b.writer is not None:
                deps.append(b.writer)
            if b.psum:
                deps.extend(r for r in b.readers.values() if r.eng != op.eng)
        for b in writes:
            if b.writer is not None:
                deps.append(b.writer)
            deps.extend(b.readers.values())
        for b in writes:
            b.writer = op
            b.readers = {}
        for b in reads:
            key = op.eng if not op.dma else (op.eng, id(op.sem))
            b.readers[key] = op
        seen = set()
        for d in deps:
            if d is op or id(d) in seen:
                continue
            seen.add(id(d))
            if not d.dma:
                if d.eng == op.eng and not op.dma:
                    if d.eng == "pe" or not SAME_ENGINE_SYNC:
                        continue
                d.needs_inc = True
            op.deps.append(d)

    def op(self, eng, meth, reads=(), writes=(), *args, **kw):
        self.nrec = getattr(self, "nrec", 0) + 1
        if self.nrec > CUT:
            return None
        o = Op(eng, lambda e: getattr(e, meth)(*args, **kw))
        o.desc = (self.nrec, eng, meth)
        self._track(o, reads, writes)
        self.ops[eng].append(o)
        return o

    def dma(self, q, out_ap, in_ap, reads=(), writes=()):
        self.nrec = getattr(self, "nrec", 0) + 1
        if self.nrec > CUT:
            return None
        o = Op(q, lambda e: e.dma_start(out=out_ap, in_=in_ap))
        o.desc = (self.nrec, q, "dma", str(out_ap)[:80])
        o.dma = True
        k = self.drr[q]
        self.drr[q] = (k + 1) % NDMASEM
        o.sem = self.dsem[q][k]
        self.dcnt[q][k] += 1
        o.semval = 16 * self.dcnt[q][k]
        o.prev = self.dlast[q][k]
        self.dlast[q][k] = o
        self._track(o, reads, writes)
        self.ops[q].append(o)
        return o

    def emit(self, final_waits=()):
        nc = self.nc
        for e in ("pe", "act", "dve", "pool"):
            n = 0
            for o in self.ops[e]:
                if o.dma:
                    continue
                if o.needs_inc:
                    n += 1
                    o.seq = n
        stats = {}

        def run(ename, eng):
            waited = {}
            nw = 0

            def wait(sem, val):
                nonlocal nw
                if waited.get(id(sem), 0) >= val:
                    return
                waited[id(sem)] = val
                eng.wait_ge(sem, val)
                nw += 1

            for o in self.ops[ename]:
                for d in o.deps:
                    if d.dma:
                        wait(d.sem, d.semval)
                    else:
                        wait(self.esem[d.eng], d.seq)
                if o.dma:
                    if o.prev is not None:
                        wait(o.sem, o.prev.semval)
                    ins = o.fn(eng)
                    ins.then_inc(o.sem, 16)
                else:
                    ins = o.fn(eng)
                    if o.needs_inc:
                        ins.then_inc(self.esem[ename], 1)
            if ename == "sp":
                for o in final_waits:
                    if o is not None:
                        wait(o.sem, o.semval)
            stats[ename] = (len(self.ops[ename]), nw)

        with nc.Block() as block:
            @block.tensor
            def _(eng):
                run("pe", eng)

            @block.scalar
            def _(eng):
                run("act", eng)

            @block.vector
            def _(eng):
                run("dve", eng)

            @block.gpsimd
            def _(eng):
                run("pool", eng)

            @block.sync
            def _(eng):
                run("sp", eng)
        return stats


class Ring:
    def __init__(self, items):
        self.items = list(items)
        self.i = 0

    def next(self):
        b = self.items[self.i % len(self.items)]
        self.i += 1
        return b


def build(S, NSLOT, stop_after=99):
    NT = S // 512
    NQ = 512 * NSLOT
    NKT = S // 128
    NBLK = S // 256
    NQC = NQ // 128
    assert NBLK >= 8

    nc = bass.Bass("TRN2", target_bir_lowering=False)
    fw = FW(nc)
    B = fw.banks

    def ext(name, shape, dt=F32):
        return fw.dram(shape, dt, name, kind="ExternalInput")

    x_all = ext("x_all", [S, D]); x_own = ext("x_own", [NQ, D]); p_own = ext("p_own", [NQ, 256])
    posb_all = ext("posb_all", [128, S], I32); posb_own = ext("posb_own", [128, NQ], I32)
    qidxb_d = ext("qidxb", [128, NQ]); qidxc_d = ext("qidxc", [128, NQC], I32)
    kidxc_d = ext("kidxc", [128, NKT])
    g_attn_d = ext("g_attn_b", [128, D]); g_ffn_d = ext("g_ffn_b", [128, D])
    g_ple_d = ext("g_ple_b", [128, D]); g_fin_d = ext("g_fin_b", [128, D])
    g_kv_d = ext("g_kv_b", [128, 512])
    w_in = ext("w_in", [D, INC]); w_ukv = ext("w_ukv", [512, 2048]); w_o = ext("w_o", [D, D])
    w_gate = ext("w_gate", [D, DFF]); w_up = ext("w_up", [D, DFF]); w_down = ext("w_down", [DFF, D])
    w_pg = ext("w_pg", [D, D]); w_pp = ext("w_pp", [256, D])
    ident_d = ext("ident", [128, 128]); rm128_d = ext("rm128", [128, 128]); rm64_d = ext("rm64", [128, 128])
    inv_d = ext("invf", [128, 2]); blki_d = ext("blki", [128, NBLK])
    y = fw.dram([NQ, D], F32, "y", kind="ExternalOutput")

    def scratch(name, shape, n):
        t = fw.dram(shape, BF16, name)
        return t, [Buf(None, "%s_%d" % (name, i)) for i in range(n)]

    KTn, KTn_b = scratch("KTn", [8, 128, S], NT)
    KTp, KTp_b = scratch("KTp", [64, S], NT)
    KTm, KTm_b = scratch("KTm", [8, 128, S], NT)
    Vn, Vn_b = scratch("Vn", [S, 1024], NT)
    Vm, Vm_b = scratch("Vm", [S, 1024], NT)
    QTn, QTn_b = scratch("QTn", [8, 128, NQ], NSLOT)
    QTp, QTp_b = scratch("QTp", [8, 64, NQ], NSLOT)
    QTm, QTm_b = scratch("QTm", [8, 128, NQ], NSLOT)
    OT, OT_b = scratch("OT", [16, 128, NQ], NSLOT)
    Wo_b = fw.dram([D, D], BF16, "Wo_b")
    Wpg_b = fw.dram([D, D], BF16, "Wpg_b")
    Wpp_b = fw.dram([256, D], BF16, "Wpp_b")
    Wd_b = fw.dram([DFF, D], BF16, "Wd_b")
    Wg_b = fw.dram([NFF, 128, 16 * 128], BF16, "Wg_b")
    Wu_b = fw.dram([NFF, 128, 16 * 128], BF16, "Wu_b")

    identb = fw.sb([128, 128], BF16, "identb")
    rm128b = fw.sb([128, 128], BF16, "rm128b")
    rm64b = fw.sb([128, 128], BF16, "rm64b")
    invf = fw.sb([128, 2], F32, "invf")
    mpi = fw.sb([128, 1], F32, "mpi")
    epst = fw.sb([128, 1], F32, "epst")
    KM = fw.sb([128, 8, NBLK], F32, "KM")
    fw.dma("pool", identb[:, :], ident_d[:, :], reads=[ident_d], writes=[identb])
    fw.dma("pool", rm128b[:, :], rm128_d[:, :], reads=[rm128_d], writes=[rm128b])
    fw.dma("pool", rm64b[:, :], rm64_d[:, :], reads=[rm64_d], writes=[rm64b])
    fw.dma("sp", invf[:, :], inv_d[:, :], reads=[inv_d], writes=[invf])
    fw.op("pool", "memset", [], [mpi], mpi[:, :], -PI)
    fw.op("pool", "memset", [], [epst], epst[:, :], EPS)
    PMARK = fw.mark()

    def rstd_from_ss(ss, rstd, n):
        fw.op("act", "activation", [ss, epst], [rstd], out=rstd[:, :], in_=ss[:, :], func=AF.Sqrt,
                                            scale=1.0 / n, bias=epst[:, 0:1])
        fw.op("dve", "reciprocal", [rstd], [rstd], out=rstd[:, :], in_=rstd[:, :])

    def norm_rows(src_ap, src_bufs, gain, dst, ss, rstd, n):
        fw.op("act", "activation", src_bufs, [dst, ss], out=dst[:, 0:n], in_=src_ap, func=AF.Square,
                                            accum_out=ss[:, :])
        rstd_from_ss(ss, rstd, n)
        fw.op("dve", "scalar_tensor_tensor", list(src_bufs) + [rstd, gain], [dst], out=dst[:, 0:n], in0=src_ap, scalar=rstd[:, 0:1],
                                                      in1=gain[:, 0:n], op0=ALU.mult, op1=ALU.mult)

    def transpose_into(src, nchunk, dstT, col0, tpbanks, evac_engs=("act", "dve")):
        for g0 in range(0, nchunk, 8):
            gn = min(8, nchunk - g0)
            bank = tpbanks.next()
            tv = bank.t[:, :].bitcast(BF16).rearrange("p (a b) -> p a b", a=8)
            for k in range(gn):
                fw.op("pe", "transpose", [src, identb], [bank],
                    out=tv[:, k, :], in_=src[:, (g0 + k) * 128:(g0 + k + 1) * 128], identity=identb[:, :])
            eng = evac_engs[(g0 // 8) % len(evac_engs)]
            if eng == "act":
                fw.op("act", "copy", [bank], [dstT],
                    out=dstT[:, g0:g0 + gn, col0:col0 + 128], in_=tv[:, 0:gn, :])
            else:
                fw.op("dve", "tensor_copy", [bank], [dstT],
                    out=dstT[:, g0:g0 + gn, col0:col0 + 128], in_=tv[:, 0:gn, :])

    def precast():
        q = []
        for r in range(4):
            q.append((Wo_b[r * 512:(r + 1) * 512, :], w_o[r * 512:(r + 1) * 512, :], w_o, Wo_b))
        for r in range(4):
            q.append((Wpg_b[r * 512:(r + 1) * 512, :], w_pg[r * 512:(r + 1) * 512, :], w_pg, Wpg_b))
        q.append((Wpp_b[:, :], w_pp[:, :], w_pp, Wpp_b))
        for r in range(11):
            q.append((Wd_b[r * 512:(r + 1) * 512, :], w_down[r * 512:(r + 1) * 512, :], w_down, Wd_b))
        for (src, dst) in ((w_gate, Wg_b), (w_up, Wu_b)):
            sv = src.t.rearrange("(k p) c -> p k c", p=128)
            for f in range(NFF):
                q.append((dst.t[f].rearrange("p (k c) -> p k c", k=16), sv[:, :, f * 128:(f + 1) * 128], src, dst))
        return q

    def precast_issue(q, n):
        for _ in range(min(n, len(q))):
            o, i, sb_, db_ = q.pop(0)
            fw.dma("pool", o, i, reads=[sb_], writes=[db_])

    def proj_phase(mode):
        fw.phase(PMARK)
        if mode == "kv":
            xd, posd, ntile = x_all, posb_all, NT
            ranges = [(1536, 2112), (3136, 5184)]
        else:
            xd, posd, ntile = x_own, posb_own, NSLOT
            ranges = [(0, 1536), (2112, 3136)]
        if mode == "kv":
            parts = [(0, 512, 1536), (512, 576, 2048), (576, 1600, 3136), (1600, 2624, 4160)]
        else:
            parts = [(0, 768, 0), (768, 1536, 768), (1536, 2560, 2112)]
        ncols = parts[-1][1]
        W = fw.sb([128, 16, ncols], BF16, "W_" + mode)
        Wp = [Buf(None, "Wp%d" % i) for i in range(len(parts))]

        def wbuf(coff):
            for i, (lo, hi, _) in enumerate(parts):
                if lo <= coff < hi:
                    return Wp[i]
            raise AssertionError(coff)

        wv = w_in.t.rearrange("(k p) c -> p k c", p=128)

        def load_part(i):
            lo, hi, src = parts[i]
            for k in range(16):
                fw.dma("pool", W[:, k, lo:hi], wv[:, k, src:src + (hi - lo)], reads=[w_in], writes=[Wp[i]])

        if mode == "kv":
            WK = fw.sb([128, 4, 1024], BF16, "WukvK")
            WV = fw.sb([128, 4, 1024], BF16, "WukvV")
            uv = w_ukv.t.rearrange("(k p) (h t c) -> p k h t c", p=128, h=8, t=2)
            gkv = fw.sb([128, 512], F32, "gkv")
            fw.dma("sp", gkv[:, :], g_kv_d[:, :], reads=[g_kv_d], writes=[gkv])

        def load_weights():
            load_part(0)
            if mode == "kv":
                for k in range(4):
                    fw.dma("pool", WK[:, k, :].rearrange("p (h c) -> p h c", h=8), uv[:, k, :, 0, :], reads=[w_ukv], writes=[WK])
                    fw.dma("pool", WV[:, k, :].rearrange("p (h c) -> p h c", h=8), uv[:, k, :, 1, :], reads=[w_ukv], writes=[WV])
            for i in range(1, len(parts)):
                load_part(i)

        pre_q = precast() if mode == "kv" else []
        gat = fw.sb([128, D], F32, "gat")
        fw.dma("sp", gat[:, :], g_attn_d[:, :], reads=[g_attn_d], writes=[gat])

        xs_r = Ring([fw.sb([128, D], F32) for _ in range(2)])
        as_r = Ring([fw.sb([128, D], BF16) for _ in range(2)])
        aT_r = Ring([fw.sb([128, 16, 512], BF16, "aT%d" % i) for i in range(2)])
        ss_r = Ring([fw.sb([128, 1], F32) for _ in range(4)])
        rs_r = Ring([fw.sb([128, 1], F32) for _ in range(4)])
        posi = fw.sb([128, 512], I32, "posi")
        posf = fw.sb([128, 512], F32, "posf")
        tt = fw.sb([128, 512], F32, "tt"); tki = fw.sb([128, 512], I32, "tki"); tkf = fw.sb([128, 512], F32, "tkf")
        tabs = {k: fw.sb([128, 512], F32, "tab" + k) for k in ("s128", "c128", "s64", "c64")}
        t1_r = Ring([fw.sb([128, 512], F32) for _ in range(2)])
        t2_r = Ring([fw.sb([128, 512], F32) for _ in range(2)])
        xb_r = Ring([fw.sb([128, 512], BF16) for _ in range(2)])
        ob_r = Ring([fw.sb([128, 512], BF16) for _ in range(4)])
        if mode == "kv":
            ckv_r = Ring([fw.sb([128, 512], BF16) for _ in range(2)])
            kvnT = fw.sb([128, 4, 512], BF16, "kvnT")
        tp_r = Ring([B[0], B[1]])
        acc_r = Ring([B[2], B[3], B[4], B[7]])
        rp_r = Ring([B[5], B[6]])

        pending_tails = []
        new_tails = []

        def rope_out(acc, M, kind, dst_ap, dst_buf, kmean=None):
            rm = rm128b if kind == "128" else rm64b
            cs, sn = tabs["c" + kind], tabs["s" + kind]
            xb = xb_r.next(); t1 = t1_r.next()
            fw.op("act", "copy", [acc], [xb], out=xb[0:M, :], in_=acc[0:M, :])
            fw.op("dve", "tensor_tensor", [acc, cs], [t1], out=t1[0:M, :], in0=acc[0:M, :], in1=cs[0:M, :], op=ALU.mult)

            def tail():
                rp = rp_r.next(); t2 = t2_r.next(); ob = ob_r.next()
                fw.op("pe", "matmul", [rm, xb], [rp], rp[0:M, :], lhsT=rm[0:M, 0:M], rhs=xb[0:M, :], start=True, stop=True)
                fw.op("dve", "tensor_tensor", [rp, sn], [t2], out=t2[0:M, :], in0=rp[0:M, :], in1=sn[0:M, :], op=ALU.mult)
                fw.op("pool", "tensor_tensor", [t1, t2], [t1], out=t1[0:M, :], in0=t1[0:M, :], in1=t2[0:M, :], op=ALU.add)
                fw.op("act", "copy", [t1], [ob], out=ob[0:M, :], in_=t1[0:M, :])
                if kmean is not None:
                    fw.op("dve", "tensor_reduce", [t1], [KM], out=kmean, in_=t1[:, :].rearrange("p (b j) -> p b j", b=2),
                          axis=AX.X, op=ALU.add)
                fw.dma(STQ, dst_ap, ob[0:M, :], reads=[ob], writes=[dst_buf])

            new_tails.append(tail)

        pcount = [0]

        def plain_out(acc, M, dst_ap, dst_buf):
            ob = ob_r.next()
            pcount[0] += 1
            if pcount[0] % 2:
                fw.op("act", "copy", [acc], [ob], out=ob[0:M, :], in_=acc[0:M, :])
            else:
                fw.op("dve", "tensor_copy", [acc], [ob], out=ob[0:M, :], in_=acc[0:M, :])
            fw.dma(STQ, dst_ap, ob[0:M, :], reads=[ob], writes=[dst_buf])

        def fm_block(coff, M):
            acc = acc_r.next()
            for k in range(16):
                fw.op("pe", "matmul", [wbuf(coff), aT], [acc], acc[0:M, :], lhsT=W[:, k, coff:coff + M], rhs=aT[:, k, :],
                                                    start=(k == 0), stop=(k == 15))
            return acc

        def tm_block(coff, sub):
            acc = acc_r.next()
            for k in range(16):
                fw.op("pe", "matmul", [wbuf(coff), aT], [acc], acc[:, :], lhsT=aT[:, k, sub * 128:(sub + 1) * 128],
                                                    rhs=W[:, k, coff:coff + 512], start=(k == 0), stop=(k == 15))
            return acc

        TABS = [("128", 0, "s", 0.5), ("128", 0, "c", 0.75), ("64", 1, "s", 0.5), ("64", 1, "c", 0.75)]

        def table_pos(st):
            tok0 = st * 512
            fw.dma("sp", posi[:, :], posd[:, tok0:tok0 + 512], reads=[posd], writes=[posi])
            fw.op("pool", "tensor_copy", [posi], [posf], out=posf[:, :], in_=posi[:, :])

        def table_pool(k):
            kind, col, fn, shift = TABS[k]
            fw.op("pool", "tensor_scalar", [posf, invf], [tt],
                  out=tt[:, :], in0=posf[:, :], scalar1=invf[:, col:col + 1], scalar2=shift,
                  op0=ALU.mult, op1=ALU.add)
            fw.op("pool", "tensor_copy", [tt], [tki], out=tki[:, :], in_=tt[:, :])
            fw.op("pool", "tensor_copy", [tki], [tkf], out=tkf[:, :], in_=tki[:, :])
            fw.op("pool", "tensor_tensor", [tt, tkf], [tt], out=tt[:, :], in0=tt[:, :], in1=tkf[:, :], op=ALU.subtract)
            fw.op("pool", "tensor_single_scalar", [tt], [tkf], out=tkf[:, :], in_=tt[:, :], scalar=0.0, op=ALU.is_lt)
            fw.op("pool", "tensor_tensor", [tt, tkf], [tkf], out=tkf[:, :], in0=tkf[:, :], in1=tt[:, :], op=ALU.add)

        def table_act(k):
            kind, col, fn, shift = TABS[k]
            tab = tabs[fn + kind]
            fw.op("act", "activation", [tkf, mpi], [tab], out=tab[:, :], in_=tkf[:, :], func=AF.Sin,
                  scale=2 * PI, bias=mpi[:, 0:1])

        def tables(st):
            table_pos(st)
            for k in range(4):
                table_pool(k)
                table_act(k)

        prep_as = {}

        def prep_norm(st, sub):
            tok0 = st * 512
            xs = xs_r.next(); a_s = as_r.next(); ss = ss_r.next(); rs = rs_r.next()
            fw.dma("sp", xs[:, :], xd[tok0 + sub * 128:tok0 + (sub + 1) * 128, :], reads=[xd], writes=[xs])
            norm_rows(xs[:, :], [xs], gat, a_s, ss, rs, D)
            prep_as[(st, sub)] = a_s

        def prep_tr(st, sub, aT_dst):
            transpose_into(prep_as.pop((st, sub)), 16, aT_dst, sub * 128, tp_r)

        def prep_sub(st, sub, aT_dst):
            prep_norm(st, sub)
            prep_tr(st, sub, aT_dst)

        def blocks_for(st):
            tok0 = st * 512
            bl = []
            if mode == "kv":
                def ckv_blk(sub):
                    acc = tm_block(0, sub)
                    ck = ckv_r.next(); ss = ss_r.next(); rs = rs_r.next()
                    norm_rows(acc[:, :], [acc], gkv, ck, ss, rs, 512)
                    new_tails.append(lambda: transpose_into(ck, 4, kvnT, sub * 128, tp_r))

                def vmb_blk(sub, cb):
                    acc = tm_block(1600 + cb * 512, sub)
                    plain_out(acc, 128, Vm[tok0 + sub * 128:tok0 + (sub + 1) * 128, cb * 512:(cb + 1) * 512], Vm_b[st])

                def knope_blk(h):
                    acc = acc_r.next()
                    for k in range(4):
                        fw.op("pe", "matmul", [WK, kvnT], [acc],
                              acc[:, :], lhsT=WK[:, k, h * 128:(h + 1) * 128], rhs=kvnT[:, k, :],
                              start=(k == 0), stop=(k == 3))
                    plain_out(acc, 128, KTn[h, :, tok0:tok0 + 512], KTn_b[st])

                def vmla_blk(sub, cb):
                    acc = acc_r.next()
                    for k in range(4):
                        fw.op("pe", "matmul", [WV, kvnT], [acc],
                              acc[:, :], lhsT=kvnT[:, k, sub * 128:(sub + 1) * 128],
                              rhs=WV[:, k, cb * 512:(cb + 1) * 512], start=(k == 0), stop=(k == 3))
                    plain_out(acc, 128, Vn[tok0 + sub * 128:tok0 + (sub + 1) * 128, cb * 512:(cb + 1) * 512], Vn_b[st])

                def kpe_blk():
                    acc = fm_block(512, 64)
                    rope_out(acc, 64, "64", KTp[0:64, tok0:tok0 + 512], KTp_b[st])

                def kmb_blk(h):
                    acc = fm_block(576 + h * 128, 128)
                    rope_out(acc, 128, "128", KTm[h, :, tok0:tok0 + 512], KTm_b[st], kmean=KM[:, h, 2 * st:2 * st + 2])

                for sub in range(4):
                    bl.append(lambda sub=sub: ckv_blk(sub))
                for sub in range(2):
                    for cb in range(2):
                        bl.append(lambda sub=sub, cb=cb: vmb_blk(sub, cb))
                for h in range(8):
                    bl.append(lambda h=h: knope_blk(h))
                for sub in range(2, 4):
                    for cb in range(2):
                        bl.append(lambda sub=sub, cb=cb: vmb_blk(sub, cb))
                for sub in range(4):
                    for cb in range(2):
                        bl.append(lambda sub=sub, cb=cb: vmla_blk(sub, cb))
                bl.append(kpe_blk)
                for h in range(8):
                    bl.append(lambda h=h: kmb_blk(h))
            else:
                def qn_blk(h):
                    acc = fm_block(h * 192, 128)
                    plain_out(acc, 128, QTn[h, :, tok0:tok0 + 512], QTn_b[st])

                def qp_blk(h):
                    acc = fm_block(h * 192 + 128, 64)
                    rope_out(acc, 64, "64", QTp[h, :, tok0:tok0 + 512], QTp_b[st])

                def qm_blk(h):
                    acc = fm_block(1536 + h * 128, 128)
                    rope_out(acc, 128, "128", QTm[h, :, tok0:tok0 + 512], QTm_b[st])

                for h in range(8):
                    bl.append(lambda h=h: qn_blk(h))
                    bl.append(lambda h=h: qp_blk(h))
                for h in range(8):
                    bl.append(lambda h=h: qm_blk(h))
            return bl

        nt_run = ntile if 'onetile' not in DBG else 1
        cur = {"aT": aT_r.next()}
        for sub in range(4):
            prep_sub(0, sub, cur["aT"])
        load_weights()
        for st in range(nt_run):
            aT = cur["aT"]
            if st > 0 and 'noprecast' not in DBG:
                precast_issue(pre_q, 4)
            bl = blocks_for(st)
            nxt = None
            sched = {}

            def at(pos, fn):
                sched.setdefault(max(0, min(pos, len(bl) - 1)), []).append(fn)

            if mode == "kv":
                table_pos(st)
                for k in range(4):
                    at(1 + 3 * k, lambda k=k: table_pool(k))
                    at(4 + 3 * k, lambda k=k: table_act(k))
            else:
                tables(st)
            if st + 1 < nt_run:
                nxt = aT_r.next()
                step = len(bl) // 5
                for sub in range(4):
                    at(step * (sub + 1) - 5, lambda sub=sub: prep_norm(st + 1, sub))
                    at(step * (sub + 1), lambda sub=sub: prep_tr(st + 1, sub, nxt))
            for bi, blk in enumerate(bl):
                blk()
                while pending_tails:
                    pending_tails.pop(0)()
                pending_tails.extend(new_tails)
                del new_tails[:]
                for fn in sched.get(bi, []):
                    fn()
            while pending_tails:
                pending_tails.pop(0)()
            if nxt is not None:
                cur["aT"] = nxt
        if 'noprecast' not in DBG:
            precast_issue(pre_q, len(pre_q))

    if stop_after >= 1:
        proj_phase("kv")
    if stop_after >= 2:
        proj_phase("q")

    def attention_phase():
        fw.phase(PMARK)
        qidxb = fw.sb([128, NQ], F32, "qidxb")
        kidxc = fw.sb([128, NKT], F32, "kidxc")
        qidxi = fw.sb([128, NQC], I32, "qidxi")
        qblkf = fw.sb([128, NQC], F32, "qblkf")
        blki = fw.sb([128, NBLK], F32, "blki")
        pastm = fw.sb([128, NQC, NBLK], F32, "pastm")
        ownm = fw.sb([128, NQC, NBLK], F32, "ownm")
        pbias = fw.sb([128, NQC, NBLK], F32, "pbias")
        KMb = fw.sb([128, 8, NBLK], BF16, "KMb")
        fw.dma("sp", qidxb[:, :], qidxb_d[:, :], reads=[qidxb_d], writes=[qidxb])
        fw.dma("sp", kidxc[:, :], kidxc_d[:, :], reads=[kidxc_d], writes=[kidxc])
        fw.dma("sp", qidxi[:, :], qidxc_d[:, :], reads=[qidxc_d], writes=[qidxi])
        fw.dma("sp", blki[:, :], blki_d[:, :], reads=[blki_d], writes=[blki])
        fw.op("dve", "tensor_single_scalar", [qidxi], [qidxi], out=qidxi[:, :], in_=qidxi[:, :], scalar=8, op=ALU.arith_shift_right)
        fw.op("dve", "tensor_copy", [qidxi], [qblkf], out=qblkf[:, :], in_=qidxi[:, :])
        for ci in range(NQC):
            fw.op("dve", "tensor_scalar", [blki, qblkf], [pastm], out=pastm[:, ci, :], in0=blki[:, :], scalar1=qblkf[:, ci:ci + 1],
                                                          scalar2=None, op0=ALU.is_lt)
            fw.op("dve", "tensor_scalar", [blki, qblkf], [ownm], out=ownm[:, ci, :], in0=blki[:, :], scalar1=qblkf[:, ci:ci + 1],
                                                          scalar2=None, op0=ALU.is_equal)
        fw.op("dve", "tensor_scalar", [pastm], [pbias], out=pbias[:, :, :], in0=pastm[:, :, :], scalar1=-1.0, scalar2=1e30,
                                               op0=ALU.add, op1=ALU.mult)
        fw.op("act", "activation", [KM], [KMb], out=KMb[:, :, :], in_=KM[:, :, :], func=AF.Copy, scale=1.0 / 256)

        NCH = 2
        Kc = [fw.sb([128, 4096], BF16, "Kc%d" % i) for i in range(NCH)]
        Kp = [fw.sb([128, 4096], BF16, "Kp%d" % i) for i in range(NCH)]
        Vc = [fw.sb([128, 32, 129], BF16, "Vc%d" % i) for i in range(NCH)]
        for i in range(NCH):
            fw.op("pool", "memset", [], [Vc[i]], Vc[i][:, :, 128:129], 1.0)
        Qn = [fw.sb([128, 512], BF16) for _ in range(2)]
        Qp = [fw.sb([128, 512], BF16) for _ in range(2)]
        pt_r = Ring([fw.sb([128, 512], BF16) for _ in range(5)])
        ptm_r = Ring([fw.sb([128, 512], BF16) for _ in range(4)])
        sc_r = Ring([B[0], B[1], B[2]])
        osets = [(B[3], B[4]), (B[5], B[6])]
        gtp = B[7]
        accs = [fw.sb([128, 129], F32, "accs%d" % c) for c in range(4)]
        on_r = Ring([fw.sb([128, 128], BF16) for _ in range(2)])
        rinv_r = Ring([fw.sb([128, 1], F32) for _ in range(2)])
        ott_r = Ring([fw.sb([128, 512], BF16) for _ in range(2)])
        gm = fw.sb([128, 4, NBLK], F32, "gm")
        mx8 = fw.sb([128, 4, 8], F32, "mx8")
        sel = fw.sb([128, 4, NBLK], F32, "sel")

        items = []
        for j in range(NSLOT):
            for h in range(8):
                items.append((j, "mla", h))
            for h in range(8):
                items.append((j, "mb", h))
        loads = []
        for ii, (j, kind, h) in enumerate(items):
            nchunk = min(j + 1, (S + 4095) // 4096)
            for ch in range(nchunk):
                loads.append((ii, ch))
        state = {"li": 0}

        def issue_load(li):
            ii, ch = loads[li]
            j, kind, h = items[ii]
            slot = li % NCH
            k0 = ch * 4096
            nk = min(4096, S - k0)
            sts = list(range(k0 // 512, (k0 + nk) // 512))
            if kind == "mla":
                fw.dma("sp", Kc[slot][:, 0:nk], KTn[h, :, k0:k0 + nk], reads=[KTn_b[s] for s in sts], writes=[Kc[slot]])
                fw.dma("sp", Kp[slot][0:64, 0:nk], KTp[0:64, k0:k0 + nk], reads=[KTp_b[s] for s in sts], writes=[Kp[slot]])
                vsrc, vb = Vn, Vn_b
            else:
                fw.dma("sp", Kc[slot][:, 0:nk], KTm[h, :, k0:k0 + nk], reads=[KTm_b[s] for s in sts], writes=[Kc[slot]])
                vsrc, vb = Vm, Vm_b
            nt = nk // 128
            half = max(1, nt // 2)
            for t0 in range(0, nt, half):
                fw.dma("sp", Vc[slot][:, t0:t0 + half, 0:128],
                       vsrc[k0 + t0 * 128:k0 + (t0 + half) * 128, h * 128:(h + 1) * 128].rearrange("(t p) c -> p t c", p=128),
                       reads=[vb[s] for s in sts], writes=[Vc[slot]])
            return slot

        def issue_q(ii):
            j, kind, h = items[ii]
            qs = ii % 2
            if kind == "mla":
                fw.dma("sp", Qn[qs][:, :], QTn[h, :, j * 512:(j + 1) * 512], reads=[QTn_b[j]], writes=[Qn[qs]])
                fw.dma("sp", Qp[qs][0:64, :], QTp[h, :, j * 512:(j + 1) * 512], reads=[QTp_b[j]], writes=[Qp[qs]])
            else:
                fw.dma("sp", Qn[qs][:, :], QTm[h, :, j * 512:(j + 1) * 512], reads=[QTm_b[j]], writes=[Qn[qs]])

        def finish(get_o, h16, j, osrc_bufs):
            ott = ott_r.next()
            tv = gtp.t[:, :].bitcast(BF16)
            for c in range(4):
                rinv = rinv_r.next(); on = on_r.next()
                o_ap = get_o(c)
                fw.op("dve", "reciprocal", osrc_bufs, [rinv], out=rinv[:, :], in_=o_ap[:, 128:129])
                fw.op("dve", "tensor_scalar", list(osrc_bufs) + [rinv], [on],
                    out=on[:, :], in0=o_ap[:, 0:128], scalar1=rinv[:, 0:1], scalar2=None, op0=ALU.mult)
                fw.op("pe", "transpose", [on, identb], [gtp], out=tv[:, c * 128:(c + 1) * 128], in_=on[:, :], identity=identb[:, :])
            fw.op("act", "copy", [gtp], [ott], out=ott[:, :], in_=tv[:, 0:512])
            fw.dma(STQ, OT[h16, :, j * 512:(j + 1) * 512], ott[:, :], reads=[ott], writes=[OT_b[j]])

        issue_q(0)
        slot_of = {0: issue_load(0)}
        li_next = 1
        li = 0
        for ii, (j, kind, h) in enumerate(items):
            if ii + 1 < len(items):
                issue_q(ii + 1)
            qs = ii % 2
            nchunk = min(j + 1, (S + 4095) // 4096)
            nkt_total = min(32 * j + 32, NKT)
            scale = (192 ** -0.5) if kind == "mla" else (128 ** -0.5)
            if kind == "mb":
                for c in range(4):
                    fw.op("pe", "matmul", [Qn[qs], KMb], [gtp], gtp[:, c * NBLK:(c + 1) * NBLK], lhsT=Qn[qs][:, c * 128:(c + 1) * 128],
                                                        rhs=KMb[:, h, :], start=True, stop=True)
                fw.op("dve", "tensor_tensor", [gtp, pbias], [gm], out=gm[:, :, :], in0=gtp[:, 0:4 * NBLK].rearrange("p (c n) -> p c n", c=4),
                                                       in1=pbias[:, 4 * j:4 * j + 4, :], op=ALU.add)
                for c in range(4):
                    fw.op("dve", "max", [gm], [mx8], out=mx8[:, c, :], in_=gm[:, c, :])
                for c in range(4):
                    fw.op("dve", "tensor_scalar", [gm, mx8], [sel], out=sel[:, c, :], in0=gm[:, c, :], scalar1=mx8[:, c, 2:3],
                                                                scalar2=None, op0=ALU.is_ge)
                fw.op("dve", "tensor_tensor", [sel, pastm], [sel], out=sel[:, :, :], in0=sel[:, :, :], in1=pastm[:, 4 * j:4 * j + 4, :], op=ALU.mult)
                fw.op("dve", "tensor_tensor", [sel, ownm], [sel], out=sel[:, :, :], in0=sel[:, :, :], in1=ownm[:, 4 * j:4 * j + 4, :], op=ALU.add)
            tiles = []
            for ch in range(nchunk):
                kt0 = ch * 32
                for ktl in range(min(32, nkt_total - kt0)):
                    tiles.append((ch, ktl, kt0 + ktl))
            st = {"slot": None, "oset_i": 0}

            def stage_a(ch, ktl, kt):
                nonlocal li, li_next
                if ktl == 0:
                    st["slot"] = slot_of.pop(li)
                    if li_next < len(loads):
                        slot_of[li_next] = issue_load(li_next)
                        li_next += 1
                    li += 1
                slot = st["slot"]
                diag = (ch == nchunk - 1)
                sc = sc_r.next()
                last_qk = (kind != "mla")
                fw.op("pe", "matmul", [Kc[slot], Qn[qs]], [sc],
                      sc[:, :], lhsT=Kc[slot][:, ktl * 128:(ktl + 1) * 128], rhs=Qn[qs][:, :], start=True, stop=last_qk)
                if kind == "mla":
                    fw.op("pe", "matmul", [Kp[slot], Qp[qs]], [sc],
                          sc[:, :], lhsT=Kp[slot][0:64, ktl * 128:(ktl + 1) * 128], rhs=Qp[qs][0:64, :], start=False, stop=True)
                pt = pt_r.next()
                fw.op("act", "activation", [sc], [pt], out=pt[:, :], in_=sc[:, :], func=AF.Exp, scale=scale)
                if diag:
                    ptm = ptm_r.next()
                    fw.op("dve", "scalar_tensor_tensor", [qidxb, kidxc, pt], [ptm],
                          out=ptm[:, :], in0=qidxb[:, j * 512:(j + 1) * 512], scalar=kidxc[:, kt:kt + 1], in1=pt[:, :],
                          op0=ALU.is_ge, op1=ALU.mult)
                    pt = ptm
                return (pt, slot, ktl, kt)

            def stage_b(pt, slot, ktl, kt):
                if kind == "mla":
                    ob = osets[0]
                    first, last = (kt == 0), (kt == nkt_total - 1)
                else:
                    ob = osets[st["oset_i"] % 2]
                    first, last = (kt % 2 == 0), (kt % 2 == 1)
                for c in range(4):
                    bank = ob[c // 2]
                    c0 = (c % 2) * 129
                    fw.op("pe", "matmul", [pt, Vc[slot]], [bank],
                          bank[:, c0:c0 + 129], lhsT=pt[:, c * 128:(c + 1) * 128], rhs=Vc[slot][:, ktl, :],
                          start=(first and c % 2 == 0), stop=last)
                if kind == "mb" and last:
                    n = kt // 2
                    for c in range(4):
                        bank = ob[c // 2]
                        c0 = (c % 2) * 129
                        if n == 0:
                            fw.op("dve", "tensor_scalar", [bank, sel], [accs[c]],
                                  out=accs[c][:, :], in0=bank[:, c0:c0 + 129], scalar1=sel[:, c, n:n + 1], scalar2=None, op0=ALU.mult)
                        else:
                            fw.op("dve", "scalar_tensor_tensor", [bank, sel, accs[c]], [accs[c]],
                                  out=accs[c][:, :], in0=bank[:, c0:c0 + 129], scalar=sel[:, c, n:n + 1], in1=accs[c][:, :],
                                  op0=ALU.mult, op1=ALU.add)
                    st["oset_i"] += 1

            pend = []
            for t in tiles:
                if t[1] == 0:
                    while pend:
                        stage_b(*pend.pop(0))
                pend.append(stage_a(*t))
                if len(pend) > SKEW:
                    stage_b(*pend.pop(0))
            while pend:
                stage_b(*pend.pop(0))
            if kind == "mla":
                ob = osets[0]
                finish(lambda c: ob[c // 2][:, (c % 2) * 129:(c % 2) * 129 + 129], h, j, [ob[0], ob[1]])
            else:
                finish(lambda c: accs[c][:, :], 8 + h, j, accs)

    if stop_after >= 3:
        attention_phase()

    def tail_phase():
        fw.phase(PMARK)
        hb = fw.sb([128, 4, D], F32, "hb")
        fT = fw.sb([128, 16, 512], BF16, "fT")
        fs_r = Ring([fw.sb([128, D], BF16) for _ in range(2)])
        HF = NFF // 2
        actT = fw.sb([128, HF, 512], BF16, "actT")
        wa_r = Ring([fw.sb([128, 16, 256], BF16) for _ in range(3)])
        wg_r = Ring([fw.sb([128, 16, 128], BF16) for _ in range(3)])
        wu_r = Ring([fw.sb([128, 16, 128], BF16) for _ in range(3)])
        wd_r = Ring([fw.sb([128, 11, 512], BF16) for _ in range(3)])
        OTt = fw.sb([128, 16, 512], BF16, "OTt")
        pT = fw.sb([128, 2, 512], BF16, "pT")
        pin_r = Ring([fw.sb([128, 256], F32) for _ in range(2)])
        pb_r = Ring([fw.sb([128, 256], BF16) for _ in range(2)])
        wpp_r = Ring([fw.sb([128, 2, 256], BF16) for _ in range(2)])
        gn = fw.sb([128, D], F32, "gn")
        sg_r = Ring([fw.sb([128, 512], F32) for _ in range(2)])
        ss_r = Ring([fw.sb([128, 1], F32) for _ in range(4)])
        rs_r = Ring([fw.sb([128, 1], F32) for _ in range(4)])
        tp_r = Ring([B[0], B[1]])
        acc_r = Ring([B[2], B[3], B[4], B[5]])
        acc2_r = Ring([B[6], B[7]])
        outs = []
        Wo_v = Wo_b.t.rearrange("(k p) c -> p k c", p=128)
        Wpg_v = Wpg_b.t.rearrange("(k p) c -> p k c", p=128)
        Wpp_v = Wpp_b.t.rearrange("(k p) c -> p k c", p=128)
        Wd_v = Wd_b.t.rearrange("(k p) c -> p k c", p=128)

        def norm_to_fT(gain_d):
            fw.dma("sp", gn[:, :], gain_d[:, :], reads=[gain_d], writes=[gn])
            for sub in range(4):
                fs = fs_r.next(); ss = ss_r.next(); rs = rs_r.next()
                norm_rows(hb[:, sub, :], [hb], gn, fs, ss, rs, D)
                transpose_into(fs, 16, fT, sub * 128, tp_r)

        for j in range(NSLOT):
            tok0 = j * 512
            for sub in range(4):
                fw.dma("sp", hb[:, sub, :], x_own[tok0 + sub * 128:tok0 + (sub + 1) * 128, :], reads=[x_own], writes=[hb])
            for hh in range(16):
                fw.dma("sp", OTt[:, hh, :], OT[hh, :, tok0:tok0 + 512], reads=[OT_b[j]], writes=[OTt])
            for cb in range(8):
                wa = wa_r.next()
                fw.dma("sp", wa[:, :, :], Wo_v[:, :, cb * 256:(cb + 1) * 256], reads=[Wo_b], writes=[wa])
                for sub in range(4):
                    acc = acc_r.next()
                    for k in range(16):
                        fw.op("pe", "matmul", [OTt, wa], [acc],
                            acc[:, 0:256], lhsT=OTt[:, k, sub * 128:(sub + 1) * 128], rhs=wa[:, k, :], start=(k == 0), stop=(k == 15))
                    fw.op("dve", "tensor_tensor", [hb, acc], [hb],
                        out=hb[:, sub, cb * 256:(cb + 1) * 256], in0=hb[:, sub, cb * 256:(cb + 1) * 256], in1=acc[:, 0:256], op=ALU.add)
            norm_to_fT(g_ffn_d)
            for half in range(2):
                for fl in range(HF):
                    f = half * HF + fl
                    wg = wg_r.next(); wu = wu_r.next()
                    fw.dma("sp", wg[:, :, :], Wg_b.t[f].rearrange("p (k c) -> p k c", k=16), reads=[Wg_b], writes=[wg])
                    fw.dma("sp", wu[:, :, :], Wu_b.t[f].rearrange("p (k c) -> p k c", k=16), reads=[Wu_b], writes=[wu])
                    ga = acc_r.next(); ua = acc_r.next()
                    for k in range(16):
                        fw.op("pe", "matmul", [wg, fT], [ga], ga[:, :], lhsT=wg[:, k, :], rhs=fT[:, k, :],
                                                                          start=(k == 0), stop=(k == 15))
                    for k in range(16):
                        fw.op("pe", "matmul", [wu, fT], [ua], ua[:, :], lhsT=wu[:, k, :], rhs=fT[:, k, :],
                                                                          start=(k == 0), stop=(k == 15))
                    sg = sg_r.next()
                    fw.op("act", "activation", [ga], [sg], out=sg[:, :], in_=ga[:, :], func=AF.Silu)
                    fw.op("dve", "tensor_tensor", [sg, ua], [actT], out=actT[:, fl, :], in0=sg[:, :], in1=ua[:, :], op=ALU.mult)
                for cb in range(4):
                    wds = []
                    for q in range(HF // 11):
                        wd = wd_r.next()
                        r0 = half * HF + q * 11
                        fw.dma("sp", wd[:, :, :], Wd_v[:, r0:r0 + 11, cb * 512:(cb + 1) * 512], reads=[Wd_b], writes=[wd])
                        wds.append(wd)
                    for sub in range(4):
                        acc = acc2_r.next()
                        for fl in range(HF):
                            wd = wds[fl // 11]
                            fw.op("pe", "matmul", [actT, wd], [acc],
                                acc[:, :], lhsT=actT[:, fl, sub * 128:(sub + 1) * 128], rhs=wd[:, fl % 11, :],
                                start=(fl == 0), stop=(fl == HF - 1))
                        fw.op("dve", "tensor_tensor", [hb, acc], [hb],
                            out=hb[:, sub, cb * 512:(cb + 1) * 512], in0=hb[:, sub, cb * 512:(cb + 1) * 512], in1=acc[:, :], op=ALU.add)
            norm_to_fT(g_ple_d)
            for sub in range(4):
                pin = pin_r.next(); pb = pb_r.next()
                fw.dma("sp", pin[:, :], p_own[tok0 + sub * 128:tok0 + (sub + 1) * 128, :], reads=[p_own], writes=[pin])
                fw.op("act", "copy", [pin], [pb], out=pb[:, :], in_=pin[:, :])
                transpose_into(pb, 2, pT, sub * 128, tp_r)
            for cb in range(8):
                wa = wa_r.next(); wpp = wpp_r.next()
                fw.dma("sp", wa[:, :, :], Wpg_v[:, :, cb * 256:(cb + 1) * 256], reads=[Wpg_b], writes=[wa])
                fw.dma("sp", wpp[:, :, :], Wpp_v[:, :, cb * 256:(cb + 1) * 256], reads=[Wpp_b], writes=[wpp])
                for sub in range(4):
                    acc = acc_r.next(); acc2 = acc2_r.next()
                    for k in range(16):
                        fw.op("pe", "matmul", [fT, wa], [acc],
                            acc[:, 0:256], lhsT=fT[:, k, sub * 128:(sub + 1) * 128], rhs=wa[:, k, :], start=(k == 0), stop=(k == 15))
                    for k in range(2):
                        fw.op("pe", "matmul", [pT, wpp], [acc2],
                            acc2[:, 0:256], lhsT=pT[:, k, sub * 128:(sub + 1) * 128], rhs=wpp[:, k, :], start=(k == 0), stop=(k == 1))
                    sg = sg_r.next()
                    fw.op("act", "activation", [acc], [sg], out=sg[:, 0:256], in_=acc[:, 0:256], func=AF.Sigmoid)
                    fw.op("dve", "tensor_tensor", [sg, acc2], [sg], out=sg[:, 0:256], in0=sg[:, 0:256], in1=acc2[:, 0:256], op=ALU.mult)
                    fw.op("pool", "tensor_tensor", [hb, sg], [hb],
                        out=hb[:, sub, cb * 256:(cb + 1) * 256], in0=hb[:, sub, cb * 256:(cb + 1) * 256], in1=sg[:, 0:256], op=ALU.add)
            fw.dma("sp", gn[:, :], g_fin_d[:, :], reads=[g_fin_d], writes=[gn])
            for sub in range(4):
                fs = fs_r.next(); ss = ss_r.next(); rs = rs_r.next()
                fw.op("act", "activation", [hb], [fs, ss], out=fs[:, :], in_=hb[:, sub, :], func=AF.Square, accum_out=ss[:, :])
                rstd_from_ss(ss, rs, D)
                fw.op("dve", "scalar_tensor_tensor", [hb, rs, gn], [hb],
                    out=hb[:, sub, :], in0=hb[:, sub, :], scalar=rs[:, 0:1], in1=gn[:, :], op0=ALU.mult, op1=ALU.mult)
                outs.append(fw.dma("sp", y[tok0 + sub * 128:tok0 + (sub + 1) * 128, :], hb[:, sub, :], reads=[hb], writes=[y]))
        return outs

    outs = []
    if stop_after >= 4:
        outs = tail_phase()
    else:
        fw.phase(PMARK)
        dummy = fw.sb([128, 8], F32, "dummy")
        fw.op("pool", "memset", [], [dummy], dummy[:, :], 0.0)
        outs = [fw.dma("sp", y[0:128, 0:8], dummy[:, :], reads=[dummy], writes=[y])]
    stats = fw.emit(final_waits=outs)
    return nc, stats


def make_in_maps(S, NSLOT, x, p, positions, attn_norm, w_in, kv_norm, w_ukv, w_o, ffn_norm,
                 w_gate, w_up, w_down, ple_norm, w_ple_gate, w_ple_proj, final_norm):
    NQ = 512 * NSLOT
    x2 = np.ascontiguousarray(np.asarray(x, dtype=np.float32).reshape(S, D))
    p2 = np.asarray(p, dtype=np.float32).reshape(S, 256)
    pos = np.asarray(positions).reshape(S).astype(np.int32)

    def bc(v, n):
        return np.ascontiguousarray(np.broadcast_to(np.asarray(v, dtype=np.float32).reshape(1, n), (128, n)))

    ident = np.eye(128, dtype=np.float32)
    rm128 = np.zeros((128, 128), np.float32)
    for i in range(64):
        rm128[i + 64, i] = -1.0
        rm128[i, i + 64] = 1.0
    rm64 = np.zeros((128, 128), np.float32)
    for i in range(32):
        rm64[i + 32, i] = -1.0
        rm64[i, i + 32] = 1.0
    invf = np.zeros((128, 2), np.float32)
    i128 = (10000.0 ** (-(np.arange(64, dtype=np.float32) * 2.0 / 128))).astype(np.float32)
    i64 = (10000.0 ** (-(np.arange(32, dtype=np.float32) * 2.0 / 64))).astype(np.float32)
    invf[:, 0] = np.tile(i128, 2) / (2 * np.pi)
    invf[:, 1] = np.tile(i64, 4) / (2 * np.pi)
    blki = bc(np.arange(S // 256, dtype=np.float32), S // 256)
    kidxc = np.ascontiguousarray(np.arange(S, dtype=np.float32).reshape(S // 128, 128).T)
    shared = {
        "x_all": x2,
        "posb_all": np.ascontiguousarray(np.broadcast_to(pos.reshape(1, S), (128, S))),
        "kidxc": kidxc,
        "g_attn_b": bc(attn_norm, D), "g_ffn_b": bc(ffn_norm, D), "g_ple_b": bc(ple_norm, D),
        "g_fin_b": bc(final_norm, D), "g_kv_b": bc(kv_norm, 512),
        "w_in": np.ascontiguousarray(np.asarray(w_in, np.float32).reshape(D, INC)),
        "w_ukv": np.ascontiguousarray(np.asarray(w_ukv, np.float32).reshape(512, 2048)),
        "w_o": np.ascontiguousarray(np.asarray(w_o, np.float32).reshape(D, D)),
        "w_gate": np.ascontiguousarray(np.asarray(w_gate, np.float32).reshape(D, DFF)),
        "w_up": np.ascontiguousarray(np.asarray(w_up, np.float32).reshape(D, DFF)),
        "w_down": np.ascontiguousarray(np.asarray(w_down, np.float32).reshape(DFF, D)),
        "w_pg": np.ascontiguousarray(np.asarray(w_ple_gate, np.float32).reshape(D, D)),
        "w_pp": np.ascontiguousarray(np.asarray(w_ple_proj, np.float32).reshape(256, D)),
        "ident": ident, "rm128": rm128, "rm64": rm64, "invf": invf, "blki": blki,
    }
    in_maps, own_idx = [], []
    for c in range(NCORES):
        idx = np.concatenate([np.arange(512 * (8 * j + c), 512 * (8 * j + c) + 512) for j in range(NSLOT)])
        own_idx.append(idx)
        m = dict(shared)
        m["x_own"] = np.ascontiguousarray(x2[idx])
        m["p_own"] = np.ascontiguousarray(p2[idx])
        m["posb_own"] = np.ascontiguousarray(np.broadcast_to(pos[idx].reshape(1, NQ), (128, NQ)))
        m["qidxb"] = np.ascontiguousarray(np.broadcast_to(idx.astype(np.float32).reshape(1, NQ), (128, NQ)))
        m["qidxc"] = np.ascontiguousarray(idx.astype(np.int32).reshape(NQ // 128, 128).T)
        in_maps.append(m)
    return in_maps, own_idx


_CACHE = {}


def kernel(**inputs):
    S = 16384
    NSLOT = S // (512 * NCORES)
    if "nc" not in _CACHE:
        _CACHE["nc"] = build(S, NSLOT)[0]
    nc = _CACHE["nc"]
    in_maps, own_idx = make_in_maps(S, NSLOT, **inputs)
    res = run_bass_kernel_spmd(nc, in_maps, core_ids=list(range(NCORES)))
    out = np.zeros((S, D), np.float32)
    for c in range(NCORES):
        out[own_idx[c]] = np.asarray(res.results[c]["y"], dtype=np.float32)
    return out.reshape(1, S, D)
```

```python
import os
import numpy as np
import concourse.bass as bass
import concourse.mybir as mybir
from concourse.bass_utils import run_bass_kernel_spmd

F32 = mybir.dt.float32
BF16 = mybir.dt.bfloat16
I32 = mybir.dt.int32
ALU = mybir.AluOpType
AF = mybir.ActivationFunctionType
AX = mybir.AxisListType

D = 2048
DFF = 5632
NFF = DFF // 128
INC = 5184
EPS = 1e-6
PI = float(np.pi)
NCORES = 8

SAME_ENGINE_SYNC = True
DBG = set(os.environ.get('KDBG', '').split(','))
CUT = int(os.environ.get('KCUT', '100000000'))
STQ = 'act' if 'stact' in DBG else 'sp'
NDMASEM = 8
SKEW = 2
ARENA_F32 = 52600


class Buf:
    __slots__ = ("t", "writer", "readers", "name", "psum")

    def __init__(self, t=None, name="", psum=False):
        self.psum = psum
        self.t = t
        self.writer = None
        self.readers = {}
        self.name = name

    def __getitem__(self, idx):
        return self.t[idx]


class Op:
    __slots__ = ("eng", "fn", "deps", "needs_inc", "seq", "dma", "sem", "semval", "prev", "desc")

    def __init__(self, eng, fn):
        self.eng = eng
        self.fn = fn
        self.deps = []
        self.needs_inc = False
        self.seq = 0
        self.dma = False
        self.sem = None
        self.semval = 0
        self.prev = None


class FW:
    ENGS = ("pe", "act", "dve", "pool", "sp")

    def __init__(self, nc):
        self.nc = nc
        self.ops = {e: [] for e in self.ENGS}
        self.esem = {e: nc.alloc_semaphore("es_" + e) for e in ("pe", "act", "dve", "pool")}
        self.dsem = {q: [nc.alloc_semaphore("ds_%s%d" % (q, i)) for i in range(NDMASEM)]
                     for q in ("sp", "pool", "act")}
        self.dcnt = {q: [0] * NDMASEM for q in self.dsem}
        self.dlast = {q: [None] * NDMASEM for q in self.dsem}
        self.drr = {q: 0 for q in self.dsem}
        self.nbuf = 0
        self.base_deps = []
        self.arena = nc.alloc_sbuf_tensor("arena", [128, ARENA_F32], F32)
        self.aoff = 0
        self.banks = [Buf(nc.alloc_psum_tensor("bank%d" % i, [128, 512], F32), "bank%d" % i, psum=True)
                      for i in range(8)]

    def sb(self, shape, dt, name=None):
        self.nbuf += 1
        n = int(np.prod(shape[1:]))
        esz = 4 if dt in (F32, I32) else 2
        nf32 = (n * esz + 3) // 4
        nf32 = (nf32 + 7) // 8 * 8
        assert self.aoff + nf32 <= ARENA_F32, "SBUF arena overflow %d" % (self.aoff + nf32)
        ap = self.arena[:, self.aoff:self.aoff + nf32]
        self.aoff += nf32
        if dt != F32:
            ap = ap.bitcast(dt)
        ap = ap[:, 0:n]
        if len(shape) == 3:
            ap = ap.rearrange("p (a b) -> p a b", a=shape[1])
        elif len(shape) == 4:
            ap = ap.rearrange("p (a b c) -> p a b c", a=shape[1], b=shape[2])
        return Buf(ap, name or "sb%d" % self.nbuf)

    def dram(self, shape, dt, name, kind="Internal"):
        return Buf(self.nc.dram_tensor(name, list(shape), dt, kind=kind).ap(), name)

    def mark(self):
        return self.aoff

    def phase(self, mark):
        self.aoff = mark
        deps = []
        for e in ("pe", "act", "dve", "pool"):
            for o in reversed(self.ops[e]):
                if not o.dma:
                    o.needs_inc = True
                    deps.append(o)
                    break
        for q in self.dsem:
            for o in self.dlast[q]:
                if o is not None:
                    deps.append(o)
        self.base_deps = deps

    def _track(self, op, reads, writes):
        deps = list(self.base_deps)
        for b in reads:
            if b.writer is not None:
                deps.append(b.writer)
            if b.psum:
                deps.extend(r for r in b.readers.values() if r.eng != op.eng)
        for b in writes:
            if b.writer is not None:
                deps.append(b.writer)
            deps.extend(b.readers.values())
        for b in writes:
            b.writer = op
            b.readers = {}
        for b in reads:
            key = op.eng if not op.dma else (op.eng, id(op.sem))
            b.readers[key] = op
        seen = set()
        for d in deps:
            if d is op or id(d) in seen:
                continue
            seen.add(id(d))
            if not d.dma:
                if d.eng == op.eng and not op.dma:
                    if d.eng == "pe" or not SAME_ENGINE_SYNC:
                        continue
                d.needs_inc = True
            op.deps.append(d)

    def op(self, eng, meth, reads=(), writes=(), *args, **kw):
        self.nrec = getattr(self, "nrec", 0) + 1
        if self.nrec > CUT:
            return None
        o = Op(eng, lambda e: getattr(e, meth)(*args, **kw))
        o.desc = (self.nrec, eng, meth)
        self._track(o, reads, writes)
        self.ops[eng].append(o)
        return o

    def dma(self, q, out_ap, in_ap, reads=(), writes=()):
        self.nrec = getattr(self, "nrec", 0) + 1
        if self.nrec > CUT:
            return None
        o = Op(q, lambda e: e.dma_start(out=out_ap, in_=in_ap))
        o.desc = (self.nrec, q, "dma", str(out_ap)[:80])
        o.dma = True
        k = self.drr[q]
        self.drr[q] = (k + 1) % NDMASEM
        o.sem = self.dsem[q][k]
        self.dcnt[q][k] += 1
        o.semval = 16 * self.dcnt[q][k]
        o.prev = self.dlast[q][k]
        self.dlast[q][k] = o
        self._track(o, reads, writes)
        self.ops[q].append(o)
        return o

    def emit(self, final_waits=()):
        nc = self.nc
        for e in ("pe", "act", "dve", "pool"):
            n = 0
            for o in self.ops[e]:
                if o.dma:
                    continue
                if o.needs_inc:
                    n += 1
                    o.seq = n
        stats = {}

        def run(ename, eng):
            waited = {}
            nw = 0

            def wait(sem, val):
                nonlocal nw
                if waited.get(id(sem), 0) >= val:
                    return
                waited[id(sem)] = val
                eng.wait_ge(sem, val)
                nw += 1

            for o in self.ops[ename]:
                for d in o.deps:
                    if d.dma:
                        wait(d.sem, d.semval)
                    else:
                        wait(self.esem[d.eng], d.seq)
                if o.dma:
                    if o.prev is not None:
                        wait(o.sem, o.prev.semval)
                    ins = o.fn(eng)
                    ins.then_inc(o.sem, 16)
                else:
                    ins = o.fn(eng)
                    if o.needs_inc:
                        ins.then_inc(self.esem[ename], 1)
            if ename == "sp":
                for o in final_waits:
                    if o is not None:
                        wait(o.sem, o.semval)
            stats[ename] = (len(self.ops[ename]), nw)

        with nc.Block() as block:
            @block.tensor
            def _(eng):
                run("pe", eng)

            @block.scalar
            def _(eng):
                run("act", eng)

            @block.vector
            def _(eng):
                run("dve", eng)

            @block.gpsimd
            def _(eng):
                run("pool", eng)

            @block.sync
            def _(eng):
                run("sp", eng)
        return stats


class Ring:
    def __init__(self, items):
        self.items = list(items)
        self.i = 0

    def next(self):
        b = self.items[self.i % len(self.items)]
        self.i += 1
        return b


def build(S, NSLOT, stop_after=99):
    NT = S // 512
    NQ = 512 * NSLOT
    NKT = S // 128
    NBLK = S // 256
    NQC = NQ // 128
    assert NBLK >= 8

    nc = bass.Bass("TRN2", target_bir_lowering=False)
    fw = FW(nc)
    B = fw.banks

    def ext(name, shape, dt=F32):
        return fw.dram(shape, dt, name, kind="ExternalInput")

    x_all = ext("x_all", [S, D]); x_own = ext("x_own", [NQ, D]); p_own = ext("p_own", [NQ, 256])
    posb_all = ext("posb_all", [128, S], I32); posb_own = ext("posb_own", [128, NQ], I32)
    qidxb_d = ext("qidxb", [128, NQ]); qidxc_d = ext("qidxc", [128, NQC], I32)
    kidxc_d = ext("kidxc", [128, NKT])
    g_attn_d = ext("g_attn_b", [128, D]); g_ffn_d = ext("g_ffn_b", [128, D])
    g_ple_d = ext("g_ple_b", [128, D]); g_fin_d = ext("g_fin_b", [128, D])
    g_kv_d = ext("g_kv_b", [128, 512])
    w_in = ext("w_in", [D, INC]); w_ukv = ext("w_ukv", [512, 2048]); w_o = ext("w_o", [D, D])
    w_gate = ext("w_gate", [D, DFF]); w_up = ext("w_up", [D, DFF]); w_down = ext("w_down", [DFF, D])
    w_pg = ext("w_pg", [D, D]); w_pp = ext("w_pp", [256, D])
    ident_d = ext("ident", [128, 128]); rm128_d = ext("rm128", [128, 128]); rm64_d = ext("rm64", [128, 128])
    inv_d = ext("invf", [128, 2]); blki_d = ext("blki", [128, NBLK])
    y = fw.dram([NQ, D], F32, "y", kind="ExternalOutput")

    def scratch(name, shape, n):
        t = fw.dram(shape, BF16, name)
        return t, [Buf(None, "%s_%d" % (name, i)) for i in range(n)]

    KTn, KTn_b = scratch("KTn", [8, 128, S], NT)
    KTp, KTp_b = scratch("KTp", [64, S], NT)
    KTm, KTm_b = scratch("KTm", [8, 128, S], NT)
    Vn, Vn_b = scratch("Vn", [S, 1024], NT)
    Vm, Vm_b = scratch("Vm", [S, 1024], NT)
    QTn, QTn_b = scratch("QTn", [8, 128, NQ], NSLOT)
    QTp, QTp_b = scratch("QTp", [8, 64, NQ], NSLOT)
    QTm, QTm_b = scratch("QTm", [8, 128, NQ], NSLOT)
    OT, OT_b = scratch("OT", [16, 128, NQ], NSLOT)
    Wo_b = fw.dram([D, D], BF16, "Wo_b")
    Wpg_b = fw.dram([D, D], BF16, "Wpg_b")
    Wpp_b = fw.dram([256, D], BF16, "Wpp_b")
    Wd_b = fw.dram([DFF, D], BF16, "Wd_b")
    Wg_b = fw.dram([NFF, 128, 16 * 128], BF16, "Wg_b")
    Wu_b = fw.dram([NFF, 128, 16 * 128], BF16, "Wu_b")

    identb = fw.sb([128, 128], BF16, "identb")
    rm128b = fw.sb([128, 128], BF16, "rm128b")
    rm64b = fw.sb([128, 128], BF16, "rm64b")
    invf = fw.sb([128, 2], F32, "invf")
    mpi = fw.sb([128, 1], F32, "mpi")
    epst = fw.sb([128, 1], F32, "epst")
    KM = fw.sb([128, 8, NBLK], F32, "KM")
    fw.dma("pool", identb[:, :], ident_d[:, :], reads=[ident_d], writes=[identb])
    fw.dma("pool", rm128b[:, :], rm128_d[:, :], reads=[rm128_d], writes=[rm128b])
    fw.dma("pool", rm64b[:, :], rm64_d[:, :], reads=[rm64_d], writes=[rm64b])
    fw.dma("sp", invf[:, :], inv_d[:, :], reads=[inv_d], writes=[invf])
    fw.op("pool", "memset", [], [mpi], mpi[:, :], -PI)
    fw.op("pool", "memset", [], [epst], epst[:, :], EPS)
    PMARK = fw.mark()

    def rstd_from_ss(ss, rstd, n):
        fw.op("act", "activation", [ss, epst], [rstd], out=rstd[:, :], in_=ss[:, :], func=AF.Sqrt,
                                            scale=1.0 / n, bias=epst[:, 0:1])
        fw.op("dve", "reciprocal", [rstd], [rstd], out=rstd[:, :], in_=rstd[:, :])

    def norm_rows(src_ap, src_bufs, gain, dst, ss, rstd, n):
        fw.op("act", "activation", src_bufs, [dst, ss], out=dst[:, 0:n], in_=src_ap, func=AF.Square,
                                            accum_out=ss[:, :])
        rstd_from_ss(ss, rstd, n)
        fw.op("dve", "scalar_tensor_tensor", list(src_bufs) + [rstd, gain], [dst], out=dst[:, 0:n], in0=src_ap, scalar=rstd[:, 0:1],
                                                      in1=gain[:, 0:n], op0=ALU.mult, op1=ALU.mult)

    def transpose_into(src, nchunk, dstT, col0, tpbanks, evac_engs=("act", "dve")):
        for g0 in range(0, nchunk, 8):
            gn = min(8, nchunk - g0)
            bank = tpbanks.next()
            tv = bank.t[:, :].bitcast(BF16).rearrange("p (a b) -> p a b", a=8)
            for k in range(gn):
                fw.op("pe", "transpose", [src, identb], [bank],
                    out=tv[:, k, :], in_=src[:, (g0 + k) * 128:(g0 + k + 1) * 128], identity=identb[:, :])
            eng = evac_engs[(g0 // 8) % len(evac_engs)]
            if eng == "act":
                fw.op("act", "copy", [bank], [dstT],
                    out=dstT[:, g0:g0 + gn, col0:col0 + 128], in_=tv[:, 0:gn, :])
            else:
                fw.op("dve", "tensor_copy", [bank], [dstT],
                    out=dstT[:, g0:g0 + gn, col0:col0 + 128], in_=tv[:, 0:gn, :])

    def precast():
        q = []
        for r in range(4):
            q.append((Wo_b[r * 512:(r + 1) * 512, :], w_o[r * 512:(r + 1) * 512, :], w_o, Wo_b))
        for r in range(4):
            q.append((Wpg_b[r * 512:(r + 1) * 512, :], w_pg[r * 512:(r + 1) * 512, :], w_pg, Wpg_b))
        q.append((Wpp_b[:, :], w_pp[:, :], w_pp, Wpp_b))
        for r in range(11):
            q.append((Wd_b[r * 512:(r + 1) * 512, :], w_down[r * 512:(r + 1) * 512, :], w_down, Wd_b))
        for (src, dst) in ((w_gate, Wg_b), (w_up, Wu_b)):
            sv = src.t.rearrange("(k p) c -> p k c", p=128)
            for f in range(NFF):
                q.append((dst.t[f].rearrange("p (k c) -> p k c", k=16), sv[:, :, f * 128:(f + 1) * 128], src, dst))
        return q

    def precast_issue(q, n):
        for _ in range(min(n, len(q))):
            o, i, sb_, db_ = q.pop(0)
            fw.dma("pool", o, i, reads=[sb_], writes=[db_])

    def proj_phase(mode):
        fw.phase(PMARK)
        if mode == "kv":
            xd, posd, ntile = x_all, posb_all, NT
            ranges = [(1536, 2112), (3136, 5184)]
        else:
            xd, posd, ntile = x_own, posb_own, NSLOT
            ranges = [(0, 1536), (2112, 3136)]
        if mode == "kv":
            parts = [(0, 512, 1536), (512, 576, 2048), (576, 1600, 3136), (1600, 2624, 4160)]
        else:
            parts = [(0, 768, 0), (768, 1536, 768), (1536, 2560, 2112)]
        ncols = parts[-1][1]
        W = fw.sb([128, 16, ncols], BF16, "W_" + mode)
        Wp = [Buf(None, "Wp%d" % i) for i in range(len(parts))]

        def wbuf(coff):
            for i, (lo, hi, _) in enumerate(parts):
                if lo <= coff < hi:
                    return Wp[i]
            raise AssertionError(coff)

        wv = w_in.t.rearrange("(k p) c -> p k c", p=128)

        def load_part(i):
            lo, hi, src = parts[i]
            for k in range(16):
                fw.dma("pool", W[:, k, lo:hi], wv[:, k, src:src + (hi - lo)], reads=[w_in], writes=[Wp[i]])

        if mode == "kv":
            WK = fw.sb([128, 4, 1024], BF16, "WukvK")
            WV = fw.sb([128, 4, 1024], BF16, "WukvV")
            uv = w_ukv.t.rearrange("(k p) (h t c) -> p k h t c", p=128, h=8, t=2)
            gkv = fw.sb([128, 512], F32, "gkv")
            fw.dma("sp", gkv[:, :], g_kv_d[:, :], reads=[g_kv_d], writes=[gkv])

        def load_weights():
            load_part(0)
            if mode == "kv":
                for k in range(4):
                    fw.dma("pool", WK[:, k, :].rearrange("p (h c) -> p h c", h=8), uv[:, k, :, 0, :], reads=[w_ukv], writes=[WK])
                    fw.dma("pool", WV[:, k, :].rearrange("p (h c) -> p h c", h=8), uv[:, k, :, 1, :], reads=[w_ukv], writes=[WV])
            for i in range(1, len(parts)):
                load_part(i)

        pre_q = precast() if mode == "kv" else []
        gat = fw.sb([128, D], F32, "gat")
        fw.dma("sp", gat[:, :], g_attn_d[:, :], reads=[g_attn_d], writes=[gat])

        xs_r = Ring([fw.sb([128, D], F32) for _ in range(2)])
        as_r = Ring([fw.sb([128, D], BF16) for _ in range(2)])
        aT_r = Ring([fw.sb([128, 16, 512], BF16, "aT%d" % i) for i in range(2)])
        ss_r = Ring([fw.sb([128, 1], F32) for _ in range(4)])
        rs_r = Ring([fw.sb([128, 1], F32) for _ in range(4)])
        posi = fw.sb([128, 512], I32, "posi")
        posf = fw.sb([128, 512], F32, "posf")
        tt = fw.sb([128, 512], F32, "tt"); tki = fw.sb([128, 512], I32, "tki"); tkf = fw.sb([128, 512], F32, "tkf")
        tabs = {k: fw.sb([128, 512], F32, "tab" + k) for k in ("s128", "c128", "s64", "c64")}
        t1_r = Ring([fw.sb([128, 512], F32) for _ in range(2)])
        t2_r = Ring([fw.sb([128, 512], F32) for _ in range(2)])
        xb_r = Ring([fw.sb([128, 512], BF16) for _ in range(2)])
        ob_r = Ring([fw.sb([128, 512], BF16) for _ in range(4)])
        if mode == "kv":
            ckv_r = Ring([fw.sb([128, 512], BF16) for _ in range(2)])
            kvnT = fw.sb([128, 4, 512], BF16, "kvnT")
        tp_r = Ring([B[0], B[1]])
        acc_r = Ring([B[2], B[3], B[4], B[7]])
        rp_r = Ring([B[5], B[6]])

        pending_tails = []
        new_tails = []

        def rope_out(acc, M, kind, dst_ap, dst_buf, kmean=None):
            rm = rm128b if kind == "128" else rm64b
            cs, sn = tabs["c" + kind], tabs["s" + kind]
            xb = xb_r.next(); t1 = t1_r.next()
            fw.op("act", "copy", [acc], [xb], out=xb[0:M, :], in_=acc[0:M, :])
            fw.op("dve", "tensor_tensor", [acc, cs], [t1], out=t1[0:M, :], in0=acc[0:M, :], in1=cs[0:M, :], op=ALU.mult)

            def tail():
                rp = rp_r.next(); t2 = t2_r.next(); ob = ob_r.next()
                fw.op("pe", "matmul", [rm, xb], [rp], rp[0:M, :], lhsT=rm[0:M, 0:M], rhs=xb[0:M, :], start=True, stop=True)
                fw.op("dve", "tensor_tensor", [rp, sn], [t2], out=t2[0:M, :], in0=rp[0:M, :], in1=sn[0:M, :], op=ALU.mult)
                fw.op("dve", "tensor_tensor", [t1, t2], [t1], out=t1[0:M, :], in0=t1[0:M, :], in1=t2[0:M, :], op=ALU.add)
                fw.op("act", "copy", [t1], [ob], out=ob[0:M, :], in_=t1[0:M, :])
                if kmean is not None:
                    fw.op("dve", "tensor_reduce", [t1], [KM], out=kmean, in_=t1[:, :].rearrange("p (b j) -> p b j", b=2),
                          axis=AX.X, op=ALU.add)
                fw.dma(STQ, dst_ap, ob[0:M, :], reads=[ob], writes=[dst_buf])

            new_tails.append(tail)

        pcount = [0]

        def plain_out(acc, M, dst_ap, dst_buf):
            ob = ob_r.next()
            pcount[0] += 1
            if pcount[0] % 2:
                fw.op("act", "copy", [acc], [ob], out=ob[0:M, :], in_=acc[0:M, :])
            else:
                fw.op("dve", "tensor_copy", [acc], [ob], out=ob[0:M, :], in_=acc[0:M, :])
            fw.dma(STQ, dst_ap, ob[0:M, :], reads=[ob], writes=[dst_buf])

        def fm_block(coff, M):
            acc = acc_r.next()
            for k in range(16):
                fw.op("pe", "matmul", [wbuf(coff), aT], [acc], acc[0:M, :], lhsT=W[:, k, coff:coff + M], rhs=aT[:, k, :],
                                                    start=(k == 0), stop=(k == 15))
            return acc

        def tm_block(coff, sub):
            acc = acc_r.next()
            for k in range(16):
                fw.op("pe", "matmul", [wbuf(coff), aT], [acc], acc[:, :], lhsT=aT[:, k, sub * 128:(sub + 1) * 128],
                                                    rhs=W[:, k, coff:coff + 512], start=(k == 0), stop=(k == 15))
            return acc

        TABS = [("128", 0, "s", 0.5), ("128", 0, "c", 0.75), ("64", 1, "s", 0.5), ("64", 1, "c", 0.75)]

        def table_pos(st):
            tok0 = st * 512
            fw.dma("sp", posi[:, :], posd[:, tok0:tok0 + 512], reads=[posd], writes=[posi])
            fw.op("dve", "tensor_copy", [posi], [posf], out=posf[:, :], in_=posi[:, :])

        def table_pool(k):
            kind, col, fn, shift = TABS[k]
            fw.op("dve", "tensor_scalar", [posf, invf], [tt],
                  out=tt[:, :], in0=posf[:, :], scalar1=invf[:, col:col + 1], scalar2=shift,
                  op0=ALU.mult, op1=ALU.add)
            fw.op("dve", "tensor_copy", [tt], [tki], out=tki[:, :], in_=tt[:, :])
            fw.op("dve", "tensor_copy", [tki], [tkf], out=tkf[:, :], in_=tki[:, :])
            fw.op("dve", "tensor_tensor", [tt, tkf], [tt], out=tt[:, :], in0=tt[:, :], in1=tkf[:, :], op=ALU.subtract)
            fw.op("dve", "tensor_single_scalar", [tt], [tkf], out=tkf[:, :], in_=tt[:, :], scalar=0.0, op=ALU.is_lt)
            fw.op("dve", "tensor_tensor", [tt, tkf], [tkf], out=tkf[:, :], in0=tkf[:, :], in1=tt[:, :], op=ALU.add)

        def table_act(k):
            kind, col, fn, shift = TABS[k]
            tab = tabs[fn + kind]
            fw.op("act", "activation", [tkf, mpi], [tab], out=tab[:, :], in_=tkf[:, :], func=AF.Sin,
                  scale=2 * PI, bias=mpi[:, 0:1])

        def tables(st):
            table_pos(st)
            for k in range(4):
                table_pool(k)
                table_act(k)

        prep_as = {}

        def prep_norm(st, sub):
            tok0 = st * 512
            xs = xs_r.next(); a_s = as_r.next(); ss = ss_r.next(); rs = rs_r.next()
            fw.dma("sp", xs[:, :], xd[tok0 + sub * 128:tok0 + (sub + 1) * 128, :], reads=[xd], writes=[xs])
            norm_rows(xs[:, :], [xs], gat, a_s, ss, rs, D)
            prep_as[(st, sub)] = a_s

        def prep_tr(st, sub, aT_dst):
            transpose_into(prep_as.pop((st, sub)), 16, aT_dst, sub * 128, tp_r)

        def prep_sub(st, sub, aT_dst):
            prep_norm(st, sub)
            prep_tr(st, sub, aT_dst)

        def blocks_for(st):
            tok0 = st * 512
            bl = []
            if mode == "kv":
                def ckv_blk(sub):
                    acc = tm_block(0, sub)
                    ck = ckv_r.next(); ss = ss_r.next(); rs = rs_r.next()
                    norm_rows(acc[:, :], [acc], gkv, ck, ss, rs, 512)
                    new_tails.append(lambda: transpose_into(ck, 4, kvnT, sub * 128, tp_r))

                def vmb_blk(sub, cb):
                    acc = tm_block(1600 + cb * 512, sub)
                    plain_out(acc, 128, Vm[tok0 + sub * 128:tok0 + (sub + 1) * 128, cb * 512:(cb + 1) * 512], Vm_b[st])

                def knope_blk(h):
                    acc = acc_r.next()
                    for k in range(4):
                        fw.op("pe", "matmul", [WK, kvnT], [acc],
                              acc[:, :], lhsT=WK[:, k, h * 128:(h + 1) * 128], rhs=kvnT[:, k, :],
                              start=(k == 0), stop=(k == 3))
                    plain_out(acc, 128, KTn[h, :, tok0:tok0 + 512], KTn_b[st])

                def vmla_blk(sub, cb):
                    acc = acc_r.next()
                    for k in range(4):
                        fw.op("pe", "matmul", [WV, kvnT], [acc],
                              acc[:, :], lhsT=kvnT[:, k, sub * 128:(sub + 1) * 128],
                              rhs=WV[:, k, cb * 512:(cb + 1) * 512], start=(k == 0), stop=(k == 3))
                    plain_out(acc, 128, Vn[tok0 + sub * 128:tok0 + (sub + 1) * 128, cb * 512:(cb + 1) * 512], Vn_b[st])

                def kpe_blk():
                    acc = fm_block(512, 64)
                    rope_out(acc, 64, "64", KTp[0:64, tok0:tok0 + 512], KTp_b[st])

                def kmb_blk(h):
                    acc = fm_block(576 + h * 128, 128)
                    rope_out(acc, 128, "128", KTm[h, :, tok0:tok0 + 512], KTm_b[st], kmean=KM[:, h, 2 * st:2 * st + 2])

                for sub in range(4):
                    bl.append(lambda sub=sub: ckv_blk(sub))
                for sub in range(2):
                    for cb in range(2):
                        bl.append(lambda sub=sub, cb=cb: vmb_blk(sub, cb))
                for h in range(8):
                    bl.append(lambda h=h: knope_blk(h))
                for sub in range(2, 4):
                    for cb in range(2):
                        bl.append(lambda sub=sub, cb=cb: vmb_blk(sub, cb))
                for sub in range(4):
                    for cb in range(2):
                        bl.append(lambda sub=sub, cb=cb: vmla_blk(sub, cb))
                bl.append(kpe_blk)
                for h in range(8):
                    bl.append(lambda h=h: kmb_blk(h))
            else:
                def qn_blk(h):
                    acc = fm_block(h * 192, 128)
                    plain_out(acc, 128, QTn[h, :, tok0:tok0 + 512], QTn_b[st])

                def qp_blk(h):
                    acc = fm_block(h * 192 + 128, 64)
                    rope_out(acc, 64, "64", QTp[h, :, tok0:tok0 + 512], QTp_b[st])

                def qm_blk(h):
                    acc = fm_block(1536 + h * 128, 128)
                    rope_out(acc, 128, "128", QTm[h, :, tok0:tok0 + 512], QTm_b[st])

                for h in range(8):
                    bl.append(lambda h=h: qn_blk(h))
                    bl.append(lambda h=h: qp_blk(h))
                for h in range(8):
                    bl.append(lambda h=h: qm_blk(h))
            return bl

        nt_run = ntile if 'onetile' not in DBG else 1
        cur = {"aT": aT_r.next()}
        for sub in range(4):
            prep_sub(0, sub, cur["aT"])
        load_weights()
        for st in range(nt_run):
            aT = cur["aT"]
            if st > 0 and 'noprecast' not in DBG:
                precast_issue(pre_q, 4)
            bl = blocks_for(st)
            nxt = None
            sched = {}

            def at(pos, fn):
                sched.setdefault(max(0, min(pos, len(bl) - 1)), []).append(fn)

            if mode == "kv":
                table_pos(st)
                for k in range(4):
                    at(1 + 3 * k, lambda k=k: table_pool(k))
                    at(4 + 3 * k, lambda k=k: table_act(k))
            else:
                tables(st)
            if st + 1 < nt_run:
                nxt = aT_r.next()
                step = len(bl) // 5
                for sub in range(4):
                    at(step * (sub + 1) - 5, lambda sub=sub: prep_norm(st + 1, sub))
                    at(step * (sub + 1), lambda sub=sub: prep_tr(st + 1, sub, nxt))
            for bi, blk in enumerate(bl):
                blk()
                while pending_tails:
                    pending_tails.pop(0)()
                pending_tails.extend(new_tails)
                del new_tails[:]
                for fn in sched.get(bi, []):
                    fn()
            while pending_tails:
                pending_tails.pop(0)()
            if nxt is not None:
                cur["aT"] = nxt
        if 'noprecast' not in DBG:
            precast_issue(pre_q, len(pre_q))

    if stop_after >= 1:
        proj_phase("kv")
    if stop_after >= 2:
        proj_phase("q")

    def attention_phase():
        fw.phase(PMARK)
        qidxb = fw.sb([128, NQ], F32, "qidxb")
        kidxc = fw.sb([128, NKT], F32, "kidxc")
        qidxi = fw.sb([128, NQC], I32, "qidxi")
        qblkf = fw.sb([128, NQC], F32, "qblkf")
        blki = fw.sb([128, NBLK], F32, "blki")
        pastm = fw.sb([128, NQC, NBLK], F32, "pastm")
        ownm = fw.sb([128, NQC, NBLK], F32, "ownm")
        pbias = fw.sb([128, NQC, NBLK], F32, "pbias")
        KMb = fw.sb([128, 8, NBLK], BF16, "KMb")
        fw.dma("sp", qidxb[:, :], qidxb_d[:, :], reads=[qidxb_d], writes=[qidxb])
        fw.dma("sp", kidxc[:, :], kidxc_d[:, :], reads=[kidxc_d], writes=[kidxc])
        fw.dma("sp", qidxi[:, :], qidxc_d[:, :], reads=[qidxc_d], writes=[qidxi])
        fw.dma("sp", blki[:, :], blki_d[:, :], reads=[blki_d], writes=[blki])
        fw.op("dve", "tensor_single_scalar", [qidxi], [qidxi], out=qidxi[:, :], in_=qidxi[:, :], scalar=8, op=ALU.arith_shift_right)
        fw.op("dve", "tensor_copy", [qidxi], [qblkf], out=qblkf[:, :], in_=qidxi[:, :])
        for ci in range(NQC):
            fw.op("dve", "tensor_scalar", [blki, qblkf], [pastm], out=pastm[:, ci, :], in0=blki[:, :], scalar1=qblkf[:, ci:ci + 1],
                                                          scalar2=None, op0=ALU.is_lt)
            fw.op("dve", "tensor_scalar", [blki, qblkf], [ownm], out=ownm[:, ci, :], in0=blki[:, :], scalar1=qblkf[:, ci:ci + 1],
                                                          scalar2=None, op0=ALU.is_equal)
        fw.op("dve", "tensor_scalar", [pastm], [pbias], out=pbias[:, :, :], in0=pastm[:, :, :], scalar1=-1.0, scalar2=1e30,
                                               op0=ALU.add, op1=ALU.mult)
        fw.op("act", "activation", [KM], [KMb], out=KMb[:, :, :], in_=KM[:, :, :], func=AF.Copy, scale=1.0 / 256)

        NCH = 2
        Kc = [fw.sb([128, 4096], BF16, "Kc%d" % i) for i in range(NCH)]
        Kp = [fw.sb([128, 4096], BF16, "Kp%d" % i) for i in range(NCH)]
        Vc = [fw.sb([128, 32, 129], BF16, "Vc%d" % i) for i in range(NCH)]
        for i in range(NCH):
            fw.op("pool", "memset", [], [Vc[i]], Vc[i][:, :, 128:129], 1.0)
        Qn = [fw.sb([128, 512], BF16) for _ in range(2)]
        Qp = [fw.sb([128, 512], BF16) for _ in range(2)]
        pt_r = Ring([fw.sb([128, 512], BF16) for _ in range(5)])
        ptm_r = Ring([fw.sb([128, 512], BF16) for _ in range(4)])
        sc_r = Ring([B[0], B[1], B[2]])
        osets = [(B[3], B[4]), (B[5], B[6])]
        gtp = B[7]
        accs = [fw.sb([128, 129], F32, "accs%d" % c) for c in range(4)]
        on_r = Ring([fw.sb([128, 128], BF16) for _ in range(2)])
        rinv_r = Ring([fw.sb([128, 1], F32) for _ in range(2)])
        ott_r = Ring([fw.sb([128, 512], BF16) for _ in range(2)])
        gm = fw.sb([128, 4, NBLK], F32, "gm")
        mx8 = fw.sb([128, 4, 8], F32, "mx8")
        sel = fw.sb([128, 4, NBLK], F32, "sel")

        items = []
        for j in range(NSLOT):
            for h in range(8):
                items.append((j, "mla", h))
            for h in range(8):
                items.append((j, "mb", h))
        loads = []
        for ii, (j, kind, h) in enumerate(items):
            nchunk = min(j + 1, (S + 4095) // 4096)
            for ch in range(nchunk):
                loads.append((ii, ch))
        state = {"li": 0}

        def issue_load(li):
            ii, ch = loads[li]
            j, kind, h = items[ii]
            slot = li % NCH
            k0 = ch * 4096
            nk = min(4096, S - k0)
            sts = list(range(k0 // 512, (k0 + nk) // 512))
            if kind == "mla":
                fw.dma("sp", Kc[slot][:, 0:nk], KTn[h, :, k0:k0 + nk], reads=[KTn_b[s] for s in sts], writes=[Kc[slot]])
                fw.dma("sp", Kp[slot][0:64, 0:nk], KTp[0:64, k0:k0 + nk], reads=[KTp_b[s] for s in sts], writes=[Kp[slot]])
                vsrc, vb = Vn, Vn_b
            else:
                fw.dma("sp", Kc[slot][:, 0:nk], KTm[h, :, k0:k0 + nk], reads=[KTm_b[s] for s in sts], writes=[Kc[slot]])
                vsrc, vb = Vm, Vm_b
            nt = nk // 128
            half = max(1, nt // 2)
            for t0 in range(0, nt, half):
                fw.dma("sp", Vc[slot][:, t0:t0 + half, 0:128],
                       vsrc[k0 + t0 * 128:k0 + (t0 + half) * 128, h * 128:(h + 1) * 128].rearrange("(t p) c -> p t c", p=128),
                       reads=[vb[s] for s in sts], writes=[Vc[slot]])
            return slot

        def issue_q(ii):
            j, kind, h = items[ii]
            qs = ii % 2
            if kind == "mla":
                fw.dma("sp", Qn[qs][:, :], QTn[h, :, j * 512:(j + 1) * 512], reads=[QTn_b[j]], writes=[Qn[qs]])
                fw.dma("sp", Qp[qs][0:64, :], QTp[h, :, j * 512:(j + 1) * 512], reads=[QTp_b[j]], writes=[Qp[qs]])
            else:
                fw.dma("sp", Qn[qs][:, :], QTm[h, :, j * 512:(j + 1) * 512], reads=[QTm_b[j]], writes=[Qn[qs]])

        def finish(get_o, h16, j, osrc_bufs):
            ott = ott_r.next()
            tv = gtp.t[:, :].bitcast(BF16)
            for c in range(4):
                rinv = rinv_r.next(); on = on_r.next()
                o_ap = get_o(c)
                fw.op("dve", "reciprocal", osrc_bufs, [rinv], out=rinv[:, :], in_=o_ap[:, 128:129])
                fw.op("dve", "tensor_scalar", list(osrc_bufs) + [rinv], [on],
                    out=on[:, :], in0=o_ap[:, 0:128], scalar1=rinv[:, 0:1], scalar2=None, op0=ALU.mult)
                fw.op("pe", "transpose", [on, identb], [gtp], out=tv[:, c * 128:(c + 1) * 128], in_=on[:, :], identity=identb[:, :])
            fw.op("act", "copy", [gtp], [ott], out=ott[:, :], in_=tv[:, 0:512])
            fw.dma(STQ, OT[h16, :, j * 512:(j + 1) * 512], ott[:, :], reads=[ott], writes=[OT_b[j]])

        issue_q(0)
        slot_of = {0: issue_load(0)}
        li_next = 1
        li = 0
        for ii, (j, kind, h) in enumerate(items):
            if ii + 1 < len(items):
                issue_q(ii + 1)
            qs = ii % 2
            nchunk = min(j + 1, (S + 4095) // 4096)
            nkt_total = min(32 * j + 32, NKT)
            scale = (192 ** -0.5) if kind == "mla" else (128 ** -0.5)
            if kind == "mb":
                for c in range(4):
                    fw.op("pe", "matmul", [Qn[qs], KMb], [gtp], gtp[:, c * NBLK:(c + 1) * NBLK], lhsT=Qn[qs][:, c * 128:(c + 1) * 128],
                                                        rhs=KMb[:, h, :], start=True, stop=True)
                fw.op("dve", "tensor_tensor", [gtp, pbias], [gm], out=gm[:, :, :], in0=gtp[:, 0:4 * NBLK].rearrange("p (c n) -> p c n", c=4),
                                                       in1=pbias[:, 4 * j:4 * j + 4, :], op=ALU.add)
                for c in range(4):
                    fw.op("dve", "max", [gm], [mx8], out=mx8[:, c, :], in_=gm[:, c, :])
                for c in range(4):
                    fw.op("dve", "tensor_scalar", [gm, mx8], [sel], out=sel[:, c, :], in0=gm[:, c, :], scalar1=mx8[:, c, 2:3],
                                                                scalar2=None, op0=ALU.is_ge)
                fw.op("dve", "tensor_tensor", [sel, pastm], [sel], out=sel[:, :, :], in0=sel[:, :, :], in1=pastm[:, 4 * j:4 * j + 4, :], op=ALU.mult)
                fw.op("dve", "tensor_tensor", [sel, ownm], [sel], out=sel[:, :, :], in0=sel[:, :, :], in1=ownm[:, 4 * j:4 * j + 4, :], op=ALU.add)
            tiles = []
            for ch in range(nchunk):
                kt0 = ch * 32
                for ktl in range(min(32, nkt_total - kt0)):
                    tiles.append((ch, ktl, kt0 + ktl))
            st = {"slot": None, "oset_i": 0}

            def stage_a(ch, ktl, kt):
                nonlocal li, li_next
                if ktl == 0:
                    st["slot"] = slot_of.pop(li)
                    if li_next < len(loads):
                        slot_of[li_next] = issue_load(li_next)
                        li_next += 1
                    li += 1
                slot = st["slot"]
                diag = (ch == nchunk - 1)
                sc = sc_r.next()
                last_qk = (kind != "mla")
                fw.op("pe", "matmul", [Kc[slot], Qn[qs]], [sc],
                      sc[:, :], lhsT=Kc[slot][:, ktl * 128:(ktl + 1) * 128], rhs=Qn[qs][:, :], start=True, stop=last_qk)
                if kind == "mla":
                    fw.op("pe", "matmul", [Kp[slot], Qp[qs]], [sc],
                          sc[:, :], lhsT=Kp[slot][0:64, ktl * 128:(ktl + 1) * 128], rhs=Qp[qs][0:64, :], start=False, stop=True)
                pt = pt_r.next()
                fw.op("act", "activation", [sc], [pt], out=pt[:, :], in_=sc[:, :], func=AF.Exp, scale=scale)
                if diag:
                    ptm = ptm_r.next()
                    fw.op("dve", "scalar_tensor_tensor", [qidxb, kidxc, pt], [ptm],
                          out=ptm[:, :], in0=qidxb[:, j * 512:(j + 1) * 512], scalar=kidxc[:, kt:kt + 1], in1=pt[:, :],
                          op0=ALU.is_ge, op1=ALU.mult)
                    pt = ptm
                return (pt, slot, ktl, kt)

            def stage_b(pt, slot, ktl, kt):
                if kind == "mla":
                    ob = osets[0]
                    first, last = (kt == 0), (kt == nkt_total - 1)
                else:
                    ob = osets[st["oset_i"] % 2]
                    first, last = (kt % 2 == 0), (kt % 2 == 1)
                for c in range(4):
                    bank = ob[c // 2]
                    c0 = (c % 2) * 129
                    fw.op("pe", "matmul", [pt, Vc[slot]], [bank],
                          bank[:, c0:c0 + 129], lhsT=pt[:, c * 128:(c + 1) * 128], rhs=Vc[slot][:, ktl, :],
                          start=(first and c % 2 == 0), stop=last)
                if kind == "mb" and last:
                    n = kt // 2
                    for c in range(4):
                        bank = ob[c // 2]
                        c0 = (c % 2) * 129
                        if n == 0:
                            fw.op("dve", "tensor_scalar", [bank, sel], [accs[c]],
                                  out=accs[c][:, :], in0=bank[:, c0:c0 + 129], scalar1=sel[:, c, n:n + 1], scalar2=None, op0=ALU.mult)
                        else:
                            fw.op("dve", "scalar_tensor_tensor", [bank, sel, accs[c]], [accs[c]],
                                  out=accs[c][:, :], in0=bank[:, c0:c0 + 129], scalar=sel[:, c, n:n + 1], in1=accs[c][:, :],
                                  op0=ALU.mult, op1=ALU.add)
                    st["oset_i"] += 1

            pend = []
            for t in tiles:
                if t[1] == 0:
                    while pend:
                        stage_b(*pend.pop(0))
                pend.append(stage_a(*t))
                if len(pend) > SKEW:
                    stage_b(*pend.pop(0))
            while pend:
                stage_b(*pend.pop(0))
            if kind == "mla":
                ob = osets[0]
                finish(lambda c: ob[c // 2][:, (c % 2) * 129:(c % 2) * 129 + 129], h, j, [ob[0], ob[1]])
            else:
                finish(lambda c: accs[c][:, :], 8 + h, j, accs)

    if stop_after >= 3:
        attention_phase()

    def tail_phase():
        fw.phase(PMARK)
        hb = fw.sb([128, 4, D], F32, "hb")
        fT = fw.sb([128, 16, 512], BF16, "fT")
        fs_r = Ring([fw.sb([128, D], BF16) for _ in range(2)])
        HF = NFF // 2
        actT = fw.sb([128, HF, 512], BF16, "actT")
        wa_r = Ring([fw.sb([128, 16, 256], BF16) for _ in range(3)])
        wg_r = Ring([fw.sb([128, 16, 128], BF16) for _ in range(3)])
        wu_r = Ring([fw.sb([128, 16, 128], BF16) for _ in range(3)])
        wd_r = Ring([fw.sb([128, 11, 512], BF16) for _ in range(3)])
        OTt = fw.sb([128, 16, 512], BF16, "OTt")
        pT = fw.sb([128, 2, 512], BF16, "pT")
        pin_r = Ring([fw.sb([128, 256], F32) for _ in range(2)])
        pb_r = Ring([fw.sb([128, 256], BF16) for _ in range(2)])
        wpp_r = Ring([fw.sb([128, 2, 256], BF16) for _ in range(2)])
        gn = fw.sb([128, D], F32, "gn")
        sg_r = Ring([fw.sb([128, 512], F32) for _ in range(2)])
        ss_r = Ring([fw.sb([128, 1], F32) for _ in range(4)])
        rs_r = Ring([fw.sb([128, 1], F32) for _ in range(4)])
        tp_r = Ring([B[0], B[1]])
        acc_r = Ring([B[2], B[3], B[4], B[5]])
        acc2_r = Ring([B[6], B[7]])
        outs = []
        Wo_v = Wo_b.t.rearrange("(k p) c -> p k c", p=128)
        Wpg_v = Wpg_b.t.rearrange("(k p) c -> p k c", p=128)
        Wpp_v = Wpp_b.t.rearrange("(k p) c -> p k c", p=128)
        Wd_v = Wd_b.t.rearrange("(k p) c -> p k c", p=128)

        def norm_to_fT(gain_d):
            fw.dma("sp", gn[:, :], gain_d[:, :], reads=[gain_d], writes=[gn])
            for sub in range(4):
                fs = fs_r.next(); ss = ss_r.next(); rs = rs_r.next()
                norm_rows(hb[:, sub, :], [hb], gn, fs, ss, rs, D)
                transpose_into(fs, 16, fT, sub * 128, tp_r)

        for j in range(NSLOT):
            tok0 = j * 512
            for sub in range(4):
                fw.dma("sp", hb[:, sub, :], x_own[tok0 + sub * 128:tok0 + (sub + 1) * 128, :], reads=[x_own], writes=[hb])
            for hh in range(16):
                fw.dma("sp", OTt[:, hh, :], OT[hh, :, tok0:tok0 + 512], reads=[OT_b[j]], writes=[OTt])
            for cb in range(8):
                wa = wa_r.next()
                fw.dma("sp", wa[:, :, :], Wo_v[:, :, cb * 256:(cb + 1) * 256], reads=[Wo_b], writes=[wa])
                for sub in range(4):
                    acc = acc_r.next()
                    for k in range(16):
                        fw.op("pe", "matmul", [OTt, wa], [acc],
                            acc[:, 0:256], lhsT=OTt[:, k, sub * 128:(sub + 1) * 128], rhs=wa[:, k, :], start=(k == 0), stop=(k == 15))
                    fw.op("dve", "tensor_tensor", [hb, acc], [hb],
                        out=hb[:, sub, cb * 256:(cb + 1) * 256], in0=hb[:, sub, cb * 256:(cb + 1) * 256], in1=acc[:, 0:256], op=ALU.add)
            norm_to_fT(g_ffn_d)
            for half in range(2):
                for fl in range(HF):
                    f = half * HF + fl
                    wg = wg_r.next(); wu = wu_r.next()
                    fw.dma("sp", wg[:, :, :], Wg_b.t[f].rearrange("p (k c) -> p k c", k=16), reads=[Wg_b], writes=[wg])
                    fw.dma("sp", wu[:, :, :], Wu_b.t[f].rearrange("p (k c) -> p k c", k=16), reads=[Wu_b], writes=[wu])
                    ga = acc_r.next(); ua = acc_r.next()
                    for k in range(16):
                        fw.op("pe", "matmul", [wg, fT], [ga], ga[:, :], lhsT=wg[:, k, :], rhs=fT[:, k, :],
                                                                          start=(k == 0), stop=(k == 15))
                    for k in range(16):
                        fw.op("pe", "matmul", [wu, fT], [ua], ua[:, :], lhsT=wu[:, k, :], rhs=fT[:, k, :],
                                                                          start=(k == 0), stop=(k == 15))
                    sg = sg_r.next()
                    fw.op("act", "activation", [ga], [sg], out=sg[:, :], in_=ga[:, :], func=AF.Silu)
                    fw.op("dve", "tensor_tensor", [sg, ua], [actT], out=actT[:, fl, :], in0=sg[:, :], in1=ua[:, :], op=ALU.mult)
                for cb in range(4):
                    wds = []
                    for q in range(HF // 11):
                        wd = wd_r.next()
                        r0 = half * HF + q * 11
                        fw.dma("sp", wd[:, :, :], Wd_v[:, r0:r0 + 11, cb * 512:(cb + 1) * 512], reads=[Wd_b], writes=[wd])
                        wds.append(wd)
                    for sub in range(4):
                        acc = acc2_r.next()
                        for fl in range(HF):
                            wd = wds[fl // 11]
                            fw.op("pe", "matmul", [actT, wd], [acc],
                                acc[:, :], lhsT=actT[:, fl, sub * 128:(sub + 1) * 128], rhs=wd[:, fl % 11, :],
                                start=(fl == 0), stop=(fl == HF - 1))
                        fw.op("dve", "tensor_tensor", [hb, acc], [hb],
                            out=hb[:, sub, cb * 512:(cb + 1) * 512], in0=hb[:, sub, cb * 512:(cb + 1) * 512], in1=acc[:, :], op=ALU.add)
            norm_to_fT(g_ple_d)
            for sub in range(4):
                pin = pin_r.next(); pb = pb_r.next()
                fw.dma("sp", pin[:, :], p_own[tok0 + sub * 128:tok0 + (sub + 1) * 128, :], reads=[p_own], writes=[pin])
                fw.op("act", "copy", [pin], [pb], out=pb[:, :], in_=pin[:, :])
                transpose_into(pb, 2, pT, sub * 128, tp_r)
            for cb in range(8):
                wa = wa_r.next(); wpp = wpp_r.next()
                fw.dma("sp", wa[:, :, :], Wpg_v[:, :, cb * 256:(cb + 1) * 256], reads=[Wpg_b], writes=[wa])
                fw.dma("sp", wpp[:, :, :], Wpp_v[:, :, cb * 256:(cb + 1) * 256], reads=[Wpp_b], writes=[wpp])
                for sub in range(4):
                    acc = acc_r.next(); acc2 = acc2_r.next()
                    for k in range(16):
                        fw.op("pe", "matmul", [fT, wa], [acc],
                            acc[:, 0:256], lhsT=fT[:, k, sub * 128:(sub + 1) * 128], rhs=wa[:, k, :], start=(k == 0), stop=(k == 15))
                    for k in range(2):
                        fw.op("pe", "matmul", [pT, wpp], [acc2],
                            acc2[:, 0:256], lhsT=pT[:, k, sub * 128:(sub + 1) * 128], rhs=wpp[:, k, :], start=(k == 0), stop=(k == 1))
                    sg = sg_r.next()
                    fw.op("act", "activation", [acc], [sg], out=sg[:, 0:256], in_=acc[:, 0:256], func=AF.Sigmoid)
                    fw.op("dve", "tensor_tensor", [sg, acc2], [sg], out=sg[:, 0:256], in0=sg[:, 0:256], in1=acc2[:, 0:256], op=ALU.mult)
                    fw.op("pool", "tensor_tensor", [hb, sg], [hb],
                        out=hb[:, sub, cb * 256:(cb + 1) * 256], in0=hb[:, sub, cb * 256:(cb + 1) * 256], in1=sg[:, 0:256], op=ALU.add)
            fw.dma("sp", gn[:, :], g_fin_d[:, :], reads=[g_fin_d], writes=[gn])
            for sub in range(4):
                fs = fs_r.next(); ss = ss_r.next(); rs = rs_r.next()
                fw.op("act", "activation", [hb], [fs, ss], out=fs[:, :], in_=hb[:, sub, :], func=AF.Square, accum_out=ss[:, :])
                rstd_from_ss(ss, rs, D)
                fw.op("dve", "scalar_tensor_tensor", [hb, rs, gn], [hb],
                    out=hb[:, sub, :], in0=hb[:, sub, :], scalar=rs[:, 0:1], in1=gn[:, :], op0=ALU.mult, op1=ALU.mult)
                outs.append(fw.dma("sp", y[tok0 + sub * 128:tok0 + (sub + 1) * 128, :], hb[:, sub, :], reads=[hb], writes=[y]))
        return outs

    outs = []
    if stop_after >= 4:
        outs = tail_phase()
    else:
        fw.phase(PMARK)
        dummy = fw.sb([128, 8], F32, "dummy")
        fw.op("pool", "memset", [], [dummy], dummy[:, :], 0.0)
        outs = [fw.dma("sp", y[0:128, 0:8], dummy[:, :], reads=[dummy], writes=[y])]
    stats = fw.emit(final_waits=outs)
    return nc, stats


def make_in_maps(S, NSLOT, x, p, positions, attn_norm, w_in, kv_norm, w_ukv, w_o, ffn_norm,
                 w_gate, w_up, w_down, ple_norm, w_ple_gate, w_ple_proj, final_norm):
    NQ = 512 * NSLOT
    x2 = np.ascontiguousarray(np.asarray(x, dtype=np.float32).reshape(S, D))
    p2 = np.asarray(p, dtype=np.float32).reshape(S, 256)
    pos = np.asarray(positions).reshape(S).astype(np.int32)

    def bc(v, n):
        return np.ascontiguousarray(np.broadcast_to(np.asarray(v, dtype=np.float32).reshape(1, n), (128, n)))

    ident = np.eye(128, dtype=np.float32)
    rm128 = np.zeros((128, 128), np.float32)
    for i in range(64):
        rm128[i + 64, i] = -1.0
        rm128[i, i + 64] = 1.0
    rm64 = np.zeros((128, 128), np.float32)
    for i in range(32):
        rm64[i + 32, i] = -1.0
        rm64[i, i + 32] = 1.0
    invf = np.zeros((128, 2), np.float32)
    i128 = (10000.0 ** (-(np.arange(64, dtype=np.float32) * 2.0 / 128))).astype(np.float32)
    i64 = (10000.0 ** (-(np.arange(32, dtype=np.float32) * 2.0 / 64))).astype(np.float32)
    invf[:, 0] = np.tile(i128, 2) / (2 * np.pi)
    invf[:, 1] = np.tile(i64, 4) / (2 * np.pi)
    blki = bc(np.arange(S // 256, dtype=np.float32), S // 256)
    kidxc = np.ascontiguousarray(np.arange(S, dtype=np.float32).reshape(S // 128, 128).T)
    shared = {
        "x_all": x2,
        "posb_all": np.ascontiguousarray(np.broadcast_to(pos.reshape(1, S), (128, S))),
        "kidxc": kidxc,
        "g_attn_b": bc(attn_norm, D), "g_ffn_b": bc(ffn_norm, D), "g_ple_b": bc(ple_norm, D),
        "g_fin_b": bc(final_norm, D), "g_kv_b": bc(kv_norm, 512),
        "w_in": np.ascontiguousarray(np.asarray(w_in, np.float32).reshape(D, INC)),
        "w_ukv": np.ascontiguousarray(np.asarray(w_ukv, np.float32).reshape(512, 2048)),
        "w_o": np.ascontiguousarray(np.asarray(w_o, np.float32).reshape(D, D)),
        "w_gate": np.ascontiguousarray(np.asarray(w_gate, np.float32).reshape(D, DFF)),
        "w_up": np.ascontiguousarray(np.asarray(w_up, np.float32).reshape(D, DFF)),
        "w_down": np.ascontiguousarray(np.asarray(w_down, np.float32).reshape(DFF, D)),
        "w_pg": np.ascontiguousarray(np.asarray(w_ple_gate, np.float32).reshape(D, D)),
        "w_pp": np.ascontiguousarray(np.asarray(w_ple_proj, np.float32).reshape(256, D)),
        "ident": ident, "rm128": rm128, "rm64": rm64, "invf": invf, "blki": blki,
    }
    in_maps, own_idx = [], []
    for c in range(NCORES):
        idx = np.concatenate([np.arange(512 * (8 * j + c), 512 * (8 * j + c) + 512) for j in range(NSLOT)])
        own_idx.append(idx)
        m = dict(shared)
        m["x_own"] = np.ascontiguousarray(x2[idx])
        m["p_own"] = np.ascontiguousarray(p2[idx])
        m["posb_own"] = np.ascontiguousarray(np.broadcast_to(pos[idx].reshape(1, NQ), (128, NQ)))
        m["qidxb"] = np.ascontiguousarray(np.broadcast_to(idx.astype(np.float32).reshape(1, NQ), (128, NQ)))
        m["qidxc"] = np.ascontiguousarray(idx.astype(np.int32).reshape(NQ // 128, 128).T)
        in_maps.append(m)
    return in_maps, own_idx


_CACHE = {}


def kernel(**inputs):
    S = 16384
    NSLOT = S // (512 * NCORES)
    if "nc" not in _CACHE:
        _CACHE["nc"] = build(S, NSLOT)[0]
    nc = _CACHE["nc"]
    in_maps, own_idx = make_in_maps(S, NSLOT, **inputs)
    res = run_bass_kernel_spmd(nc, in_maps, core_ids=list(range(NCORES)))
    out = np.zeros((S, D), np.float32)
    for c in range(NCORES):
        out[own_idx[c]] = np.asarray(res.results[c]["y"], dtype=np.float32)
    return out.reshape(1, S, D)
```

```python
import os
import numpy as np
import concourse.bass as bass
import concourse.mybir as mybir
from concourse.bass_utils import run_bass_kernel_spmd

F32 = mybir.dt.float32
BF16 = mybir.dt.bfloat16
I32 = mybir.dt.int32
ALU = mybir.AluOpType
AF = mybir.ActivationFunctionType
AX = mybir.AxisListType

D = 2048
DFF = 5632
NFF = DFF // 128
INC = 5184
EPS = 1e-6
PI = float(np.pi)
NCORES = 8

SAME_ENGINE_SYNC = True
DBG = set(os.environ.get('KDBG', '').split(','))
CUT = int(os.environ.get('KCUT', '100000000'))
STQ = 'act' if 'stact' in DBG else 'sp'
NDMASEM = 8
SKEW = 2
ARENA_F32 = 52600


class Buf:
    __slots__ = ("t", "writer", "readers", "name", "psum")

    def __init__(self, t=None, name="", psum=False):
        self.psum = psum
        self.t = t
        self.writer = None
        self.readers = {}
        self.name = name

    def __getitem__(self, idx):
        return self.t[idx]


class Op:
    __slots__ = ("eng", "fn", "deps", "needs_inc", "seq", "dma", "sem", "semval", "prev", "desc")

    def __init__(self, eng, fn):
        self.eng = eng
        self.fn = fn
        self.deps = []
        self.needs_inc = False
        self.seq = 0
        self.dma = False
        self.sem = None
        self.semval = 0
        self.prev = None


class FW:
    ENGS = ("pe", "act", "dve", "pool", "sp")

    def __init__(self, nc):
        self.nc = nc
        self.ops = {e: [] for e in self.ENGS}
        self.esem = {e: nc.alloc_semaphore("es_" + e) for e in ("pe", "act", "dve", "pool")}
        self.dsem = {q: [nc.alloc_semaphore("ds_%s%d" % (q, i)) for i in range(NDMASEM)]
                     for q in ("sp", "pool", "act")}
        self.dcnt = {q: [0] * NDMASEM for q in self.dsem}
        self.dlast = {q: [None] * NDMASEM for q in self.dsem}
        self.drr = {q: 0 for q in self.dsem}
        self.nbuf = 0
        self.base_deps = []
        self.arena = nc.alloc_sbuf_tensor("arena", [128, ARENA_F32], F32)
        self.aoff = 0
        self.banks = [Buf(nc.alloc_psum_tensor("bank%d" % i, [128, 512], F32), "bank%d" % i, psum=True)
                      for i in range(8)]

    def sb(self, shape, dt, name=None):
        self.nbuf += 1
        n = int(np.prod(shape[1:]))
        esz = 4 if dt in (F32, I32) else 2
        nf32 = (n * esz + 3) // 4
        nf32 = (nf32 + 7) // 8 * 8
        assert self.aoff + nf32 <= ARENA_F32, "SBUF arena overflow %d" % (self.aoff + nf32)
        ap = self.arena[:, self.aoff:self.aoff + nf32]
        self.aoff += nf32
        if dt != F32:
            ap = ap.bitcast(dt)
        ap = ap[:, 0:n]
        if len(shape) == 3:
            ap = ap.rearrange("p (a b) -> p a b", a=shape[1])
        elif len(shape) == 4:
            ap = ap.rearrange("p (a b c) -> p a b c", a=shape[1], b=shape[2])
        return Buf(ap, name or "sb%d" % self.nbuf)

    def dram(self, shape, dt, name, kind="Internal"):
        return Buf(self.nc.dram_tensor(name, list(shape), dt, kind=kind).ap(), name)

    def mark(self):
        return self.aoff

    def phase(self, mark):
        self.aoff = mark
        deps = []
        for e in ("pe", "act", "dve", "pool"):
            for o in reversed(self.ops[e]):
                if not o.dma:
                    o.needs_inc = True
                    deps.append(o)
                    break
        for q in self.dsem:
            for o in self.dlast[q]:
                if o is not None:
                    deps.append(o)
        self.base_deps = deps

    def _track(self, op, reads, writes):
        deps = list(self.base_deps)
        for b in reads:
            if b.writer is not None:
                deps.append(b.writer)
            if b.psum:
                deps.extend(r for r in b.readers.values() if r.eng != op.eng)
        for b in writes:
            if b.writer is not None:
                deps.append(b.writer)
            deps.extend(b.readers.values())
        for b in writes:
            b.writer = op
            b.readers = {}
        for b in reads:
            key = op.eng if not op.dma else (op.eng, id(op.sem))
            b.readers[key] = op
        seen = set()
        for d in deps:
            if d is op or id(d) in seen:
                continue
            seen.add(id(d))
            if not d.dma:
                if d.eng == op.eng and not op.dma:
                    if d.eng == "pe" or not SAME_ENGINE_SYNC:
                        continue
                d.needs_inc = True
            op.deps.append(d)

    def op(self, eng, meth, reads=(), writes=(), *args, **kw):
        self.nrec = getattr(self, "nrec", 0) + 1
        if self.nrec > CUT:
            return None
        o = Op(eng, lambda e: getattr(e, meth)(*args, **kw))
        o.desc = (self.nrec, eng, meth)
        self._track(o, reads, writes)
        self.ops[eng].append(o)
        return o

    def dma(self, q, out_ap, in_ap, reads=(), writes=()):
        self.nrec = getattr(self, "nrec", 0) + 1
        if self.nrec > CUT:
            return None
        o = Op(q, lambda e: e.dma_start(out=out_ap, in_=in_ap))
        o.desc = (self.nrec, q, "dma", str(out_ap)[:80])
        o.dma = True
        k = self.drr[q]
        self.drr[q] = (k + 1) % NDMASEM
        o.sem = self.dsem[q][k]
        self.dcnt[q][k] += 1
        o.semval = 16 * self.dcnt[q][k]
        o.prev = self.dlast[q][k]
        self.dlast[q][k] = o
        self._track(o, reads, writes)
        self.ops[q].append(o)
        return o

    def emit(self, final_waits=()):
        nc = self.nc
        for e in ("pe", "act", "dve", "pool"):
            n = 0
            for o in self.ops[e]:
                if o.dma:
                    continue
                if o.needs_inc:
                    n += 1
                    o.seq = n
        stats = {}

        def run(ename, eng):
            waited = {}
            nw = 0

            def wait(sem, val):
                nonlocal nw
                if waited.get(id(sem), 0) >= val:
                    return
                waited[id(sem)] = val
                eng.wait_ge(sem, val)
                nw += 1

            for o in self.ops[ename]:
                for d in o.deps:
                    if d.dma:
                        wait(d.sem, d.semval)
                    else:
                        wait(self.esem[d.eng], d.seq)
                if o.dma:
                    if o.prev is not None:
                        wait(o.sem, o.prev.semval)
                    ins = o.fn(eng)
                    ins.then_inc(o.sem, 16)
                else:
                    ins = o.fn(eng)
                    if o.needs_inc:
                        ins.then_inc(self.esem[ename], 1)
            if ename == "sp":
                for o in final_waits:
                    if o is not None:
                        wait(o.sem, o.semval)
            stats[ename] = (len(self.ops[ename]), nw)

        with nc.Block() as block:
            @block.tensor
            def _(eng):
                run("pe", eng)

            @block.scalar
            def _(eng):
                run("act", eng)

            @block.vector
            def _(eng):
                run("dve", eng)

            @block.gpsimd
            def _(eng):
                run("pool", eng)

            @block.sync
            def _(eng):
                run("sp", eng)
        return stats


class Ring:
    def __init__(self, items):
        self.items = list(items)
        self.i = 0

    def next(self):
        b = self.items[self.i % len(self.items)]
        self.i += 1
        return b


def build(S, NSLOT, stop_after=99):
    NT = S // 512
    NQ = 512 * NSLOT
    NKT = S // 128
    NBLK = S // 256
    NQC = NQ // 128
    assert NBLK >= 8

    nc = bass.Bass("TRN2", target_bir_lowering=False)
    fw = FW(nc)
    B = fw.banks

    def ext(name, shape, dt=F32):
        return fw.dram(shape, dt, name, kind="ExternalInput")

    x_all = ext("x_all", [S, D]); x_own = ext("x_own", [NQ, D]); p_own = ext("p_own", [NQ, 256])
    posb_all = ext("posb_all", [128, S], I32); posb_own = ext("posb_own", [128, NQ], I32)
    qidxb_d = ext("qidxb", [128, NQ]); qidxc_d = ext("qidxc", [128, NQC], I32)
    kidxc_d = ext("kidxc", [128, NKT])
    g_attn_d = ext("g_attn_b", [128, D]); g_ffn_d = ext("g_ffn_b", [128, D])
    g_ple_d = ext("g_ple_b", [128, D]); g_fin_d = ext("g_fin_b", [128, D])
    g_kv_d = ext("g_kv_b", [128, 512])
    w_in = ext("w_in", [D, INC]); w_ukv = ext("w_ukv", [512, 2048]); w_o = ext("w_o", [D, D])
    w_gate = ext("w_gate", [D, DFF]); w_up = ext("w_up", [D, DFF]); w_down = ext("w_down", [DFF, D])
    w_pg = ext("w_pg", [D, D]); w_pp = ext("w_pp", [256, D])
    ident_d = ext("ident", [128, 128]); rm128_d = ext("rm128", [128, 128]); rm64_d = ext("rm64", [128, 128])
    inv_d = ext("invf", [128, 2]); blki_d = ext("blki", [128, NBLK])
    y = fw.dram([NQ, D], F32, "y", kind="ExternalOutput")

    def scratch(name, shape, n):
        t = fw.dram(shape, BF16, name)
        return t, [Buf(None, "%s_%d" % (name, i)) for i in range(n)]

    KTn, KTn_b = scratch("KTn", [8, 128, S], NT)
    KTp, KTp_b = scratch("KTp", [64, S], NT)
    KTm, KTm_b = scratch("KTm", [8, 128, S], NT)
    Vn, Vn_b = scratch("Vn", [S, 1024], NT)
    Vm, Vm_b = scratch("Vm", [S, 1024], NT)
    QTn, QTn_b = scratch("QTn", [8, 128, NQ], NSLOT)
    QTp, QTp_b = scratch("QTp", [8, 64, NQ], NSLOT)
    QTm, QTm_b = scratch("QTm", [8, 128, NQ], NSLOT)
    OT, OT_b = scratch("OT", [16, 128, NQ], NSLOT)
    Wo_b = fw.dram([D, D], BF16, "Wo_b")
    Wpg_b = fw.dram([D, D], BF16, "Wpg_b")
    Wpp_b = fw.dram([256, D], BF16, "Wpp_b")
    Wd_b = fw.dram([DFF, D], BF16, "Wd_b")
    Wg_b = fw.dram([NFF, 128, 16 * 128], BF16, "Wg_b")
    Wu_b = fw.dram([NFF, 128, 16 * 128], BF16, "Wu_b")

    identb = fw.sb([128, 128], BF16, "identb")
    rm128b = fw.sb([128, 128], BF16, "rm128b")
    rm64b = fw.sb([128, 128], BF16, "rm64b")
    invf = fw.sb([128, 2], F32, "invf")
    mpi = fw.sb([128, 1], F32, "mpi")
    epst = fw.sb([128, 1], F32, "epst")
    KM = fw.sb([128, 8, NBLK], F32, "KM")
    fw.dma("pool", identb[:, :], ident_d[:, :], reads=[ident_d], writes=[identb])
    fw.dma("pool", rm128b[:, :], rm128_d[:, :], reads=[rm128_d], writes=[rm128b])
    fw.dma("pool", rm64b[:, :], rm64_d[:, :], reads=[rm64_d], writes=[rm64b])
    fw.dma("sp", invf[:, :], inv_d[:, :], reads=[inv_d], writes=[invf])
    fw.op("pool", "memset", [], [mpi], mpi[:, :], -PI)
    fw.op("pool", "memset", [], [epst], epst[:, :], EPS)
    PMARK = fw.mark()

    def rstd_from_ss(ss, rstd, n):
        fw.op("act", "activation", [ss, epst], [rstd], out=rstd[:, :], in_=ss[:, :], func=AF.Sqrt,
                                            scale=1.0 / n, bias=epst[:, 0:1])
        fw.op("dve", "reciprocal", [rstd], [rstd], out=rstd[:, :], in_=rstd[:, :])

    def norm_rows(src_ap, src_bufs, gain, dst, ss, rstd, n):
        fw.op("act", "activation", src_bufs, [dst, ss], out=dst[:, 0:n], in_=src_ap, func=AF.Square,
                                            accum_out=ss[:, :])
        rstd_from_ss(ss, rstd, n)
        fw.op("dve", "scalar_tensor_tensor", list(src_bufs) + [rstd, gain], [dst], out=dst[:, 0:n], in0=src_ap, scalar=rstd[:, 0:1],
                                                      in1=gain[:, 0:n], op0=ALU.mult, op1=ALU.mult)

    def transpose_into(src, nchunk, dstT, col0, tpbanks, evac_engs=("act", "dve")):
        for g0 in range(0, nchunk, 8):
            gn = min(8, nchunk - g0)
            bank = tpbanks.next()
            tv = bank.t[:, :].bitcast(BF16).rearrange("p (a b) -> p a b", a=8)
            for k in range(gn):
                fw.op("pe", "transpose", [src, identb], [bank],
                    out=tv[:, k, :], in_=src[:, (g0 + k) * 128:(g0 + k + 1) * 128], identity=identb[:, :])
            eng = evac_engs[(g0 // 8) % len(evac_engs)]
            if eng == "act":
                fw.op("act", "copy", [bank], [dstT],
                    out=dstT[:, g0:g0 + gn, col0:col0 + 128], in_=tv[:, 0:gn, :])
            else:
                fw.op("dve", "tensor_copy", [bank], [dstT],
                    out=dstT[:, g0:g0 + gn, col0:col0 + 128], in_=tv[:, 0:gn, :])

    def precast():
        q = []
        for r in range(4):
            q.append((Wo_b[r * 512:(r + 1) * 512, :], w_o[r * 512:(r + 1) * 512, :], w_o, Wo_b))
        for r in range(4):
            q.append((Wpg_b[r * 512:(r + 1) * 512, :], w_pg[r * 512:(r + 1) * 512, :], w_pg, Wpg_b))
        q.append((Wpp_b[:, :], w_pp[:, :], w_pp, Wpp_b))
        for r in range(11):
            q.append((Wd_b[r * 512:(r + 1) * 512, :], w_down[r * 512:(r + 1) * 512, :], w_down, Wd_b))
        for (src, dst) in ((w_gate, Wg_b), (w_up, Wu_b)):
            sv = src.t.rearrange("(k p) c -> p k c", p=128)
            for f in range(NFF):
                q.append((dst.t[f].rearrange("p (k c) -> p k c", k=16), sv[:, :, f * 128:(f + 1) * 128], src, dst))
        return q

    def precast_issue(q, n):
        for _ in range(min(n, len(q))):
            o, i, sb_, db_ = q.pop(0)
            fw.dma("pool", o, i, reads=[sb_], writes=[db_])

    def proj_phase(mode):
        fw.phase(PMARK)
        if mode == "kv":
            xd, posd, ntile = x_all, posb_all, NT
            ranges = [(1536, 2112), (3136, 5184)]
        else:
            xd, posd, ntile = x_own, posb_own, NSLOT
            ranges = [(0, 1536), (2112, 3136)]
        if mode == "kv":
            parts = [(0, 512, 1536), (512, 576, 2048), (576, 1600, 3136), (1600, 2624, 4160)]
        else:
            parts = [(0, 768, 0), (768, 1536, 768), (1536, 2560, 2112)]
        ncols = parts[-1][1]
        W = fw.sb([128, 16, ncols], BF16, "W_" + mode)
        Wp = [Buf(None, "Wp%d" % i) for i in range(len(parts))]

        def wbuf(coff):
            for i, (lo, hi, _) in enumerate(parts):
                if lo <= coff < hi:
                    return Wp[i]
            raise AssertionError(coff)

        wv = w_in.t.rearrange("(k p) c -> p k c", p=128)

        def load_part(i):
            lo, hi, src = parts[i]
            for k in range(16):
                fw.dma("pool", W[:, k, lo:hi], wv[:, k, src:src + (hi - lo)], reads=[w_in], writes=[Wp[i]])

        if mode == "kv":
            WK = fw.sb([128, 4, 1024], BF16, "WukvK")
            WV = fw.sb([128, 4, 1024], BF16, "WukvV")
            uv = w_ukv.t.rearrange("(k p) (h t c) -> p k h t c", p=128, h=8, t=2)
            gkv = fw.sb([128, 512], F32, "gkv")
            fw.dma("sp", gkv[:, :], g_kv_d[:, :], reads=[g_kv_d], writes=[gkv])

        def load_weights():
            load_part(0)
            if mode == "kv":
                for k in range(4):
                    fw.dma("pool", WK[:, k, :].rearrange("p (h c) -> p h c", h=8), uv[:, k, :, 0, :], reads=[w_ukv], writes=[WK])
                    fw.dma("pool", WV[:, k, :].rearrange("p (h c) -> p h c", h=8), uv[:, k, :, 1, :], reads=[w_ukv], writes=[WV])
            for i in range(1, len(parts)):
                load_part(i)

        pre_q = precast() if mode == "kv" else []
        gat = fw.sb([128, D], F32, "gat")
        fw.dma("sp", gat[:, :], g_attn_d[:, :], reads=[g_attn_d], writes=[gat])

        xs_r = Ring([fw.sb([128, D], F32) for _ in range(2)])
        as_r = Ring([fw.sb([128, D], BF16) for _ in range(2)])
        aT_r = Ring([fw.sb([128, 16, 512], BF16, "aT%d" % i) for i in range(2)])
        ss_r = Ring([fw.sb([128, 1], F32) for _ in range(4)])
        rs_r = Ring([fw.sb([128, 1], F32) for _ in range(4)])
        posi = fw.sb([128, 512], I32, "posi")
        posf = fw.sb([128, 512], F32, "posf")
        tt = fw.sb([128, 512], F32, "tt"); tki = fw.sb([128, 512], I32, "tki"); tkf = fw.sb([128, 512], F32, "tkf")
        tabs = {k: fw.sb([128, 512], F32, "tab" + k) for k in ("s128", "c128", "s64", "c64")}
        t1_r = Ring([fw.sb([128, 512], F32) for _ in range(2)])
        t2_r = Ring([fw.sb([128, 512], F32) for _ in range(2)])
        xb_r = Ring([fw.sb([128, 512], BF16) for _ in range(2)])
        ob_r = Ring([fw.sb([128, 512], BF16) for _ in range(4)])
        if mode == "kv":
            ckv_r = Ring([fw.sb([128, 512], BF16) for _ in range(2)])
            kvnT = fw.sb([128, 4, 512], BF16, "kvnT")
        tp_r = Ring([B[0], B[1]])
        acc_r = Ring([B[2], B[3], B[4], B[7]])
        rp_r = Ring([B[5], B[6]])

        pending_tails = []
        new_tails = []

        def rope_out(acc, M, kind, dst_ap, dst_buf, kmean=None):
            rm = rm128b if kind == "128" else rm64b
            cs, sn = tabs["c" + kind], tabs["s" + kind]
            xb = xb_r.next(); t1 = t1_r.next()
            fw.op("act", "copy", [acc], [xb], out=xb[0:M, :], in_=acc[0:M, :])
            fw.op("dve", "tensor_tensor", [acc, cs], [t1], out=t1[0:M, :], in0=acc[0:M, :], in1=cs[0:M, :], op=ALU.mult)

            def tail():
                rp = rp_r.next(); t2 = t2_r.next(); ob = ob_r.next()
                fw.op("pe", "matmul", [rm, xb], [rp], rp[0:M, :], lhsT=rm[0:M, 0:M], rhs=xb[0:M, :], start=True, stop=True)
                fw.op("dve", "tensor_tensor", [rp, sn], [t2], out=t2[0:M, :], in0=rp[0:M, :], in1=sn[0:M, :], op=ALU.mult)
                fw.op("dve", "tensor_tensor", [t1, t2], [t1], out=t1[0:M, :], in0=t1[0:M, :], in1=t2[0:M, :], op=ALU.add)
                fw.op("act", "copy", [t1], [ob], out=ob[0:M, :], in_=t1[0:M, :])
                if kmean is not None:
                    fw.op("dve", "tensor_reduce", [t1], [KM], out=kmean, in_=t1[:, :].rearrange("p (b j) -> p b j", b=2),
                          axis=AX.X, op=ALU.add)
                fw.dma(STQ, dst_ap, ob[0:M, :], reads=[ob], writes=[dst_buf])

            new_tails.append(tail)

        pcount = [0]

        def plain_out(acc, M, dst_ap, dst_buf):
            ob = ob_r.next()
            pcount[0] += 1
            if pcount[0] % 2:
                fw.op("act", "copy", [acc], [ob], out=ob[0:M, :], in_=acc[0:M, :])
            else:
                fw.op("dve", "tensor_copy", [acc], [ob], out=ob[0:M, :], in_=acc[0:M, :])
            fw.dma(STQ, dst_ap, ob[0:M, :], reads=[ob], writes=[dst_buf])

        def fm_block(coff, M):
            acc = acc_r.next()
            for k in range(16):
                fw.op("pe", "matmul", [wbuf(coff), aT], [acc], acc[0:M, :], lhsT=W[:, k, coff:coff + M], rhs=aT[:, k, :],
                                                    start=(k == 0), stop=(k == 15))
            return acc

        def tm_block(coff, sub):
            acc = acc_r.next()
            for k in range(16):
                fw.op("pe", "matmul", [wbuf(coff), aT], [acc], acc[:, :], lhsT=aT[:, k, sub * 128:(sub + 1) * 128],
                                                    rhs=W[:, k, coff:coff + 512], start=(k == 0), stop=(k == 15))
            return acc

        TABS = [("128", 0, "s", 0.5), ("128", 0, "c", 0.75), ("64", 1, "s", 0.5), ("64", 1, "c", 0.75)]

        def table_pos(st):
            tok0 = st * 512
            fw.dma("sp", posi[:, :], posd[:, tok0:tok0 + 512], reads=[posd], writes=[posi])
            fw.op("dve", "tensor_copy", [posi], [posf], out=posf[:, :], in_=posi[:, :])

        def table_pool(k):
            kind, col, fn, shift = TABS[k]
            fw.op("dve", "tensor_scalar", [posf, invf], [tt],
                  out=tt[:, :], in0=posf[:, :], scalar1=invf[:, col:col + 1], scalar2=shift,
                  op0=ALU.mult, op1=ALU.add)
            fw.op("dve", "tensor_copy", [tt], [tki], out=tki[:, :], in_=tt[:, :])
            fw.op("dve", "tensor_copy", [tki], [tkf], out=tkf[:, :], in_=tki[:, :])
            fw.op("dve", "tensor_tensor", [tt, tkf], [tt], out=tt[:, :], in0=tt[:, :], in1=tkf[:, :], op=ALU.subtract)
            fw.op("dve", "tensor_single_scalar", [tt], [tkf], out=tkf[:, :], in_=tt[:, :], scalar=0.0, op=ALU.is_lt)
            fw.op("dve", "tensor_tensor", [tt, tkf], [tkf], out=tkf[:, :], in0=tkf[:, :], in1=tt[:, :], op=ALU.add)

        def table_act(k):
            kind, col, fn, shift = TABS[k]
            tab = tabs[fn + kind]
            fw.op("act", "activation", [tkf, mpi], [tab], out=tab[:, :], in_=tkf[:, :], func=AF.Sin,
                  scale=2 * PI, bias=mpi[:, 0:1])

        def tables(st):
            table_pos(st)
            for k in range(4):
                table_pool(k)
                table_act(k)

        prep_as = {}

        def prep_norm(st, sub):
            tok0 = st * 512
            xs = xs_r.next(); a_s = as_r.next(); ss = ss_r.next(); rs = rs_r.next()
            fw.dma("sp", xs[:, :], xd[tok0 + sub * 128:tok0 + (sub + 1) * 128, :], reads=[xd], writes=[xs])
            norm_rows(xs[:, :], [xs], gat, a_s, ss, rs, D)
            prep_as[(st, sub)] = a_s

        def prep_tr(st, sub, aT_dst):
            transpose_into(prep_as.pop((st, sub)), 16, aT_dst, sub * 128, tp_r)

        def prep_sub(st, sub, aT_dst):
            prep_norm(st, sub)
            prep_tr(st, sub, aT_dst)

        def blocks_for(st):
            tok0 = st * 512
            bl = []
            if mode == "kv":
                def ckv_blk(sub):
                    acc = tm_block(0, sub)
                    ck = ckv_r.next(); ss = ss_r.next(); rs = rs_r.next()
                    norm_rows(acc[:, :], [acc], gkv, ck, ss, rs, 512)
                    new_tails.append(lambda: transpose_into(ck, 4, kvnT, sub * 128, tp_r))

                def vmb_blk(sub, cb):
                    acc = tm_block(1600 + cb * 512, sub)
                    plain_out(acc, 128, Vm[tok0 + sub * 128:tok0 + (sub + 1) * 128, cb * 512:(cb + 1) * 512], Vm_b[st])

                def knope_blk(h):
                    acc = acc_r.next()
                    for k in range(4):
                        fw.op("pe", "matmul", [WK, kvnT], [acc],
                              acc[:, :], lhsT=WK[:, k, h * 128:(h + 1) * 128], rhs=kvnT[:, k, :],
                              start=(k == 0), stop=(k == 3))
                    plain_out(acc, 128, KTn[h, :, tok0:tok0 + 512], KTn_b[st])

                def vmla_blk(sub, cb):
                    acc = acc_r.next()
                    for k in range(4):
                        fw.op("pe", "matmul", [WV, kvnT], [acc],
                              acc[:, :], lhsT=kvnT[:, k, sub * 128:(sub + 1) * 128],
                              rhs=WV[:, k, cb * 512:(cb + 1) * 512], start=(k == 0), stop=(k == 3))
                    plain_out(acc, 128, Vn[tok0 + sub * 128:tok0 + (sub + 1) * 128, cb * 512:(cb + 1) * 512], Vn_b[st])

                def kpe_blk():
                    acc = fm_block(512, 64)
                    rope_out(acc, 64, "64", KTp[0:64, tok0:tok0 + 512], KTp_b[st])

                def kmb_blk(h):
                    acc = fm_block(576 + h * 128, 128)
                    rope_out(acc, 128, "128", KTm[h, :, tok0:tok0 + 512], KTm_b[st], kmean=KM[:, h, 2 * st:2 * st + 2])

                for sub in range(4):
                    bl.append(lambda sub=sub: ckv_blk(sub))
                for sub in range(2):
                    for cb in range(2):
                        bl.append(lambda sub=sub, cb=cb: vmb_blk(sub, cb))
                for h in range(8):
                    bl.append(lambda h=h: knope_blk(h))
                for sub in range(2, 4):
                    for cb in range(2):
                        bl.append(lambda sub=sub, cb=cb: vmb_blk(sub, cb))
                for sub in range(4):
                    for cb in range(2):
                        bl.append(lambda sub=sub, cb=cb: vmla_blk(sub, cb))
                bl.append(kpe_blk)
                for h in range(8):
                    bl.append(lambda h=h: kmb_blk(h))
            else:
                def qn_blk(h):
                    acc = fm_block(h * 192, 128)
                    plain_out(acc, 128, QTn[h, :, tok0:tok0 + 512], QTn_b[st])

                def qp_blk(h):
                    acc = fm_block(h * 192 + 128, 64)
                    rope_out(acc, 64, "64", QTp[h, :, tok0:tok0 + 512], QTp_b[st])

                def qm_blk(h):
                    acc = fm_block(1536 + h * 128, 128)
                    rope_out(acc, 128, "128", QTm[h, :, tok0:tok0 + 512], QTm_b[st])

                for h in range(8):
                    bl.append(lambda h=h: qn_blk(h))
                    bl.append(lambda h=h: qp_blk(h))
                for h in range(8):
                    bl.append(lambda h=h: qm_blk(h))
            return bl

        nt_run = ntile if 'onetile' not in DBG else 1
        cur = {"aT": aT_r.next()}
        for sub in range(4):
            prep_sub(0, sub, cur["aT"])
        load_weights()
        for st in range(nt_run):
            aT = cur["aT"]
            if st > 0 and 'noprecast' not in DBG:
                precast_issue(pre_q, 4)
            bl = blocks_for(st)
            nxt = None
            sched = {}

            def at(pos, fn):
                sched.setdefault(max(0, min(pos, len(bl) - 1)), []).append(fn)

            if mode == "kv":
                table_pos(st)
                for k in range(4):
                    at(1 + 3 * k, lambda k=k: table_pool(k))
                    at(4 + 3 * k, lambda k=k: table_act(k))
            else:
                tables(st)
            if st + 1 < nt_run:
                nxt = aT_r.next()
                step = len(bl) // 5
                for sub in range(4):
                    at(step * (sub + 1) - 5, lambda sub=sub: prep_norm(st + 1, sub))
                    at(step * (sub + 1), lambda sub=sub: prep_tr(st + 1, sub, nxt))
            for bi, blk in enumerate(bl):
                blk()
                while pending_tails:
                    pending_tails.pop(0)()
                pending_tails.extend(new_tails)
                del new_tails[:]
                for fn in sched.get(bi, []):
                    fn()
            while pending_tails:
                pending_tails.pop(0)()
            if nxt is not None:
                cur["aT"] = nxt
        if 'noprecast' not in DBG:
            precast_issue(pre_q, len(pre_q))

    if stop_after >= 1:
        proj_phase("kv")
    if stop_after >= 2:
        proj_phase("q")

    def attention_phase():
        fw.phase(PMARK)
        qidxb = fw.sb([128, NQ], F32, "qidxb")
        kidxc = fw.sb([128, NKT], F32, "kidxc")
        qidxi = fw.sb([128, NQC], I32, "qidxi")
        qblkf = fw.sb([128, NQC], F32, "qblkf")
        blki = fw.sb([128, NBLK], F32, "blki")
        pastm = fw.sb([128, NQC, NBLK], F32, "pastm")
        ownm = fw.sb([128, NQC, NBLK], F32, "ownm")
        pbias = fw.sb([128, NQC, NBLK], F32, "pbias")
        KMb = fw.sb([128, 8, NBLK], BF16, "KMb")
        fw.dma("sp", qidxb[:, :], qidxb_d[:, :], reads=[qidxb_d], writes=[qidxb])
        fw.dma("sp", kidxc[:, :], kidxc_d[:, :], reads=[kidxc_d], writes=[kidxc])
        fw.dma("sp", qidxi[:, :], qidxc_d[:, :], reads=[qidxc_d], writes=[qidxi])
        fw.dma("sp", blki[:, :], blki_d[:, :], reads=[blki_d], writes=[blki])
        fw.op("dve", "tensor_single_scalar", [qidxi], [qidxi], out=qidxi[:, :], in_=qidxi[:, :], scalar=8, op=ALU.arith_shift_right)
        fw.op("dve", "tensor_copy", [qidxi], [qblkf], out=qblkf[:, :], in_=qidxi[:, :])
        for ci in range(NQC):
            fw.op("dve", "tensor_scalar", [blki, qblkf], [pastm], out=pastm[:, ci, :], in0=blki[:, :], scalar1=qblkf[:, ci:ci + 1],
                                                          scalar2=None, op0=ALU.is_lt)
            fw.op("dve", "tensor_scalar", [blki, qblkf], [ownm], out=ownm[:, ci, :], in0=blki[:, :], scalar1=qblkf[:, ci:ci + 1],
                                                          scalar2=None, op0=ALU.is_equal)
        fw.op("dve", "tensor_scalar", [pastm], [pbias], out=pbias[:, :, :], in0=pastm[:, :, :], scalar1=-1.0, scalar2=1e30,
                                               op0=ALU.add, op1=ALU.mult)
        fw.op("act", "activation", [KM], [KMb], out=KMb[:, :, :], in_=KM[:, :, :], func=AF.Copy, scale=1.0 / 256)

        NCH = 2
        Kc = [fw.sb([128, 4096], BF16, "Kc%d" % i) for i in range(NCH)]
        Kp = [fw.sb([128, 4096], BF16, "Kp%d" % i) for i in range(NCH)]
        Vc = [fw.sb([128, 32, 129], BF16, "Vc%d" % i) for i in range(NCH)]
        for i in range(NCH):
            fw.op("pool", "memset", [], [Vc[i]], Vc[i][:, :, 128:129], 1.0)
        Qn = [fw.sb([128, 512], BF16) for _ in range(2)]
        Qp = [fw.sb([128, 512], BF16) for _ in range(2)]
        pt_r = Ring([fw.sb([128, 512], BF16) for _ in range(5)])
        ptm_r = Ring([fw.sb([128, 512], BF16) for _ in range(4)])
        sc_mb = Ring([B[0], B[1], B[2]])
        sc_mla = Ring([B[0], B[1], B[2], B[5]])
        osets = [(B[3], B[4]), (B[5], B[6])]
        gtp = B[7]
        accs = [fw.sb([128, 129], F32, "accs%d" % c) for c in range(4)]
        on_r = Ring([fw.sb([128, 128], BF16) for _ in range(2)])
        rinv_r = Ring([fw.sb([128, 1], F32) for _ in range(2)])
        ott_r = Ring([fw.sb([128, 512], BF16) for _ in range(2)])
        gm = fw.sb([128, 4, NBLK], F32, "gm")
        mx8 = fw.sb([128, 4, 8], F32, "mx8")
        sel = fw.sb([128, 4, NBLK], F32, "sel")

        items = []
        for j in range(NSLOT):
            for h in range(8):
                items.append((j, "mla", h))
            for h in range(8):
                items.append((j, "mb", h))
        loads = []
        for ii, (j, kind, h) in enumerate(items):
            nchunk = min(j + 1, (S + 4095) // 4096)
            for ch in range(nchunk):
                loads.append((ii, ch))
        state = {"li": 0}

        def issue_load(li):
            ii, ch = loads[li]
            j, kind, h = items[ii]
            slot = li % NCH
            k0 = ch * 4096
            nk = min(4096, S - k0)
            sts = list(range(k0 // 512, (k0 + nk) // 512))
            if kind == "mla":
                fw.dma("sp", Kc[slot][:, 0:nk], KTn[h, :, k0:k0 + nk], reads=[KTn_b[s] for s in sts], writes=[Kc[slot]])
                fw.dma("sp", Kp[slot][0:64, 0:nk], KTp[0:64, k0:k0 + nk], reads=[KTp_b[s] for s in sts], writes=[Kp[slot]])
                vsrc, vb = Vn, Vn_b
            else:
                fw.dma("sp", Kc[slot][:, 0:nk], KTm[h, :, k0:k0 + nk], reads=[KTm_b[s] for s in sts], writes=[Kc[slot]])
                vsrc, vb = Vm, Vm_b
            nt = nk // 128
            half = max(1, nt // 2)
            for t0 in range(0, nt, half):
                fw.dma("sp", Vc[slot][:, t0:t0 + half, 0:128],
                       vsrc[k0 + t0 * 128:k0 + (t0 + half) * 128, h * 128:(h + 1) * 128].rearrange("(t p) c -> p t c", p=128),
                       reads=[vb[s] for s in sts], writes=[Vc[slot]])
            return slot

        def issue_q(ii):
            j, kind, h = items[ii]
            qs = ii % 2
            if kind == "mla":
                fw.dma("sp", Qn[qs][:, :], QTn[h, :, j * 512:(j + 1) * 512], reads=[QTn_b[j]], writes=[Qn[qs]])
                fw.dma("sp", Qp[qs][0:64, :], QTp[h, :, j * 512:(j + 1) * 512], reads=[QTp_b[j]], writes=[Qp[qs]])
            else:
                fw.dma("sp", Qn[qs][:, :], QTm[h, :, j * 512:(j + 1) * 512], reads=[QTm_b[j]], writes=[Qn[qs]])

        def finish(get_o, h16, j, osrc_bufs):
            ott = ott_r.next()
            tv = gtp.t[:, :].bitcast(BF16)
            for c in range(4):
                rinv = rinv_r.next(); on = on_r.next()
                o_ap = get_o(c)
                fw.op("dve", "reciprocal", osrc_bufs, [rinv], out=rinv[:, :], in_=o_ap[:, 128:129])
                fw.op("dve", "tensor_scalar", list(osrc_bufs) + [rinv], [on],
                    out=on[:, :], in0=o_ap[:, 0:128], scalar1=rinv[:, 0:1], scalar2=None, op0=ALU.mult)
                fw.op("pe", "transpose", [on, identb], [gtp], out=tv[:, c * 128:(c + 1) * 128], in_=on[:, :], identity=identb[:, :])
            fw.op("act", "copy", [gtp], [ott], out=ott[:, :], in_=tv[:, 0:512])
            fw.dma(STQ, OT[h16, :, j * 512:(j + 1) * 512], ott[:, :], reads=[ott], writes=[OT_b[j]])

        issue_q(0)
        slot_of = {0: issue_load(0)}
        li_next = 1
        li = 0
        for ii, (j, kind, h) in enumerate(items):
            if ii + 1 < len(items):
                issue_q(ii + 1)
            qs = ii % 2
            nchunk = min(j + 1, (S + 4095) // 4096)
            nkt_total = min(32 * j + 32, NKT)
            scale = (192 ** -0.5) if kind == "mla" else (128 ** -0.5)
            if kind == "mb":
                for c in range(4):
                    fw.op("pe", "matmul", [Qn[qs], KMb], [gtp], gtp[:, c * NBLK:(c + 1) * NBLK], lhsT=Qn[qs][:, c * 128:(c + 1) * 128],
                                                        rhs=KMb[:, h, :], start=True, stop=True)
                fw.op("dve", "tensor_tensor", [gtp, pbias], [gm], out=gm[:, :, :], in0=gtp[:, 0:4 * NBLK].rearrange("p (c n) -> p c n", c=4),
                                                       in1=pbias[:, 4 * j:4 * j + 4, :], op=ALU.add)
                for c in range(4):
                    fw.op("dve", "max", [gm], [mx8], out=mx8[:, c, :], in_=gm[:, c, :])
                for c in range(4):
                    fw.op("dve", "tensor_scalar", [gm, mx8], [sel], out=sel[:, c, :], in0=gm[:, c, :], scalar1=mx8[:, c, 2:3],
                                                                scalar2=None, op0=ALU.is_ge)
                fw.op("dve", "tensor_tensor", [sel, pastm], [sel], out=sel[:, :, :], in0=sel[:, :, :], in1=pastm[:, 4 * j:4 * j + 4, :], op=ALU.mult)
                fw.op("dve", "tensor_tensor", [sel, ownm], [sel], out=sel[:, :, :], in0=sel[:, :, :], in1=ownm[:, 4 * j:4 * j + 4, :], op=ALU.add)
            tiles = []
            for ch in range(nchunk):
                kt0 = ch * 32
                for ktl in range(min(32, nkt_total - kt0)):
                    tiles.append((ch, ktl, kt0 + ktl))
            st = {"slot": None, "oset_i": 0}

            def stage_a(ch, ktl, kt):
                nonlocal li, li_next
                if ktl == 0:
                    st["slot"] = slot_of.pop(li)
                    if li_next < len(loads):
                        slot_of[li_next] = issue_load(li_next)
                        li_next += 1
                    li += 1
                slot = st["slot"]
                diag = (ch == nchunk - 1)
                sc = (sc_mla if kind == "mla" else sc_mb).next()
                last_qk = (kind != "mla")
                fw.op("pe", "matmul", [Kc[slot], Qn[qs]], [sc],
                      sc[:, :], lhsT=Kc[slot][:, ktl * 128:(ktl + 1) * 128], rhs=Qn[qs][:, :], start=True, stop=last_qk)
                if kind == "mla":
                    fw.op("pe", "matmul", [Kp[slot], Qp[qs]], [sc],
                          sc[:, :], lhsT=Kp[slot][0:64, ktl * 128:(ktl + 1) * 128], rhs=Qp[qs][0:64, :], start=False, stop=True)
                pt = pt_r.next()
                fw.op("act", "activation", [sc], [pt], out=pt[:, :], in_=sc[:, :], func=AF.Exp, scale=scale)
                if diag:
                    ptm = ptm_r.next()
                    fw.op("dve", "scalar_tensor_tensor", [qidxb, kidxc, pt], [ptm],
                          out=ptm[:, :], in0=qidxb[:, j * 512:(j + 1) * 512], scalar=kidxc[:, kt:kt + 1], in1=pt[:, :],
                          op0=ALU.is_ge, op1=ALU.mult)
                    pt = ptm
                return (pt, slot, ktl, kt)

            def stage_b(pt, slot, ktl, kt):
                if kind == "mla":
                    ob = osets[0]
                    first, last = (kt == 0), (kt == nkt_total - 1)
                else:
                    ob = osets[st["oset_i"] % 2]
                    first, last = (kt % 2 == 0), (kt % 2 == 1)
                for c in range(4):
                    bank = ob[c // 2]
                    c0 = (c % 2) * 129
                    fw.op("pe", "matmul", [pt, Vc[slot]], [bank],
                          bank[:, c0:c0 + 129], lhsT=pt[:, c * 128:(c + 1) * 128], rhs=Vc[slot][:, ktl, :],
                          start=(first and c % 2 == 0), stop=last)
                if kind == "mb" and last:
                    n = kt // 2
                    for c in range(4):
                        bank = ob[c // 2]
                        c0 = (c % 2) * 129
                        if n == 0:
                            fw.op("dve", "tensor_scalar", [bank, sel], [accs[c]],
                                  out=accs[c][:, :], in0=bank[:, c0:c0 + 129], scalar1=sel[:, c, n:n + 1], scalar2=None, op0=ALU.mult)
                        else:
                            fw.op("dve", "scalar_tensor_tensor", [bank, sel, accs[c]], [accs[c]],
                                  out=accs[c][:, :], in0=bank[:, c0:c0 + 129], scalar=sel[:, c, n:n + 1], in1=accs[c][:, :],
                                  op0=ALU.mult, op1=ALU.add)
                    st["oset_i"] += 1

            pend = []
            for t in tiles:
                if t[1] == 0:
                    while pend:
                        stage_b(*pend.pop(0))
                pend.append(stage_a(*t))
                if len(pend) > (SKEW + 1 if kind == "mla" else SKEW):
                    stage_b(*pend.pop(0))
            while pend:
                stage_b(*pend.pop(0))
            if kind == "mla":
                ob = osets[0]
                finish(lambda c: ob[c // 2][:, (c % 2) * 129:(c % 2) * 129 + 129], h, j, [ob[0], ob[1]])
            else:
                finish(lambda c: accs[c][:, :], 8 + h, j, accs)

    if stop_after >= 3:
        attention_phase()

    def tail_phase():
        fw.phase(PMARK)
        hb = fw.sb([128, 4, D], F32, "hb")
        fT = fw.sb([128, 16, 512], BF16, "fT")
        fs_r = Ring([fw.sb([128, D], BF16) for _ in range(2)])
        HF = NFF // 2
        actT = fw.sb([128, HF, 512], BF16, "actT")
        wa_r = Ring([fw.sb([128, 16, 256], BF16) for _ in range(3)])
        wg_r = Ring([fw.sb([128, 16, 128], BF16) for _ in range(3)])
        wu_r = Ring([fw.sb([128, 16, 128], BF16) for _ in range(3)])
        wd_r = Ring([fw.sb([128, 11, 512], BF16) for _ in range(3)])
        OTt = fw.sb([128, 16, 512], BF16, "OTt")
        pT = fw.sb([128, 2, 512], BF16, "pT")
        pin_r = Ring([fw.sb([128, 256], F32) for _ in range(2)])
        pb_r = Ring([fw.sb([128, 256], BF16) for _ in range(2)])
        wpp_r = Ring([fw.sb([128, 2, 256], BF16) for _ in range(2)])
        gn = fw.sb([128, D], F32, "gn")
        sg_r = Ring([fw.sb([128, 512], F32) for _ in range(2)])
        ss_r = Ring([fw.sb([128, 1], F32) for _ in range(4)])
        rs_r = Ring([fw.sb([128, 1], F32) for _ in range(4)])
        tp_r = Ring([B[0], B[1]])
        acc_r = Ring([B[2], B[3], B[4], B[5]])
        acc2_r = Ring([B[6], B[7]])
        outs = []
        Wo_v = Wo_b.t.rearrange("(k p) c -> p k c", p=128)
        Wpg_v = Wpg_b.t.rearrange("(k p) c -> p k c", p=128)
        Wpp_v = Wpp_b.t.rearrange("(k p) c -> p k c", p=128)
        Wd_v = Wd_b.t.rearrange("(k p) c -> p k c", p=128)

        def norm_to_fT(gain_d):
            fw.dma("sp", gn[:, :], gain_d[:, :], reads=[gain_d], writes=[gn])
            for sub in range(4):
                fs = fs_r.next(); ss = ss_r.next(); rs = rs_r.next()
                norm_rows(hb[:, sub, :], [hb], gn, fs, ss, rs, D)
                transpose_into(fs, 16, fT, sub * 128, tp_r)

        for j in range(NSLOT):
            tok0 = j * 512
            for sub in range(4):
                fw.dma("sp", hb[:, sub, :], x_own[tok0 + sub * 128:tok0 + (sub + 1) * 128, :], reads=[x_own], writes=[hb])
            for hh in range(16):
                fw.dma("sp", OTt[:, hh, :], OT[hh, :, tok0:tok0 + 512], reads=[OT_b[j]], writes=[OTt])
            for cb in range(8):
                wa = wa_r.next()
                fw.dma("sp", wa[:, :, :], Wo_v[:, :, cb * 256:(cb + 1) * 256], reads=[Wo_b], writes=[wa])
                for sub in range(4):
                    acc = acc_r.next()
                    for k in range(16):
                        fw.op("pe", "matmul", [OTt, wa], [acc],
                            acc[:, 0:256], lhsT=OTt[:, k, sub * 128:(sub + 1) * 128], rhs=wa[:, k, :], start=(k == 0), stop=(k == 15))
                    fw.op("dve", "tensor_tensor", [hb, acc], [hb],
                        out=hb[:, sub, cb * 256:(cb + 1) * 256], in0=hb[:, sub, cb * 256:(cb + 1) * 256], in1=acc[:, 0:256], op=ALU.add)
            norm_to_fT(g_ffn_d)
            for half in range(2):
                for fl in range(HF):
                    f = half * HF + fl
                    wg = wg_r.next(); wu = wu_r.next()
                    fw.dma("sp", wg[:, :, :], Wg_b.t[f].rearrange("p (k c) -> p k c", k=16), reads=[Wg_b], writes=[wg])
                    fw.dma("sp", wu[:, :, :], Wu_b.t[f].rearrange("p (k c) -> p k c", k=16), reads=[Wu_b], writes=[wu])
                    ga = acc_r.next(); ua = acc_r.next()
                    for k in range(16):
                        fw.op("pe", "matmul", [wg, fT], [ga], ga[:, :], lhsT=wg[:, k, :], rhs=fT[:, k, :],
                                                                          start=(k == 0), stop=(k == 15))
                    for k in range(16):
                        fw.op("pe", "matmul", [wu, fT], [ua], ua[:, :], lhsT=wu[:, k, :], rhs=fT[:, k, :],
                                                                          start=(k == 0), stop=(k == 15))
                    sg = sg_r.next()
                    fw.op("act", "activation", [ga], [sg], out=sg[:, :], in_=ga[:, :], func=AF.Silu)
                    fw.op("dve", "tensor_tensor", [sg, ua], [actT], out=actT[:, fl, :], in0=sg[:, :], in1=ua[:, :], op=ALU.mult)
                for cb in range(4):
                    wds = []
                    for q in range(HF // 11):
                        wd = wd_r.next()
                        r0 = half * HF + q * 11
                        fw.dma("sp", wd[:, :, :], Wd_v[:, r0:r0 + 11, cb * 512:(cb + 1) * 512], reads=[Wd_b], writes=[wd])
                        wds.append(wd)
                    for sub in range(4):
                        acc = acc2_r.next()
                        for fl in range(HF):
                            wd = wds[fl // 11]
                            fw.op("pe", "matmul", [actT, wd], [acc],
                                acc[:, :], lhsT=actT[:, fl, sub * 128:(sub + 1) * 128], rhs=wd[:, fl % 11, :],
                                start=(fl == 0), stop=(fl == HF - 1))
                        fw.op("dve", "tensor_tensor", [hb, acc], [hb],
                            out=hb[:, sub, cb * 512:(cb + 1) * 512], in0=hb[:, sub, cb * 512:(cb + 1) * 512], in1=acc[:, :], op=ALU.add)
            norm_to_fT(g_ple_d)
            for sub in range(4):
                pin = pin_r.next(); pb = pb_r.next()
                fw.dma("sp", pin[:, :], p_own[tok0 + sub * 128:tok0 + (sub + 1) * 128, :], reads=[p_own], writes=[pin])
                fw.op("act", "copy", [pin], [pb], out=pb[:, :], in_=pin[:, :])
                transpose_into(pb, 2, pT, sub * 128, tp_r)
            for cb in range(8):
                wa = wa_r.next(); wpp = wpp_r.next()
                fw.dma("sp", wa[:, :, :], Wpg_v[:, :, cb * 256:(cb + 1) * 256], reads=[Wpg_b], writes=[wa])
                fw.dma("sp", wpp[:, :, :], Wpp_v[:, :, cb * 256:(cb + 1) * 256], reads=[Wpp_b], writes=[wpp])
                for sub in range(4):
                    acc = acc_r.next(); acc2 = acc2_r.next()
                    for k in range(16):
                        fw.op("pe", "matmul", [fT, wa], [acc],
                            acc[:, 0:256], lhsT=fT[:, k, sub * 128:(sub + 1) * 128], rhs=wa[:, k, :], start=(k == 0), stop=(k == 15))
                    for k in range(2):
                        fw.op("pe", "matmul", [pT, wpp], [acc2],
                            acc2[:, 0:256], lhsT=pT[:, k, sub * 128:(sub + 1) * 128], rhs=wpp[:, k, :], start=(k == 0), stop=(k == 1))
                    sg = sg_r.next()
                    fw.op("act", "activation", [acc], [sg], out=sg[:, 0:256], in_=acc[:, 0:256], func=AF.Sigmoid)
                    fw.op("dve", "tensor_tensor", [sg, acc2], [sg], out=sg[:, 0:256], in0=sg[:, 0:256], in1=acc2[:, 0:256], op=ALU.mult)
                    fw.op("pool", "tensor_tensor", [hb, sg], [hb],
                        out=hb[:, sub, cb * 256:(cb + 1) * 256], in0=hb[:, sub, cb * 256:(cb + 1) * 256], in1=sg[:, 0:256], op=ALU.add)
            fw.dma("sp", gn[:, :], g_fin_d[:, :], reads=[g_fin_d], writes=[gn])
            for sub in range(4):
                fs = fs_r.next(); ss = ss_r.next(); rs = rs_r.next()
                fw.op("act", "activation", [hb], [fs, ss], out=fs[:, :], in_=hb[:, sub, :], func=AF.Square, accum_out=ss[:, :])
                rstd_from_ss(ss, rs, D)
                fw.op("dve", "scalar_tensor_tensor", [hb, rs, gn], [hb],
                    out=hb[:, sub, :], in0=hb[:, sub, :], scalar=rs[:, 0:1], in1=gn[:, :], op0=ALU.mult, op1=ALU.mult)
                outs.append(fw.dma("sp", y[tok0 + sub * 128:tok0 + (sub + 1) * 128, :], hb[:, sub, :], reads=[hb], writes=[y]))
        return outs

    outs = []
    if stop_after >= 4:
        outs = tail_phase()
    else:
        fw.phase(PMARK)
        dummy = fw.sb([128, 8], F32, "dummy")
        fw.op("pool", "memset", [], [dummy], dummy[:, :], 0.0)
        outs = [fw.dma("sp", y[0:128, 0:8], dummy[:, :], reads=[dummy], writes=[y])]
    stats = fw.emit(final_waits=outs)
    return nc, stats


def make_in_maps(S, NSLOT, x, p, positions, attn_norm, w_in, kv_norm, w_ukv, w_o, ffn_norm,
                 w_gate, w_up, w_down, ple_norm, w_ple_gate, w_ple_proj, final_norm):
    NQ = 512 * NSLOT
    x2 = np.ascontiguousarray(np.asarray(x, dtype=np.float32).reshape(S, D))
    p2 = np.asarray(p, dtype=np.float32).reshape(S, 256)
    pos = np.asarray(positions).reshape(S).astype(np.int32)

    def bc(v, n):
        return np.ascontiguousarray(np.broadcast_to(np.asarray(v, dtype=np.float32).reshape(1, n), (128, n)))

    ident = np.eye(128, dtype=np.float32)
    rm128 = np.zeros((128, 128), np.float32)
    for i in range(64):
        rm128[i + 64, i] = -1.0
        rm128[i, i + 64] = 1.0
    rm64 = np.zeros((128, 128), np.float32)
    for i in range(32):
        rm64[i + 32, i] = -1.0
        rm64[i, i + 32] = 1.0
    invf = np.zeros((128, 2), np.float32)
    i128 = (10000.0 ** (-(np.arange(64, dtype=np.float32) * 2.0 / 128))).astype(np.float32)
    i64 = (10000.0 ** (-(np.arange(32, dtype=np.float32) * 2.0 / 64))).astype(np.float32)
    invf[:, 0] = np.tile(i128, 2) / (2 * np.pi)
    invf[:, 1] = np.tile(i64, 4) / (2 * np.pi)
    blki = bc(np.arange(S // 256, dtype=np.float32), S // 256)
    kidxc = np.ascontiguousarray(np.arange(S, dtype=np.float32).reshape(S // 128, 128).T)
    shared = {
        "x_all": x2,
        "posb_all": np.ascontiguousarray(np.broadcast_to(pos.reshape(1, S), (128, S))),
        "kidxc": kidxc,
        "g_attn_b": bc(attn_norm, D), "g_ffn_b": bc(ffn_norm, D), "g_ple_b": bc(ple_norm, D),
        "g_fin_b": bc(final_norm, D), "g_kv_b": bc(kv_norm, 512),
        "w_in": np.ascontiguousarray(np.asarray(w_in, np.float32).reshape(D, INC)),
        "w_ukv": np.ascontiguousarray(np.asarray(w_ukv, np.float32).reshape(512, 2048)),
        "w_o": np.ascontiguousarray(np.asarray(w_o, np.float32).reshape(D, D)),
        "w_gate": np.ascontiguousarray(np.asarray(w_gate, np.float32).reshape(D, DFF)),
        "w_up": np.ascontiguousarray(np.asarray(w_up, np.float32).reshape(D, DFF)),
        "w_down": np.ascontiguousarray(np.asarray(w_down, np.float32).reshape(DFF, D)),
        "w_pg": np.ascontiguousarray(np.asarray(w_ple_gate, np.float32).reshape(D, D)),
        "w_pp": np.ascontiguousarray(np.asarray(w_ple_proj, np.float32).reshape(256, D)),
        "ident": ident, "rm128": rm128, "rm64": rm64, "invf": invf, "blki": blki,
    }
    in_maps, own_idx = [], []
    for c in range(NCORES):
        idx = np.concatenate([np.arange(512 * (8 * j + c), 512 * (8 * j + c) + 512) for j in range(NSLOT)])
        own_idx.append(idx)
        m = dict(shared)
        m["x_own"] = np.ascontiguousarray(x2[idx])
        m["p_own"] = np.ascontiguousarray(p2[idx])
        m["posb_own"] = np.ascontiguousarray(np.broadcast_to(pos[idx].reshape(1, NQ), (128, NQ)))
        m["qidxb"] = np.ascontiguousarray(np.broadcast_to(idx.astype(np.float32).reshape(1, NQ), (128, NQ)))
        m["qidxc"] = np.ascontiguousarray(idx.astype(np.int32).reshape(NQ // 128, 128).T)
        in_maps.append(m)
    return in_maps, own_idx


_CACHE = {}


def kernel(**inputs):
    S = 16384
    NSLOT = S // (512 * NCORES)
    if "nc" not in _CACHE:
        _CACHE["nc"] = build(S, NSLOT)[0]
    nc = _CACHE["nc"]
    in_maps, own_idx = make_in_maps(S, NSLOT, **inputs)
    res = run_bass_kernel_spmd(nc, in_maps, core_ids=list(range(NCORES)))
    out = np.zeros((S, D), np.float32)
    for c in range(NCORES):
        out[own_idx[c]] = np.asarray(res.results[c]["y"], dtype=np.float32)
    return out.reshape(1, S, D)
```
